# Optimizing a Trainium2 kernel written in Bass

```python
import math
import jax
import jax.numpy as jnp
from jax import lax
import numpy as np

D_MODEL = 1024
BATCH = 2
SEQ = 16384
DEPTH = 4
DEC_BATCH = 1
DEC_SEQ = 16384
PAST_LEN = 128

N_MIXERS = 3
EXPAND = 2
D_INNER = EXPAND * D_MODEL
EPS = 1e-6

SG_CHUNK = 128
SG_GROUPS = 8
SG_GDIM = D_INNER // SG_GROUPS

GLA_HEADS = 4
GLA_KEY = D_MODEL // 2
GLA_DK = GLA_KEY // GLA_HEADS
GLA_DV = D_INNER // GLA_HEADS
GLA_RANK = 16
GLA_TAU = 16.0
GLA_CHUNK = 64
GLA_IN = 2 * GLA_KEY + 2 * D_INNER + 2 * GLA_RANK
GLA_SPLITS = (GLA_KEY, 2 * GLA_KEY, 2 * GLA_KEY + D_INNER, 2 * GLA_KEY + 2 * D_INNER,
              2 * GLA_KEY + 2 * D_INNER + GLA_RANK)

DIFF_HEADS = 8
DIFF_DQK = 128
DIFF_DV = D_INNER // DIFF_HEADS
DIFF_QBLOCK = 128
DIFF_IN = 4 * D_INNER

N_LAYERS_A = (DEPTH + 2) // N_MIXERS
N_LAYERS_B = (DEPTH + 1) // N_MIXERS
N_LAYERS_C = DEPTH // N_MIXERS

kernel_name = 'hybrid_bidir_sgu_gla_diffattn_encoder'


def rms_norm(x, g):
    xf = x.astype(jnp.float32)
    y = xf * lax.rsqrt(jnp.mean(xf * xf, axis=-1, keepdims=True) + EPS)
    return (y * g.astype(jnp.float32)).astype(x.dtype)


def spatial_gating_mixer(h, w_in, v_g, w_s, b_s, w_out):
    b, s, _ = h.shape
    n = s // SG_CHUNK
    u, v, z = jnp.split(h @ w_in, 3, axis=-1)
    v = rms_norm(v, v_g).reshape(b, n, SG_CHUNK, SG_GROUPS, SG_GDIM)
    sv = jnp.einsum('gqp,bnpgc->bnqgc', w_s, v) + jnp.swapaxes(b_s, 0, 1)[:, :, None]
    y = u * sv.reshape(b, s, D_INNER)
    return (y * jax.nn.silu(z)) @ w_out


def gla_direction(q, k, v, log_a, strict):
    b, s, nh, dk = q.shape
    dv = v.shape[-1]
    c = GLA_CHUNK
    n = s // c
    qc, kc, vc, lc = (t.astype(jnp.float32).reshape(b, n, c, nh, t.shape[-1]) for t in (q, k, v, log_a))
    cum = jnp.cumsum(lc, axis=2)
    ref = cum[:, :, c // 2:c // 2 + 1]
    scores = jnp.einsum('bnthd,bnshd->bnhts', qc * jnp.exp(cum - ref), kc * jnp.exp(ref - cum))
    idx = jnp.arange(c)
    mask = (idx[:, None] > idx[None, :]) if strict else (idx[:, None] >= idx[None, :])
    scores = jnp.where(mask, scores, 0.0)
    o_intra = jnp.einsum('bnhts,bnshv->bnthv', scores, vc)
    last = cum[:, :, -1:]
    q_inter = qc * jnp.exp(cum)
    k_inter = kc * jnp.exp(last - cum)
    decay = jnp.exp(last[:, :, 0])

    def step(state, xs):
        qi, ki, vi, di = xs
        o = jnp.einsum('bthd,bhdv->bthv', qi, state)
        state = state * di[..., None] + jnp.einsum('bthd,bthv->bhdv', ki, vi)
        return state, o

    state0 = jnp.zeros((b, nh, dk, dv), jnp.float32)
    xs = tuple(jnp.moveaxis(t, 1, 0) for t in (q_inter, k_inter, vc, decay))
    _, o_inter = lax.scan(step, state0, xs)
    o = o_intra + jnp.moveaxis(o_inter, 0, 1)
    return o.reshape(b, s, nh, dv)


def gla_mixer(h, w_in, w_gate, gate_bias, o_g, w_out):
    b, s, _ = h.shape
    q, k, v, g, a_f, a_b = jnp.split(h @ w_in, GLA_SPLITS, axis=-1)
    q = q.reshape(b, s, GLA_HEADS, GLA_DK) * GLA_DK ** -0.5
    k = k.reshape(b, s, GLA_HEADS, GLA_DK)
    v = v.reshape(b, s, GLA_HEADS, GLA_DV)

    def log_decay(code, w, bias):
        pre = (code @ w + bias).astype(jnp.float32)
        return (jax.nn.log_sigmoid(pre) / GLA_TAU).reshape(b, s, GLA_HEADS, GLA_DK)

    la_f = log_decay(a_f, w_gate[0], gate_bias[0])
    la_b = log_decay(a_b, w_gate[1], gate_bias[1])
    flip = lambda t: jnp.flip(t, axis=1)
    o_f = gla_direction(q, k, v, la_f, strict=False)
    o_b = flip(gla_direction(flip(q), flip(k), flip(v), flip(la_b), strict=True))
    o = rms_norm(o_f + o_b, o_g).astype(h.dtype).reshape(b, s, D_INNER)
    return (o * jax.nn.silu(g)) @ w_out


def diff_attn_mixer(h, w_in, q_g, k_g, lam, o_g, w_out, lambda_init):
    b, s, _ = h.shape
    nb = s // DIFF_QBLOCK
    q, k, v, z = jnp.split(h @ w_in, 4, axis=-1)
    q = rms_norm(q.reshape(b, s, DIFF_HEADS, 2, DIFF_DQK), q_g)
    k = rms_norm(k.reshape(b, s, DIFF_HEADS, 2, DIFF_DQK), k_g)
    kf = k.astype(jnp.float32)
    vf = v.reshape(b, s, DIFF_HEADS, DIFF_DV).astype(jnp.float32)
    lf = lam.astype(jnp.float32)
    lam_full = jnp.exp(jnp.sum(lf[0] * lf[1])) - jnp.exp(jnp.sum(lf[2] * lf[3])) + lambda_init
    slopes = jnp.asarray(2.0 ** (-8.0 * np.arange(1, DIFF_HEADS + 1) / DIFF_HEADS), jnp.float32)
    pos_k = jnp.arange(s, dtype=jnp.float32)
    qb = (q.astype(jnp.float32) * DIFF_DQK ** -0.5).reshape(b, nb, DIFF_QBLOCK, DIFF_HEADS, 2, DIFF_DQK)
    qb = jnp.moveaxis(qb, 1, 0)
    starts = jnp.arange(nb, dtype=jnp.float32) * DIFF_QBLOCK

    def block(args):
        qi, start = args
        logits = jnp.einsum('bqhmd,bkhmd->bhmqk', qi, kf)
        pos_q = start + jnp.arange(DIFF_QBLOCK, dtype=jnp.float32)
        dist = jnp.abs(pos_q[:, None] - pos_k[None, :])
        logits = logits - slopes[None, :, None, None, None] * dist[None, None, None]
        p = jax.nn.softmax(logits, axis=-1)
        attn = p[:, :, 0] - lam_full * p[:, :, 1]
        return jnp.einsum('bhqk,bkhv->bqhv', attn, vf)

    o = lax.map(block, (qb, starts))
    o = jnp.moveaxis(o, 0, 1).reshape(b, s, DIFF_HEADS, DIFF_DV)
    o = (rms_norm(o, o_g) * (1.0 - lambda_init)).astype(h.dtype).reshape(b, s, D_INNER)
    return (o * jax.nn.silu(z)) @ w_out


def trunk(x, norm_g, a_w_in, a_v_g, a_w_s, a_b_s, a_w_out,
          b_w_in, b_w_gate, b_gate_bias, b_o_g, b_w_out,
          c_w_in, c_q_g, c_k_g, c_lam, c_o_g, c_w_out):
    for i in range(DEPTH):
        h = rms_norm(x, norm_g[i])
        kind, j = i % N_MIXERS, i // N_MIXERS
        if kind == 0:
            y = spatial_gating_mixer(h, a_w_in[j], a_v_g[j], a_w_s[j], a_b_s[j], a_w_out[j])
        elif kind == 1:
            y = gla_mixer(h, b_w_in[j], b_w_gate[j], b_gate_bias[j], b_o_g[j], b_w_out[j])
        else:
            lambda_init = 0.8 - 0.6 * math.exp(-0.3 * i)
            y = diff_attn_mixer(h, c_w_in[j], c_q_g[j], c_k_g[j], c_lam[j], c_o_g[j], c_w_out[j], lambda_init)
        x = x + y.astype(x.dtype)
    return x


def setup_inputs(seed: int = 0) -> dict:
    key = jax.random.key(seed)
    ks = jax.random.split(key, 20)
    f32 = jnp.float32

    def nrm(k, shape, scale):
        return jax.random.normal(k, shape, f32) * scale

    return {
        'x_prompt': nrm(ks[0], (BATCH, SEQ, D_MODEL), 1.0),
        'x_sample': nrm(ks[1], (DEC_BATCH, DEC_SEQ, D_MODEL), 1.0),
        'norm_g': 1.0 + nrm(ks[2], (DEPTH, D_MODEL), 0.02),
        'a_w_in': nrm(ks[3], (N_LAYERS_A, D_MODEL, 3 * D_INNER), D_MODEL ** -0.5),
        'a_v_g': 1.0 + nrm(ks[4], (N_LAYERS_A, D_INNER), 0.02),
        'a_w_s': nrm(ks[5], (N_LAYERS_A, SG_GROUPS, SG_CHUNK, SG_CHUNK), SG_CHUNK ** -0.5),
        'a_b_s': 1.0 + nrm(ks[6], (N_LAYERS_A, SG_GROUPS, SG_CHUNK), 0.1),
        'a_w_out': nrm(ks[7], (N_LAYERS_A, D_INNER, D_MODEL), D_INNER ** -0.5),
        'b_w_in': nrm(ks[8], (N_LAYERS_B, D_MODEL, GLA_IN), D_MODEL ** -0.5),
        'b_w_gate': nrm(ks[9], (N_LAYERS_B, 2, GLA_RANK, GLA_KEY), GLA_RANK ** -0.5),
        'b_gate_bias': nrm(ks[10], (N_LAYERS_B, 2, GLA_KEY), 0.1),
        'b_o_g': 1.0 + nrm(ks[11], (N_LAYERS_B, GLA_DV), 0.02),
        'b_w_out': nrm(ks[12], (N_LAYERS_B, D_INNER, D_MODEL), D_INNER ** -0.5),
        'c_w_in': nrm(ks[13], (N_LAYERS_C, D_MODEL, DIFF_IN), D_MODEL ** -0.5),
        'c_q_g': 1.0 + nrm(ks[14], (N_LAYERS_C, DIFF_DQK), 0.02),
        'c_k_g': 1.0 + nrm(ks[15], (N_LAYERS_C, DIFF_DQK), 0.02),
        'c_lam': nrm(ks[16], (N_LAYERS_C, 4, DIFF_DQK), 0.1),
        'c_o_g': 1.0 + nrm(ks[17], (N_LAYERS_C, DIFF_DV), 0.02),
        'c_w_out': nrm(ks[18], (N_LAYERS_C, D_INNER, D_MODEL), D_INNER ** -0.5),
    }


def reference(x_prompt, x_sample, norm_g, a_w_in, a_v_g, a_w_s, a_b_s, a_w_out,
              b_w_in, b_w_gate, b_gate_bias, b_o_g, b_w_out,
              c_w_in, c_q_g, c_k_g, c_lam, c_o_g, c_w_out):
    y_prompt = trunk(x_prompt, norm_g, a_w_in, a_v_g, a_w_s, a_b_s, a_w_out,
                     b_w_in, b_w_gate, b_gate_bias, b_o_g, b_w_out,
                     c_w_in, c_q_g, c_k_g, c_lam, c_o_g, c_w_out)
    y_sample = trunk(x_sample, norm_g, a_w_in, a_v_g, a_w_s, a_b_s, a_w_out,
                     b_w_in, b_w_gate, b_gate_bias, b_o_g, b_w_out,
                     c_w_in, c_q_g, c_k_g, c_lam, c_o_g, c_w_out)
    return (y_prompt, y_sample)
```

```python
import os
import numpy as np
import ml_dtypes
import concourse.bass as bass
import concourse.mybir as mybir
from concourse.bass_utils import run_bass_kernel_spmd

F32 = mybir.dt.float32
BF16 = mybir.dt.bfloat16
AF = mybir.ActivationFunctionType
ALU = mybir.AluOpType
AX = mybir.AxisListType

NCORES = 8
D = 1024
DI = 2048
EPS = 1e-6


class _Op:
    __slots__ = ("eng", "fn", "deps", "semkey", "val", "isdma")


def _is_psum(b):
    n = b[0] if isinstance(b, tuple) else b
    return isinstance(n, str) and n.startswith("ps")


class Prog:
    ENGS = ("pe", "act", "dve", "pool", "sp")

    def __init__(self):
        self.ops = {e: [] for e in self.ENGS}
        self.ncomp = {e: 0 for e in self.ENGS}
        self.bufs = {}
        self.dma_last = {}
        self.dma_cnt = {}
        self.all_dma_keys = []
        self.bar = {e: [] for e in self.ENGS}
        self.eng_epochs = set()
        self.EPOCH = int(os.environ.get("EPOCH", "32000"))

    def barrier(self):
        deps = [ops[-1] for e, ops in self.ops.items() if ops]
        for e in self.ENGS:
            cl = [o for o in reversed(self.ops[e]) if not o.isdma]
            if cl:
                deps.append(cl[0])
        deps += list(self.dma_last.values())
        for e in self.ENGS:
            self.bar[e] = list(deps)

    def add(self, eng, fn, r=(), w=(), dma=None):
        op = _Op()
        op.eng = eng
        op.fn = fn
        op.deps = []
        if self.bar[eng]:
            op.deps.extend(self.bar[eng])
            self.bar[eng] = []
        op.isdma = dma is not None
        for b in r:
            st = self.bufs.get(b)
            if st is not None and st[0] is not None:
                op.deps.append(st[0])
            if st is not None and _is_psum(b):
                for o in st[1]:
                    if o.eng != eng:
                        op.deps.append(o)
        for b in w:
            st = self.bufs.get(b)
            if st is not None:
                if st[0] is not None:
                    op.deps.append(st[0])
                op.deps.extend(st[1])
        for b in r:
            st = self.bufs.setdefault(b, [None, []])
            st[1].append(op)
        for b in w:
            self.bufs[b] = [op, []]
        if dma is not None:
            if isinstance(dma, str):
                dma = ("misc", len(dma) % 2)
            prev = self.dma_last.get(dma)
            if prev is not None:
                op.deps.append(prev)
            else:
                self.all_dma_keys.append(dma)
            self.dma_last[dma] = op
            self.dma_cnt[dma] = self.dma_cnt.get(dma, 0) + 16
            op.semkey = ("dma", dma)
            op.val = self.dma_cnt[dma]
        else:
            n = self.ncomp[eng]
            self.ncomp[eng] += 1
            op.semkey = ("eng", eng, n // self.EPOCH)
            op.val = n % self.EPOCH + 1
            self.eng_epochs.add(op.semkey)
        self.ops[eng].append(op)
        return op

    def emit(self, nc, tail_wait_keys=()):
        import contextlib
        with contextlib.ExitStack() as es:
            sems = {}
            for k in sorted(self.eng_epochs):
                sems[k] = es.enter_context(nc.semaphore("s_%s_%d" % (k[1], k[2])))
            for i, k in enumerate(self.all_dma_keys):
                sems[("dma", k)] = es.enter_context(nc.semaphore("d%d" % i))
            block = es.enter_context(nc.Block())

            def run(engname, eng):
                waited = {}
                for op in self.ops[engname]:
                    for d in op.deps:
                        if d.eng == "pe" and engname == "pe" and not d.isdma and not op.isdma:
                            continue
                        if waited.get(d.semkey, 0) >= d.val:
                            continue
                        eng.wait_ge(sems[d.semkey], d.val)
                        waited[d.semkey] = d.val
                    ins = op.fn(eng)
                    ins.then_inc(sems[op.semkey], 16 if op.isdma else 1)
                for k in self.all_dma_keys:
                    last = self.dma_last[k]
                    if last.eng == engname and waited.get(last.semkey, 0) < last.val:
                        eng.wait_ge(sems[last.semkey], last.val)

            @block.tensor
            def _(e):
                run("pe", e)

            @block.scalar
            def _(e):
                run("act", e)

            @block.vector
            def _(e):
                run("dve", e)

            @block.gpsimd
            def _(e):
                run("pool", e)

            @block.sync
            def _(e):
                run("sp", e)


def bcast_rows(ap_row, nparts):
    return ap_row.partition_broadcast(nparts)


class Ctx:
    pass


def emit_norm_T(P, C, x_src_ap, tag, gtab, slot):
    xt = C.xt[slot]
    xb = C.xb
    hT = C.hT[slot]
    P.add("sp", lambda e: e.dma_start(out=xt[:], in_=x_src_ap), w=[("xt", slot)], dma=("xt", slot))
    P.add("act", lambda e: e.activation(out=C.junk[:, 0:D], in_=xt[:], func=AF.Square, accum_out=C.ss[:, 0:1]),
          r=[("xt", slot)], w=["junk", "ss"])
    P.add("act", lambda e: e.activation(out=C.ss[:, 1:2], in_=C.ss[:, 0:1], func=AF.Sqrt, scale=1.0 / D, bias=C.epsb[:, 0:1]),
          r=["ss", "epsb"], w=["ss1"])
    P.add("dve", lambda e: e.reciprocal(out=C.ss[:, 2:3], in_=C.ss[:, 1:2]), r=["ss1"], w=["ss2"])
    P.add("dve", lambda e: e.scalar_tensor_tensor(out=xb[:], in0=xt[:], scalar=C.ss[:, 2:3], in1=gtab[:],
                                                  op0=ALU.mult, op1=ALU.mult),
          r=[("xt", slot), "ss2", "gtab"], w=["xb"])
    for kc in range(8):
        P.add("pe", lambda e, kc=kc: e.transpose(out=C.psT[:, kc * 128:(kc + 1) * 128], in_=xb[:, kc * 128:(kc + 1) * 128],
                                                  identity=C.ident[:]),
              r=["xb", "ident"], w=["psT"])
    P.add("act", lambda e: e.copy(out=hT[:], in_=C.psT[:]), r=["psT"], w=[("hT", slot)])


def build_program(SL, depth):
    NT = SL // 128
    nc = bass.Bass("TRN2", target_bir_lowering=False)
    P = Prog()
    C = Ctx()
    C.NT = NT

    def din(name, shape, dt=F32):
        return nc.dram_tensor(name, list(shape), dt, kind="ExternalInput").ap()

    x_in = din("x", [SL, D])
    W = {}
    for k, shp in WSHAPES.items():
        W[k] = din(k, shp)
    ident_in = din("ident", [128, 128], BF16)
    C.cm_in = din("cm", [6, 128, 128])
    C.gmask_in = din("gmask", [2, 128, 512], BF16)
    C.alibi_in = din("alibi", [8, 4, 128, 512])
    y_out = nc.dram_tensor("y", [SL, D], F32, kind="ExternalOutput").ap()
    xs = [nc.dram_tensor("xs%d" % i, [SL, D], F32, kind="Internal").ap() for i in range(2)]
    C.dram = lambda name, shape, dt: nc.dram_tensor(name, list(shape), dt, kind="Internal").ap()

    import contextlib
    with contextlib.ExitStack() as es:
        def sb(name, shape, dt):
            return es.enter_context(nc.sbuf_tensor("sb_" + name, list(shape), dt))

        def ps(name, shape, dt):
            return es.enter_context(nc.psum_tensor("ps_" + name, list(shape), dt))

        C.ident = sb("ident", [128, 128], BF16)
        C.xt = [sb("xt%d" % i, [128, D], F32) for i in range(2)]
        C.xb = sb("xb", [128, D], BF16)
        C.hT = [sb("hT%d" % i, [128, D], BF16) for i in range(2)]
        C.junk = sb("junk", [128, D], BF16)
        C.ss = sb("ss", [128, 16], F32)
        C.epsb = sb("epsb", [128, 1], F32)
        C.oneb = sb("oneb", [128, 1], F32)
        C.gtab = sb("gtab", [128, D], F32)
        C.psA = [ps("psA%d" % i, [128, 512], F32) for i in range(3)]
        C.bankT = ps("bankT", [128, 512], F32)
        C.psT = C.bankT.bitcast(BF16)
        C.bankY = ps("bankY", [128, 1024], F32)
        C.psYT = C.bankY.bitcast(BF16)
        C.psO = ps("psO", [128, D], F32)
        C.nc = nc
        C.pa = [0]

        P.add("sp", lambda e: e.dma_start(out=C.ident[:], in_=ident_in), w=["ident"], dma="ident")
        P.add("pool", lambda e: e.memset(C.epsb[:], EPS), w=["epsb"])
        P.add("pool", lambda e: e.memset(C.oneb[:], 1.0), w=["oneb"])

        layer_kinds = [0, 1, 2, 0][:depth]
        cur = x_in
        for li, kind in enumerate(layer_kinds):
            dst = y_out if li == depth - 1 else xs[li % 2]
            with contextlib.ExitStack() as les:
                P.barrier()
                if kind == 0:
                    layer_A(nc, P, C, les, cur, dst, li, li // 3, NT, W["norm_g"], W["a_w_in"], W["a_v_g"], W["a_w_s"],
                            W["a_b_s"], W["a_w_out"])
                elif kind == 1:
                    layer_B(nc, P, C, les, cur, dst, li, NT, W)
                else:
                    layer_C(nc, P, C, les, cur, dst, li, NT, W)
            cur = dst
        P.emit(nc)
    return nc


WSHAPES = {
    "norm_g": [4, D], "a_w_in": [2, D, 3 * DI], "a_v_g": [2, DI], "a_w_s": [2, 8, 128, 128], "a_b_s": [2, 8, 128],
    "a_w_out": [2, DI, D], "b_w_in": [1, D, 5152], "b_w_gate": [1, 2, 16, 512], "b_gate_bias": [1, 2, 512],
    "b_o_g": [1, 512], "b_w_out": [1, DI, D], "c_w_in": [1, D, 4 * DI], "c_q_g": [1, 128], "c_k_g": [1, 128],
    "c_lam": [1, 4, 128], "c_o_g": [1, 256], "c_w_out": [1, DI, D],
}


def interleave(gens):
    gens = list(gens)
    while gens:
        nxt = []
        for g in gens:
            try:
                next(g)
                nxt.append(g)
            except StopIteration:
                pass
        gens = nxt


def layer_A(nc, P, C, es, x_src, x_dst, li, j, NT, norm_g, a_w_in, a_v_g, a_w_s, a_b_s, a_w_out):
    L = "A%d" % li

    def sb(name, shape, dt):
        return es.enter_context(nc.sbuf_tensor("sb_" + L + name, list(shape), dt))

    Win = sb("Win", [128, 8, 3 * DI], BF16)
    Wout = sb("Wout", [128, 16, D], BF16)
    t1 = [sb("t1%d" % i, [128, 512], F32) for i in range(2)]
    yb = sb("yb", [128, DI], BF16)
    wsq = sb("wsq", [128, 8, 128], F32)
    wsqb = yb[:, 0:1024].rearrange("p (g q) -> p g q", g=8)
    wsT = sb("wsT", [128, 8, 128], BF16)
    bs = sb("bs", [128, 8], F32)
    vgtab = sb("vgtab", [128, DI], F32)
    u = [sb("u%d" % i, [128, DI], BF16) for i in range(2)]
    sz = [sb("sz%d" % i, [128, DI], BF16) for i in range(2)]
    v = sb("v", [128, DI], F32)
    vs = sb("vs", [128, DI], BF16)
    yT = sb("yT", [128, DI], BF16)
    xn = sb("xn", [128, D], F32)
    st = sb("st", [128, 16], F32)

    wv = a_w_in[j].rearrange("(kc p) f -> p kc f", p=128)
    for kc in range(8):
        P.add("pool", lambda e, kc=kc: e.dma_start(out=Win[:, kc, :], in_=wv[:, kc, :]),
              w=[(L, "Win", kc)], dma=(L, "Win", kc % 2))
    wo = a_w_out[j].rearrange("(kc p) f -> p kc f", p=128)
    for kc in range(0, 16, 4):
        P.add("pool", lambda e, kc=kc: e.dma_start(out=Wout[:, kc:kc + 4, :], in_=wo[:, kc:kc + 4, :]),
              w=[(L, "Wout", kc)], dma=(L, "Wout", (kc // 4) % 2))
    P.add("sp", lambda e: e.dma_start(out=wsq[:], in_=a_w_s[j].rearrange("g q p -> q g p")), w=[L + "wsq"], dma=L + "wsq")
    P.add("sp", lambda e: e.dma_start(out=bs[:], in_=a_b_s[j].rearrange("g q -> q g"), allow_slow_non_contiguous=True),
          w=[L + "bs"], dma=L + "bs")
    P.add("sp", lambda e: e.dma_start(out=C.gtab[:], in_=norm_g[li:li + 1, :].partition_broadcast(128)),
          w=["gtab"], dma="gtab")
    P.add("sp", lambda e: e.dma_start(out=vgtab[:], in_=a_v_g[j:j + 1, :].partition_broadcast(128)),
          w=[L + "vgtab"], dma=L + "vgtab")
    P.add("dve", lambda e: e.tensor_copy(out=wsqb, in_=wsq[:]), r=[L + "wsq"], w=[(L, "yb", 0), (L, "yb", 1)])
    for g in range(8):
        P.add("pe", lambda e, g=g: e.transpose(out=C.psT[:, g * 128:(g + 1) * 128], in_=wsqb[:, g, :], identity=C.ident[:]),
              r=[(L, "yb", 0), (L, "yb", 1), "ident"], w=["psT"])
    P.add("act", lambda e: e.copy(out=wsT[:].rearrange("p g q -> p (g q)"), in_=C.psT[:]), r=["psT"], w=[L + "wsT"])

    Win_bufs = [(L, "Win", kc) for kc in range(8)]
    Wout_bufs = [(L, "Wout", kc) for kc in range(0, 16, 4)]

    def next_psA():
        i = C.pa[0] % 3
        C.pa[0] += 1
        return i

    def front(ti):
        slot = ti % 2
        rows = slice(ti * 128, (ti + 1) * 128)
        emit_norm_T(P, C, x_src[rows, :], L, C.gtab, slot)
        hT = C.hT[slot]
        u_, sz_ = u[slot], sz[slot]
        yield
        for cb in range(12):
            pi = next_psA()
            pst = C.psA[pi]
            for kc in range(8):
                P.add("pe", lambda e, kc=kc, cb=cb, pst=pst: e.matmul(
                    pst[:], lhsT=hT[:, kc * 128:(kc + 1) * 128], rhs=Win[:, kc, cb * 512:(cb + 1) * 512],
                    start=(kc == 0), stop=(kc == 7)),
                    r=[("hT", slot), Win_bufs[kc]], w=[("psA", pi)])
            c0 = (cb % 4) * 512
            if cb < 4:
                P.add("act", lambda e, pst=pst, c0=c0: e.copy(out=u_[:, c0:c0 + 512], in_=pst[:]),
                      r=[("psA", pi)], w=[(L, "u", slot, cb)])
            elif cb < 8:
                P.add("dve", lambda e, pst=pst, c0=c0: e.tensor_copy(out=v[:, c0:c0 + 512], in_=pst[:]),
                      r=[("psA", pi)], w=[(L, "v", cb - 4)])
                P.add("act", lambda e, pst=pst, cb=cb: e.activation(out=C.junk[:, 0:512], in_=pst[:], func=AF.Square,
                                                                      accum_out=st[:, cb - 4:cb - 3]),
                      r=[("psA", pi)], w=["junk", (L, "ssv", cb - 4)])
            else:
                P.add("act", lambda e, pst=pst, c0=c0: e.activation(out=sz_[:, c0:c0 + 512], in_=pst[:], func=AF.Silu),
                      r=[("psA", pi)], w=[(L, "sz", slot, cb - 8)])
            yield

    def back(ti):
        slot = ti % 2
        rows = slice(ti * 128, (ti + 1) * 128)
        xt = C.xt[slot]
        u_, sz_ = u[slot], sz[slot]
        P.add("dve", lambda e: e.tensor_reduce(out=st[:, 4:5], in_=st[:, 0:4], axis=AX.X, op=ALU.add),
              r=[(L, "ssv", i) for i in range(4)], w=[L + "st4"])
        P.add("act", lambda e: e.activation(out=st[:, 5:6], in_=st[:, 4:5], func=AF.Sqrt, scale=1.0 / DI, bias=C.epsb[:, 0:1]),
              r=[L + "st4", "epsb"], w=[L + "st5"])
        P.add("dve", lambda e: e.reciprocal(out=st[:, 6:7], in_=st[:, 5:6]), r=[L + "st5"], w=[L + "st6"])
        for b in range(4):
            P.add("dve", lambda e, b=b: e.tensor_scalar(out=vs[:, b * 512:(b + 1) * 512], in0=v[:, b * 512:(b + 1) * 512],
                                                         scalar1=st[:, 6:7], scalar2=None, op0=ALU.mult),
                  r=[(L, "v", b), L + "st6"], w=[(L, "vs", b)])
        yield
        for b in range(4):
            pi = next_psA()
            pst = C.psA[pi]
            for gg in range(2):
                g = 2 * b + gg
                P.add("pe", lambda e, g=g, gg=gg, pst=pst: e.matmul(
                    pst[:, gg * 256:(gg + 1) * 256], lhsT=wsT[:, g, :], rhs=vs[:, g * 256:(g + 1) * 256],
                    start=True, stop=True),
                    r=[L + "wsT", (L, "vs", b)], w=[("psA", pi)])
            tt = t1[b % 2]
            P.add("dve", lambda e, pst=pst, tt=tt, b=b: e.tensor_tensor(out=tt[:], in0=pst[:], in1=vgtab[:, b * 512:(b + 1) * 512],
                                                                        op=ALU.mult),
                  r=[("psA", pi), L + "vgtab"], w=[(L, "t1", b % 2)])
            for gg in range(2):
                g = 2 * b + gg
                P.add("dve", lambda e, tt=tt, g=g, gg=gg: e.scalar_tensor_tensor(
                    out=tt[:, gg * 256:(gg + 1) * 256], in0=tt[:, gg * 256:(gg + 1) * 256], scalar=bs[:, g:g + 1],
                    in1=u_[:, g * 256:(g + 1) * 256], op0=ALU.add, op1=ALU.mult),
                    r=[(L, "t1", b % 2), L + "bs", (L, "u", slot, b)], w=[(L, "t1", b % 2)])
            P.add("pool", lambda e, tt=tt, b=b: e.tensor_tensor(out=yb[:, b * 512:(b + 1) * 512], in0=tt[:],
                                                                in1=sz_[:, b * 512:(b + 1) * 512], op=ALU.mult),
                  r=[(L, "t1", b % 2), (L, "sz", slot, b)], w=[(L, "yb", b // 2)])
            yield
        for kc in range(16):
            P.add("pe", lambda e, kc=kc: e.transpose(out=C.psYT[:, kc * 128:(kc + 1) * 128], in_=yb[:, kc * 128:(kc + 1) * 128],
                                                      identity=C.ident[:]),
                  r=[(L, "yb", kc // 8), "ident"], w=["psYT"])
        P.add("act", lambda e: e.copy(out=yT[:], in_=C.psYT[:]), r=["psYT"], w=[L + "yT"])
        yield
        for nb in range(2):
            for kc in range(16):
                P.add("pe", lambda e, kc=kc, nb=nb: e.matmul(
                    C.psO[:, nb * 512:(nb + 1) * 512], lhsT=yT[:, kc * 128:(kc + 1) * 128],
                    rhs=Wout[:, kc, nb * 512:(nb + 1) * 512], start=(kc == 0), stop=(kc == 15)),
                    r=[L + "yT", Wout_bufs[kc // 4]], w=[("psO", nb)])
            yield
        P.add("dve", lambda e: e.tensor_tensor(out=xn[:], in0=C.psO[:], in1=xt[:], op=ALU.add),
              r=[("psO", 0), ("psO", 1), ("xt", slot)], w=[L + "xn"])
        P.add("sp", lambda e, rows=rows: e.dma_start(out=x_dst[rows, :], in_=xn[:]),
              r=[L + "xn"], w=[("xdram", li, ti)], dma=("xst", li, ti % 2))
        yield

    interleave([front(0)])
    for ti in range(NT):
        gens = [back(ti)]
        if ti + 1 < NT:
            gens.append(front(ti + 1))
        interleave(gens)


def layer_B(nc, P, C, es, x_src, x_dst, li, NT, W):
    L = "B%d" % li
    w_in = W["b_w_in"][0]

    def sb(name, shape, dt):
        return es.enter_context(nc.sbuf_tensor("sb_" + L + name, list(shape), dt))

    Wqk = sb("Wqk", [128, 8, 1024], BF16)
    Wvg = sb("Wvg", [128, 8, 4096], BF16)
    Waf = sb("Waf", [128, 8, 16], BF16)
    Wab = sb("Wab", [128, 8, 16], BF16)
    Wout = Wvg[:, 0:4, :].rearrange("p a (b f) -> p (a b) f", b=4)
    wg = [sb("wg%d" % d, [32, 512], BF16) for d in range(2)]
    aT = [sb("aT%d" % d, [32, 128], BF16) for d in range(2)]
    cm = sb("cm", [128, 6, 128], F32)
    gmask = sb("gmask", [128, 2, 512], BF16)
    ogtab = sb("ogtab", [128, 512], F32)
    qT2 = [sb("qT%d" % i, [128, 512], F32) for i in range(2)]
    kT2 = [sb("kT%d" % i, [128, 512], F32) for i in range(2)]
    ex = sb("ex", [128, 512], F32)
    sp2 = [[sb("sp%d_%d" % (i, d), [128, 512], F32) for d in range(2)] for i in range(2)]
    E = [sb("E%d" % i, [128, 512], F32) for i in range(2)]
    qq = [sb("qq%d" % d, [128, 512], BF16) for d in range(2)]
    kk = [sb("kk%d" % d, [128, 512], BF16) for d in range(2)]
    qi = [sb("qi%d" % d, [128, 512], BF16) for d in range(2)]
    ki = [sb("ki%d" % d, [128, 512], BF16) for d in range(2)]
    kiT = sb("kiT", [128, 1024], BF16)
    scT = [sb("scT%d" % d, [128, 512], BF16) for d in range(2)]
    vb2 = [sb("vb%d" % i, [128, DI], BF16) for i in range(2)]
    sg2 = [sb("sg%d" % i, [128, DI], BF16) for i in range(2)]
    opart = sb("opart", [128, DI], F32)
    dec = sb("dec", [128, 8], F32)
    St = sb("St", [128, DI], F32)
    Stb = sb("Stb", [128, DI], BF16)
    yb = sb("yb", [128, DI], BF16)
    yT = sb("yT", [128, DI], BF16)
    xn = sb("xn", [128, D], F32)
    st = sb("st", [128, 16], F32)
    qi2 = sb("qi2", [128, 512], BF16)
    kiT2 = sb("kiT2", [128, 512], BF16)
    dec2 = sb("dec2", [128, 4], F32)

    st_o = C.dram(L + "st_o", [NT * 128, DI], F32)
    st_v = C.dram(L + "st_v", [NT * 128, DI], BF16)
    st_sg = C.dram(L + "st_sg", [NT * 128, DI], BF16)
    st_qi = C.dram(L + "st_qi", [NT * 128, 512], BF16)
    st_ki = C.dram(L + "st_ki", [NT * 128, 512], BF16)
    st_df = C.dram(L + "st_df", [NT * 128, 4], F32)

    def next_psA():
        i = C.pa[0] % 3
        C.pa[0] += 1
        return i

    wv = w_in.rearrange("(kc p) f -> p kc f", p=128)
    for kc in range(8):
        P.add("pool", lambda e, kc=kc: e.dma_start(out=Wqk[:, kc, :], in_=wv[:, kc, 0:1024]), w=[(L, "Wqk", kc)], dma=(L, "W", 0))
        P.add("pool", lambda e, kc=kc: e.dma_start(out=Wvg[:, kc, :], in_=wv[:, kc, 1024:5120]), w=[(L, "Wvg", kc)], dma=(L, "W", 1))
        P.add("pool", lambda e, kc=kc: e.dma_start(out=Waf[:, kc, :], in_=wv[:, kc, 5120:5136]), w=[(L, "Waf")], dma=(L, "W", 2))
        P.add("pool", lambda e, kc=kc: e.dma_start(out=Wab[:, kc, :], in_=wv[:, kc, 5136:5152]), w=[(L, "Wab")], dma=(L, "W", 3))
    for d in range(2):
        P.add("pool", lambda e, d=d: e.dma_start(out=wg[d][0:16, :], in_=W["b_w_gate"][0, d]), w=[(L, "wg", d)], dma=(L, "W", 2))
        P.add("pool", lambda e, d=d: e.dma_start(out=wg[d][16:17, :], in_=W["b_gate_bias"][0, d:d + 1, :]), w=[(L, "wgb", d)], dma=(L, "W", 3))
        P.add("pool", lambda e, d=d: e.memset(aT[d][:], 1.0), w=[(L, "aT", d)])
    P.add("sp", lambda e: e.dma_start(out=cm[:], in_=C.cm_in.rearrange("m a b -> a m b")), w=[L + "cm"], dma=L + "cm")
    P.add("sp", lambda e: e.dma_start(out=gmask[:], in_=C.gmask_in.rearrange("m a b -> a m b")), w=[L + "gmask"], dma=L + "gmask")
    P.add("sp", lambda e: e.dma_start(out=C.gtab[:], in_=W["norm_g"][li:li + 1, :].partition_broadcast(128)), w=["gtab"], dma="gtab")
    P.add("sp", lambda e: e.dma_start(out=ogtab[:], in_=W["b_o_g"][0:1, :].partition_broadcast(128)), w=[L + "ogtab"], dma=L + "ogtab")
    P.add("dve", lambda e: e.memset(St[:], 0.0), w=[L + "St"])
    P.add("pool", lambda e: e.memset(Stb[:], 0.0), w=[(L, "Stb", h) for h in range(4)])
    Wout_bufs = [(L, "Wout", kc) for kc in range(0, 16, 4)]

    def front1(ti, par):
        slot = par
        rows = slice(ti * 128, (ti + 1) * 128)
        emit_norm_T(P, C, x_src[rows, :], L, C.gtab, slot)
        hT = C.hT[slot]
        hTb = ("hT", slot)
        qT, kT, sp, vb, sg = qT2[par], kT2[par], sp2[par], vb2[par], sg2[par]
        yield
        for qk in range(2):
            pi = next_psA()
            pst = C.psA[pi]
            for h in range(4):
                blk = qk * 4 + h
                for kc in range(8):
                    P.add("pe", lambda e, kc=kc, blk=blk, h=h, pst=pst: e.matmul(
                        pst[:, h * 128:(h + 1) * 128], lhsT=Wqk[:, kc, blk * 128:(blk + 1) * 128], rhs=hT[:, kc * 128:(kc + 1) * 128],
                        start=(kc == 0), stop=(kc == 7)), r=[hTb, (L, "Wqk", kc)], w=[("psA", pi)])
            if qk == 0:
                P.add("act", lambda e, pst=pst: e.activation(out=qT[:], in_=pst[:], func=AF.Copy, scale=128.0 ** -0.5),
                      r=[("psA", pi)], w=[(L, "qT", par)])
            else:
                P.add("act", lambda e, pst=pst: e.copy(out=kT[:], in_=pst[:]), r=[("psA", pi)], w=[(L, "kT", par)])
            yield
        for d, Wa in enumerate((Waf, Wab)):
            pi = next_psA()
            pst = C.psA[pi]
            for kc in range(8):
                P.add("pe", lambda e, kc=kc, Wa=Wa, pst=pst: e.matmul(
                    pst[0:16, 0:128], lhsT=Wa[:, kc, :], rhs=hT[:, kc * 128:(kc + 1) * 128], start=(kc == 0), stop=(kc == 7)),
                    r=[hTb, (L, "Waf"), (L, "Wab")], w=[("psA", pi)])
            P.add("dve", lambda e, d=d, pst=pst: e.tensor_copy(out=aT[d][0:16, :], in_=pst[0:16, 0:128]),
                  r=[("psA", pi)], w=[(L, "aT", d)])
        for d in range(2):
            pi = next_psA()
            pst = C.psA[pi]
            P.add("pe", lambda e, d=d, pst=pst: e.matmul(pst[:], lhsT=aT[d][0:17, :], rhs=wg[d][0:17, :], start=True, stop=True),
                  r=[(L, "aT", d), (L, "wg", d), (L, "wgb", d)], w=[("psA", pi)])
            P.add("act", lambda e, pst=pst: e.activation(out=ex[:], in_=pst[:], func=AF.Exp, scale=-1.0),
                  r=[("psA", pi)], w=[L + "ex"])
            P.add("act", lambda e, d=d: e.activation(out=sp[d][:], in_=ex[:], func=AF.Ln, bias=C.oneb[:, 0:1]),
                  r=[L + "ex", "oneb"], w=[(L, "sp", par, d)])
            yield
        for cb in range(8):
            pi = next_psA()
            pst = C.psA[pi]
            for kc in range(8):
                P.add("pe", lambda e, kc=kc, cb=cb, pst=pst: e.matmul(
                    pst[:], lhsT=hT[:, kc * 128:(kc + 1) * 128], rhs=Wvg[:, kc, cb * 512:(cb + 1) * 512],
                    start=(kc == 0), stop=(kc == 7)), r=[hTb, (L, "Wvg", kc)], w=[("psA", pi)])
            c0 = (cb % 4) * 512
            if cb < 4:
                P.add("dve", lambda e, pst=pst, c0=c0: e.tensor_copy(out=vb[:, c0:c0 + 512], in_=pst[:]),
                      r=[("psA", pi)], w=[(L, "vb", par, cb)])
            else:
                P.add("act", lambda e, pst=pst, c0=c0: e.activation(out=sg[:, c0:c0 + 512], in_=pst[:], func=AF.Silu),
                      r=[("psA", pi)], w=[(L, "sg", par, cb - 4)])
            yield

    def back1(ti, par):
        rows = slice(ti * 128, (ti + 1) * 128)
        qT, kT, sp, vb, sg = qT2[par], kT2[par], sp2[par], vb2[par], sg2[par]
        for d in range(2):
            for m in range(3):
                pi = next_psA()
                pst = C.psA[pi]
                for h in range(4):
                    P.add("pe", lambda e, d=d, m=m, h=h, pst=pst: e.matmul(
                        pst[:, h * 128:(h + 1) * 128], lhsT=sp[d][:, h * 128:(h + 1) * 128], rhs=cm[:, d * 3 + m, :],
                        start=True, stop=True), r=[(L, "sp", par, d), L + "cm"], w=[("psA", pi)])
                if m == 0:
                    P.add("act", lambda e, pst=pst: e.activation(out=E[0][:], in_=pst[:], func=AF.Exp), r=[("psA", pi)], w=[(L, "E", 0)])
                    P.add("dve", lambda e, d=d: e.tensor_tensor(out=qq[d][:], in0=qT[:], in1=E[0][:], op=ALU.mult),
                          r=[(L, "qT", par), (L, "E", 0)], w=[(L, "qq", d)])
                    P.add("act", lambda e, pst=pst: e.activation(out=E[1][:], in_=pst[:], func=AF.Exp, scale=-1.0),
                          r=[("psA", pi)], w=[(L, "E", 1)])
                    P.add("dve", lambda e, d=d: e.tensor_tensor(out=kk[d][:], in0=kT[:], in1=E[1][:], op=ALU.mult),
                          r=[(L, "kT", par), (L, "E", 1)], w=[(L, "kk", d)])
                elif m == 1:
                    P.add("act", lambda e, pst=pst: e.activation(out=E[0][:], in_=pst[:], func=AF.Exp), r=[("psA", pi)], w=[(L, "E", 0)])
                    P.add("dve", lambda e, d=d: e.tensor_tensor(out=qi[d][:], in0=qT[:], in1=E[0][:], op=ALU.mult),
                          r=[(L, "qT", par), (L, "E", 0)], w=[(L, "qi", d)])
                    col = 127 if d == 0 else 0
                    P.add("dve", lambda e, d=d, col=col: e.tensor_copy(
                        out=dec[:, d * 4:(d + 1) * 4], in_=E[0][:].rearrange("p (h t) -> p h t", h=4)[:, :, col]),
                        r=[(L, "E", 0)], w=[(L, "dec", d)])
                else:
                    P.add("act", lambda e, pst=pst: e.activation(out=E[1][:], in_=pst[:], func=AF.Exp), r=[("psA", pi)], w=[(L, "E", 1)])
                    P.add("dve", lambda e, d=d: e.tensor_tensor(out=ki[d][:], in0=kT[:], in1=E[1][:], op=ALU.mult),
                          r=[(L, "kT", par), (L, "E", 1)], w=[(L, "ki", d)])
                yield
        for d in range(2):
            for h in range(4):
                blk = d * 4 + h
                P.add("pe", lambda e, d=d, h=h, blk=blk: e.transpose(out=C.psT[:, blk * 128:(blk + 1) * 128],
                                                                      in_=ki[d][:, h * 128:(h + 1) * 128], identity=C.ident[:]),
                      r=[(L, "ki", d), "ident"], w=["psT"])
        P.add("act", lambda e: e.copy(out=kiT[:], in_=C.psT[:]), r=["psT"], w=[L + "kiT"])
        yield
        for d in range(2):
            pi = next_psA()
            pst = C.psA[pi]
            for h in range(4):
                P.add("pe", lambda e, d=d, h=h, pst=pst: e.matmul(
                    pst[:, h * 128:(h + 1) * 128], lhsT=kk[d][:, h * 128:(h + 1) * 128], rhs=qq[d][:, h * 128:(h + 1) * 128],
                    start=True, stop=True), r=[(L, "kk", d), (L, "qq", d)], w=[("psA", pi)])
            P.add("dve", lambda e, d=d, pst=pst: e.tensor_tensor(out=scT[d][:], in0=pst[:], in1=gmask[:, d, :], op=ALU.mult),
                  r=[("psA", pi), L + "gmask"], w=[(L, "scT", d)])
            yield
        for h in range(4):
            pi = next_psA()
            pst = C.psA[pi]
            hs = slice(h * 128, (h + 1) * 128)
            vs_ = slice(h * 512, (h + 1) * 512)
            P.add("pe", lambda e, pst=pst, hs=hs, vs_=vs_: e.matmul(pst[:], lhsT=scT[0][:, hs], rhs=vb[:, vs_], start=True, stop=False),
                  r=[(L, "scT", 0), (L, "vb", par, h)], w=[("psA", pi)])
            P.add("pe", lambda e, pst=pst, hs=hs, vs_=vs_: e.matmul(pst[:], lhsT=scT[1][:, hs], rhs=vb[:, vs_], start=False, stop=False),
                  r=[(L, "scT", 1), (L, "vb", par, h)], w=[("psA", pi)])
            P.add("pe", lambda e, pst=pst, hs=hs, vs_=vs_: e.matmul(pst[:], lhsT=qi[1][:, hs], rhs=Stb[:, vs_], start=False, stop=True),
                  r=[(L, "qi", 1), (L, "Stb", h)], w=[("psA", pi)])
            P.add("act", lambda e, pst=pst, vs_=vs_: e.copy(out=opart[:, vs_], in_=pst[:]), r=[("psA", pi)], w=[(L, "opart", h)])
            yield
        for h in range(4):
            pi = next_psA()
            pst = C.psA[pi]
            hs2 = slice((4 + h) * 128, (5 + h) * 128)
            vs_ = slice(h * 512, (h + 1) * 512)
            P.add("pe", lambda e, pst=pst, hs2=hs2, vs_=vs_: e.matmul(pst[:], lhsT=kiT[:, hs2], rhs=vb[:, vs_], start=True, stop=True),
                  r=[L + "kiT", (L, "vb", par, h)], w=[("psA", pi)])
            P.add("dve", lambda e, pst=pst, vs_=vs_, h=h: e.scalar_tensor_tensor(
                out=St[:, vs_], in0=St[:, vs_], scalar=dec[:, 4 + h:5 + h], in1=pst[:], op0=ALU.mult, op1=ALU.add),
                r=[("psA", pi), (L, "dec", 1), L + "St"], w=[L + "St"])
            P.add("act", lambda e, vs_=vs_: e.copy(out=Stb[:, vs_], in_=St[:, vs_]), r=[L + "St"], w=[(L, "Stb", h)])
            yield
        k2 = ti % 2
        P.add("sp", lambda e: e.dma_start(out=st_o[rows, :], in_=opart[:]), r=[(L, "opart", h) for h in range(4)],
              w=[(L, "d_o", ti)], dma=(L, "s0", k2))
        P.add("sp", lambda e: e.dma_start(out=st_v[rows, :], in_=vb[:]), r=[(L, "vb", par, h) for h in range(4)],
              w=[(L, "d_v", ti)], dma=(L, "s1", k2))
        P.add("sp", lambda e: e.dma_start(out=st_sg[rows, :], in_=sg[:]), r=[(L, "sg", par, h) for h in range(4)],
              w=[(L, "d_sg", ti)], dma=(L, "s2", k2))
        P.add("sp", lambda e: e.dma_start(out=st_qi[rows, :], in_=qi[0][:]), r=[(L, "qi", 0)], w=[(L, "d_qi", ti)], dma=(L, "s3", k2))
        P.add("sp", lambda e: e.dma_start(out=st_ki[rows, :], in_=kiT[:, 0:512]), r=[L + "kiT"], w=[(L, "d_ki", ti)], dma=(L, "s4", k2))
        P.add("sp", lambda e: e.dma_start(out=st_df[rows, :], in_=dec[:, 0:4]), r=[(L, "dec", 0)], w=[(L, "d_df", ti)], dma=(L, "s5", k2))

        yield

    order = list(reversed(range(NT)))
    interleave([front1(order[0], 0)])
    for n, ti in enumerate(order):
        gens = [back1(ti, n % 2)]
        if n + 1 < NT:
            gens.append(front1(order[n + 1], (n + 1) % 2))
        interleave(gens)

    P.add("dve", lambda e: e.memset(St[:], 0.0), r=[L + "St"], w=[L + "St"])
    P.add("pool", lambda e: e.memset(Stb[:], 0.0), w=[(L, "Stb", h) for h in range(4)])
    wo = W["b_w_out"][0].rearrange("(kc p) f -> p kc f", p=128)
    for kc in range(0, 16, 4):
        P.add("pool", lambda e, kc=kc: e.dma_start(out=Wout[:, kc:kc + 4, :], in_=wo[:, kc:kc + 4, :]),
              w=[(L, "Wout", kc), (L, "Wvg", kc // 4)], dma=(L, "W", 0))

    vb, sg = vb2[0], sg2[0]

    def pass2(ti):
        slot = ti % 2
        rows = slice(ti * 128, (ti + 1) * 128)
        xt = C.xt[slot]
        P.add("sp", lambda e: e.dma_start(out=xt[:], in_=x_src[rows, :]), w=[("xt", slot)], dma=("xt", slot))
        P.add("sp", lambda e: e.dma_start(out=opart[:], in_=st_o[rows, :]), r=[(L, "d_o", ti)], w=[(L, "opart", h) for h in range(4)], dma=(L, "l0"))
        P.add("sp", lambda e: e.dma_start(out=vb[:], in_=st_v[rows, :]), r=[(L, "d_v", ti)], w=[(L, "vb", 0, h) for h in range(4)], dma=(L, "l1"))
        P.add("sp", lambda e: e.dma_start(out=sg[:], in_=st_sg[rows, :]), r=[(L, "d_sg", ti)], w=[(L, "sg", 0, h) for h in range(4)], dma=(L, "l2"))
        P.add("sp", lambda e: e.dma_start(out=qi2[:], in_=st_qi[rows, :]), r=[(L, "d_qi", ti)], w=[L + "qi2"], dma=(L, "l3"))
        P.add("sp", lambda e: e.dma_start(out=kiT2[:], in_=st_ki[rows, :]), r=[(L, "d_ki", ti)], w=[L + "kiT2"], dma=(L, "l4"))
        P.add("sp", lambda e: e.dma_start(out=dec2[:], in_=st_df[rows, :]), r=[(L, "d_df", ti)], w=[L + "dec2"], dma=(L, "l5"))
        for h in range(4):
            pi = next_psA()
            pst = C.psA[pi]
            hs = slice(h * 128, (h + 1) * 128)
            vs_ = slice(h * 512, (h + 1) * 512)
            P.add("pe", lambda e, pst=pst, hs=hs, vs_=vs_: e.matmul(pst[:], lhsT=qi2[:, hs], rhs=Stb[:, vs_], start=True, stop=True),
                  r=[L + "qi2", (L, "Stb", h)], w=[("psA", pi)])
            P.add("dve", lambda e, pst=pst, vs_=vs_: e.tensor_tensor(out=opart[:, vs_], in0=pst[:], in1=opart[:, vs_], op=ALU.add),
                  r=[("psA", pi), (L, "opart", h)], w=[(L, "opart", h)])
            P.add("act", lambda e, vs_=vs_, h=h: e.activation(out=C.junk[:, 0:512], in_=opart[:, vs_], func=AF.Square,
                                                               accum_out=st[:, h:h + 1]),
                  r=[(L, "opart", h)], w=["junk", (L, "sso", h)])
        P.add("act", lambda e: e.activation(out=st[:, 4:8], in_=st[:, 0:4], func=AF.Sqrt, scale=1.0 / 512, bias=C.epsb[:, 0:1]),
              r=[(L, "sso", h) for h in range(4)] + ["epsb"], w=[L + "st4"])
        P.add("dve", lambda e: e.reciprocal(out=st[:, 8:12], in_=st[:, 4:8]), r=[L + "st4"], w=[L + "st8"])
        for h in range(4):
            vs_ = slice(h * 512, (h + 1) * 512)
            P.add("dve", lambda e, vs_=vs_, h=h: e.scalar_tensor_tensor(
                out=opart[:, vs_], in0=opart[:, vs_], scalar=st[:, 8 + h:9 + h], in1=ogtab[:], op0=ALU.mult, op1=ALU.mult),
                r=[(L, "opart", h), L + "st8", L + "ogtab"], w=[(L, "opart", h)])
            P.add("pool", lambda e, vs_=vs_: e.tensor_tensor(out=yb[:, vs_], in0=opart[:, vs_], in1=sg[:, vs_], op=ALU.mult),
                  r=[(L, "opart", h), (L, "sg", 0, h)], w=[(L, "yb", h)])
        for h in range(4):
            pi = next_psA()
            pst = C.psA[pi]
            hs = slice(h * 128, (h + 1) * 128)
            vs_ = slice(h * 512, (h + 1) * 512)
            P.add("pe", lambda e, pst=pst, hs=hs, vs_=vs_: e.matmul(pst[:], lhsT=kiT2[:, hs], rhs=vb[:, vs_], start=True, stop=True),
                  r=[L + "kiT2", (L, "vb", 0, h)], w=[("psA", pi)])
            P.add("dve", lambda e, pst=pst, vs_=vs_, h=h: e.scalar_tensor_tensor(
                out=St[:, vs_], in0=St[:, vs_], scalar=dec2[:, h:h + 1], in1=pst[:], op0=ALU.mult, op1=ALU.add),
                r=[("psA", pi), L + "dec2", L + "St"], w=[L + "St"])
            P.add("act", lambda e, vs_=vs_: e.copy(out=Stb[:, vs_], in_=St[:, vs_]), r=[L + "St"], w=[(L, "Stb", h)])
        emit_out_proj(P, C, L, li, ti, yb, [(L, "yb", h) for h in range(4)], yT, Wout, Wout_bufs, xt, ("xt", slot), xn, x_dst, rows)

    for ti in range(NT):
        pass2(ti)


BAND = 140.0


def layer_C(nc, P, C, es0, x_src, x_dst, li, NT, W):
    import contextlib
    import math
    L = "C%d" % li
    S = NT * 128
    lambda_init = 0.8 - 0.6 * math.exp(-0.3 * li)
    w_in = W["c_w_in"][0]
    wv = w_in.rearrange("(kc p) f -> p kc f", p=128)

    st_qT = C.dram(L + "st_qT", [16, 128, S], BF16)
    st_kT = C.dram(L + "st_kT", [16, 128, S], BF16)
    st_v = C.dram(L + "st_v", [S, DI], BF16)
    st_sz = C.dram(L + "st_sz", [S, DI], BF16)
    st_o = C.dram(L + "st_o", [S, DI], F32)

    def next_psA():
        i = C.pa[0] % 3
        C.pa[0] += 1
        return i

    P.add("sp", lambda e: e.dma_start(out=C.gtab[:], in_=W["norm_g"][li:li + 1, :].partition_broadcast(128)), w=["gtab"], dma="gtab")

    with contextlib.ExitStack() as es:
        def sb(name, shape, dt):
            return es.enter_context(nc.sbuf_tensor("sb_" + L + "a" + name, list(shape), dt))
        Wqk = sb("Wqk", [128, 8, 4096], BF16)
        gt = [sb("gt%d" % i, [128, 128], F32) for i in range(2)]
        qf = sb("qf", [128, DI], F32)
        sq = sb("sq", [128, 512], F32)
        ssq = sb("ssq", [128, 48], F32)
        qn = sb("qn", [128, DI], BF16)
        qTt = sb("qTt", [128, DI], BF16)
        for kc in range(8):
            P.add("pool", lambda e, kc=kc: e.dma_start(out=Wqk[:, kc, :], in_=wv[:, kc, 0:4096]), w=[(L, "Wqk", kc)], dma=(L, "W", kc % 2))
        P.add("sp", lambda e: e.dma_start(out=gt[0][:], in_=W["c_q_g"][0:1, :].partition_broadcast(128)), w=[(L, "gt", 0)], dma=L + "gt0")
        P.add("sp", lambda e: e.dma_start(out=gt[1][:], in_=W["c_k_g"][0:1, :].partition_broadcast(128)), w=[(L, "gt", 1)], dma=L + "gt1")

        def p1a(ti):
            slot = ti % 2
            rows = slice(ti * 128, (ti + 1) * 128)
            emit_norm_T(P, C, x_src[rows, :], L, C.gtab, slot)
            hT = C.hT[slot]
            for qk in range(2):
                for cb in range(4):
                    pi = next_psA()
                    pst = C.psA[pi]
                    col = qk * 2048 + cb * 512
                    for kc in range(8):
                        P.add("pe", lambda e, kc=kc, col=col, pst=pst: e.matmul(
                            pst[:], lhsT=hT[:, kc * 128:(kc + 1) * 128], rhs=Wqk[:, kc, col:col + 512],
                            start=(kc == 0), stop=(kc == 7)), r=[("hT", slot), (L, "Wqk", kc)], w=[("psA", pi)])
                    P.add("act", lambda e, pst=pst, cb=cb: e.copy(out=qf[:, cb * 512:(cb + 1) * 512], in_=pst[:]),
                          r=[("psA", pi)], w=[(L, "qf", cb)])
                    P.add("pool", lambda e, cb=cb: e.tensor_tensor(out=sq[:], in0=qf[:, cb * 512:(cb + 1) * 512],
                                                                   in1=qf[:, cb * 512:(cb + 1) * 512], op=ALU.mult),
                          r=[(L, "qf", cb)], w=[L + "sq"])
                    P.add("dve", lambda e, cb=cb: e.tensor_reduce(out=ssq[:, cb * 4:(cb + 1) * 4],
                                                                   in_=sq[:].rearrange("p (g d) -> p g d", g=4), axis=AX.X, op=ALU.add),
                          r=[L + "sq"], w=[(L, "ssq", cb)])
                P.add("act", lambda e: e.activation(out=ssq[:, 16:32], in_=ssq[:, 0:16], func=AF.Sqrt, scale=1.0 / 128, bias=C.epsb[:, 0:1]),
                      r=[(L, "ssq", cb) for cb in range(4)] + ["epsb"], w=[L + "ssq16"])
                P.add("dve", lambda e: e.reciprocal(out=ssq[:, 32:48], in_=ssq[:, 16:32]), r=[L + "ssq16"], w=[L + "ssq32"])
                if qk == 0:
                    P.add("dve", lambda e: e.tensor_scalar(out=ssq[:, 32:48], in0=ssq[:, 32:48], scalar1=128.0 ** -0.5, scalar2=None, op0=ALU.mult),
                          r=[L + "ssq32"], w=[L + "ssq32"])
                for g in range(16):
                    P.add("dve", lambda e, g=g, qk=qk: e.scalar_tensor_tensor(
                        out=qn[:, g * 128:(g + 1) * 128], in0=qf[:, g * 128:(g + 1) * 128], scalar=ssq[:, 32 + g:33 + g],
                        in1=gt[qk][:], op0=ALU.mult, op1=ALU.mult),
                        r=[(L, "qf", g // 4), L + "ssq32", (L, "gt", qk)], w=[(L, "qn", g // 4)])
                for half in range(2):
                    for g8 in range(8):
                        g = half * 8 + g8
                        P.add("pe", lambda e, g=g, g8=g8: e.transpose(out=C.psT[:, g8 * 128:(g8 + 1) * 128], in_=qn[:, g * 128:(g + 1) * 128],
                                                                      identity=C.ident[:]),
                              r=[(L, "qn", g // 4), "ident"], w=["psT"])
                    P.add("act", lambda e, half=half: e.copy(out=qTt[:, half * 1024:(half + 1) * 1024], in_=C.psT[:]),
                          r=["psT"], w=[(L, "qTt", half)])
                dst = st_qT if qk == 0 else st_kT
                P.add("sp", lambda e, dst=dst: e.dma_start(out=dst.rearrange("g d s -> d g s")[:, :, ti * 128:(ti + 1) * 128],
                                                           in_=qTt[:].rearrange("p (g t) -> p g t", g=16)),
                      r=[(L, "qTt", 0), (L, "qTt", 1)], w=[(L, "d_qk", qk, ti)], dma=(L, "sq", qk))
        for ti in range(NT):
            p1a(ti)
    P.barrier()

    with contextlib.ExitStack() as es:
        def sb(name, shape, dt):
            return es.enter_context(nc.sbuf_tensor("sb_" + L + "b" + name, list(shape), dt))
        Wvz = sb("Wvz", [128, 8, 4096], BF16)
        vb = sb("vb", [128, DI], BF16)
        szb = sb("szb", [128, DI], BF16)
        for kc in range(8):
            P.add("pool", lambda e, kc=kc: e.dma_start(out=Wvz[:, kc, :], in_=wv[:, kc, 4096:8192]), w=[(L, "Wvz", kc)], dma=(L, "W", kc % 2))

        def p1b(ti):
            slot = ti % 2
            rows = slice(ti * 128, (ti + 1) * 128)
            emit_norm_T(P, C, x_src[rows, :], L, C.gtab, slot)
            hT = C.hT[slot]
            for cb in range(8):
                pi = next_psA()
                pst = C.psA[pi]
                for kc in range(8):
                    P.add("pe", lambda e, kc=kc, cb=cb, pst=pst: e.matmul(
                        pst[:], lhsT=hT[:, kc * 128:(kc + 1) * 128], rhs=Wvz[:, kc, cb * 512:(cb + 1) * 512],
                        start=(kc == 0), stop=(kc == 7)), r=[("hT", slot), (L, "Wvz", kc)], w=[("psA", pi)])
                c0 = (cb % 4) * 512
                if cb < 4:
                    P.add("dve", lambda e, pst=pst, c0=c0: e.tensor_copy(out=vb[:, c0:c0 + 512], in_=pst[:]),
                          r=[("psA", pi)], w=[(L, "vb", cb)])
                else:
                    P.add("act", lambda e, pst=pst, c0=c0: e.activation(out=szb[:, c0:c0 + 512], in_=pst[:], func=AF.Silu),
                          r=[("psA", pi)], w=[(L, "szb", cb - 4)])
            P.add("sp", lambda e: e.dma_start(out=st_v[rows, :], in_=vb[:]), r=[(L, "vb", i) for i in range(4)], w=[(L, "d_v", ti)], dma=(L, "sv", ti % 2))
            P.add("sp", lambda e: e.dma_start(out=st_sz[rows, :], in_=szb[:]), r=[(L, "szb", i) for i in range(4)], w=[(L, "d_sz", ti)], dma=(L, "ssz", ti % 2))
        for ti in range(NT):
            p1b(ti)
    P.barrier()

    with contextlib.ExitStack() as es:
        def sb(name, shape, dt):
            return es.enter_context(nc.sbuf_tensor("sb_" + L + "c" + name, list(shape), dt))
        kS = sb("kS", [128, 2, S], BF16)
        vS = sb("vS", [128, NT, 257], BF16)
        tab = sb("tab", [128, 4, 512], F32)
        qS = [sb("qS%d" % i, [128, 2, 256], BF16) for i in range(2)]
        tmp = [sb("tmp%d" % i, [128, 512], F32) for i in range(4)]
        pT = [sb("pT%d" % i, [128, 512], BF16) for i in range(4)]
        sbank = [C.psA[0][:, :], C.psA[1][:, :], C.psO[:, 0:512], C.psO[:, 512:1024]]
        sbankb = [("psA", 0), ("psA", 1), ("psO", 0), ("psO", 1)]
        lam = sb("lam", [128, 4, 128], F32)
        lw = sb("lw", [128, 2, 128], F32)
        lv = sb("lv", [128, 8], F32)
        rr = sb("rr", [128, 8], F32)
        ot = [sb("ot%d" % i, [128, 256], F32) for i in range(2)]
        oo = [sb("oo%d" % i, [128, 256], F32) for i in range(2)]
        acc = [C.psA[2][:, 0:257], C.bankT[:, 0:257], C.bankY[:, 0:257], C.bankY[:, 512:769]]
        accb = [("psA", 2), "psT", ("psYT", 0), ("psYT", 1)]

        P.add("sp", lambda e: e.dma_start(out=lam[:].rearrange("p a b -> p (a b)"),
                                          in_=W["c_lam"][0:1].rearrange("o a b -> o (a b)").partition_broadcast(128)),
              w=[L + "lam"], dma=L + "lam")
        P.add("dve", lambda e: e.tensor_tensor(out=lw[:, 0, :], in0=lam[:, 0, :], in1=lam[:, 1, :], op=ALU.mult), r=[L + "lam"], w=[L + "lw0"])
        P.add("dve", lambda e: e.tensor_tensor(out=lw[:, 1, :], in0=lam[:, 2, :], in1=lam[:, 3, :], op=ALU.mult), r=[L + "lam"], w=[L + "lw1"])
        P.add("dve", lambda e: e.tensor_reduce(out=lv[:, 0:2], in_=lw[:], axis=AX.X, op=ALU.add), r=[L + "lw0", L + "lw1"], w=[L + "lv0"])
        P.add("act", lambda e: e.activation(out=lv[:, 2:4], in_=lv[:, 0:2], func=AF.Exp), r=[L + "lv0"], w=[L + "lv2"])
        P.add("dve", lambda e: e.tensor_tensor(out=lv[:, 4:5], in0=lv[:, 2:3], in1=lv[:, 3:4], op=ALU.subtract), r=[L + "lv2"], w=[L + "lv4"])
        P.add("dve", lambda e: e.tensor_scalar(out=lv[:, 5:6], in0=lv[:, 4:5], scalar1=-1.0, scalar2=-lambda_init, op0=ALU.mult, op1=ALU.add),
              r=[L + "lv4"], w=[L + "neglam"])
        P.add("pool", lambda e: e.memset(vS[:], 1.0), w=[L + "vS"])

        QT = 256
        NQ = S // QT
        cnt = [0]
        for h in range(8):
            slope = 2.0 ** (-(h + 1))
            dmax = BAND / slope
            for m in range(2):
                P.add("sp", lambda e, h=h, m=m: e.dma_start(out=kS[:, m, :], in_=st_kT[2 * h + m]), w=[(L, "kS", m)], dma=(L, "kS", m))
            NPART = max(4, NT // 8)
            for part in range(NPART):
                n0 = part * NT // NPART
                n1 = (part + 1) * NT // NPART
                if n1 > n0:
                    P.add("pool", lambda e, h=h, n0=n0, n1=n1: e.dma_start(
                        out=vS[:, n0:n1, 0:256], in_=st_v[n0 * 128:n1 * 128, h * 256:(h + 1) * 256].rearrange("(n p) c -> p n c", p=128)),
                        r=[], w=[L + "vS"], dma=(L, "vS", part % 4))
            P.add("sp", lambda e, h=h: e.dma_start(out=tab[:], in_=C.alibi_in[h].rearrange("a p j -> p a j")), w=[L + "tab"], dma=L + "tab")

            units = []
            for qi_ in range(NQ):
                q0 = qi_ * QT
                kbs = []
                for kb in range(NT):
                    k0 = kb * 128
                    if k0 >= q0 + QT:
                        dist = k0 - (q0 + QT - 1)
                    elif k0 + 127 < q0:
                        dist = q0 - (k0 + 127)
                    else:
                        dist = 0
                    if dist <= dmax:
                        kbs.append(kb)
                for ik, kb in enumerate(kbs):
                    units.append((qi_, kb, ik, len(kbs)))

            def front(un, h=h, slope=slope):
                qi_, kb, ik, nk = un
                q0 = qi_ * QT
                qslot = qi_ % 2
                qs_ = qS[qslot]
                if ik == 0:
                    P.add("sp", lambda e: e.dma_start(out=qs_[:], in_=st_qT[2 * h:2 * h + 2, :, q0:q0 + QT].rearrange("m d s -> d m s")),
                          w=[(L, "qS", qslot)], dma=(L, "qS", qslot))
                k0 = kb * 128
                delta = q0 - k0
                if delta >= 128:
                    tsel, cc = 0, -slope * delta
                elif delta <= -256:
                    tsel, cc = 1, slope * delta
                elif delta == 0:
                    tsel, cc = 2, 0.0
                else:
                    assert delta == -128
                    tsel, cc = 3, 0.0
                u = cnt[0] % 4
                cnt[0] += 1
                pst = sbank[u]
                for m in range(2):
                    P.add("pe", lambda e, m=m: e.matmul(
                        pst[:, m * 256:(m + 1) * 256], lhsT=kS[:, m, k0:k0 + 128], rhs=qs_[:, m, :], start=True, stop=True),
                        r=[(L, "kS", m), (L, "qS", qslot)], w=[sbankb[u]])
                tm = tmp[u]
                P.add("dve", lambda e: e.scalar_tensor_tensor(
                    out=tm[:], in0=pst, scalar=float(cc), in1=tab[:, tsel, :], op0=ALU.add, op1=ALU.add),
                    r=[sbankb[u], L + "tab"], w=[(L, "tmp", u)])
                pt = pT[u]
                P.add("act", lambda e: e.activation(out=pt[:], in_=tm[:], func=AF.Exp),
                      r=[(L, "tmp", u)], w=[(L, "pT", u)])
                return u

            def back(un, u, h=h):
                qi_, kb, ik, nk = un
                q0 = qi_ * QT
                pt = pT[u]
                for m in range(2):
                    for qh in range(2):
                        a = m * 2 + qh
                        P.add("pe", lambda e, a=a, m=m, qh=qh: e.matmul(
                            acc[a], lhsT=pt[:, m * 256 + qh * 128:m * 256 + (qh + 1) * 128], rhs=vS[:, kb, :],
                            start=(ik == 0), stop=(ik == nk - 1)),
                            r=[(L, "pT", u), L + "vS"], w=[accb[a]])
                if ik != nk - 1:
                    return
                for a in range(4):
                    P.add("dve", lambda e, a=a: e.reciprocal(out=rr[:, a:a + 1], in_=acc[a][:, 256:257]), r=[accb[a]], w=[(L, "rr", a)])
                for qh in range(2):
                    a0, a1 = qh, 2 + qh
                    P.add("dve", lambda e, a1=a1: e.tensor_tensor(out=rr[:, 4 + a1:5 + a1], in0=rr[:, a1:a1 + 1], in1=lv[:, 5:6], op=ALU.mult),
                          r=[(L, "rr", a1), L + "neglam"], w=[(L, "rl", a1)])
                    P.add("act", lambda e, a0=a0, qh=qh: e.activation(out=ot[qh][:], in_=acc[a0][:, 0:256], func=AF.Copy, scale=rr[:, a0:a0 + 1]),
                          r=[accb[a0], (L, "rr", a0)], w=[(L, "ot", qh)])
                    o_ = oo[qh]
                    P.add("dve", lambda e, a1=a1, o_=o_, qh=qh: e.scalar_tensor_tensor(
                        out=o_[:], in0=acc[a1][:, 0:256], scalar=rr[:, 4 + a1:5 + a1], in1=ot[qh][:], op0=ALU.mult, op1=ALU.add),
                        r=[accb[a1], (L, "rl", a1), (L, "ot", qh)], w=[(L, "oo", qh)])
                    r0 = q0 + qh * 128
                    P.add("sp", lambda e, o_=o_, r0=r0: e.dma_start(out=st_o[r0:r0 + 128, h * 256:(h + 1) * 256], in_=o_[:]),
                          r=[(L, "oo", qh)], w=[(L, "d_o", h, qi_, qh)], dma=(L, "so", qh))

            LAG = 3
            ubuf = {}
            for idx in range(len(units) + LAG):
                if idx < len(units):
                    ubuf[idx] = front(units[idx])
                if idx - LAG >= 0:
                    back(units[idx - LAG], ubuf.pop(idx - LAG))
    P.barrier()

    with contextlib.ExitStack() as es:
        def sb(name, shape, dt):
            return es.enter_context(nc.sbuf_tensor("sb_" + L + "d" + name, list(shape), dt))
        Wout = sb("Wout", [128, 16, D], BF16)
        ogtab = sb("ogtab", [128, 256], F32)
        of = sb("of", [128, DI], F32)
        sq3 = sb("sq", [128, DI], F32)
        szb3 = sb("szb", [128, DI], BF16)
        st = sb("st", [128, 32], F32)
        yb = sb("yb", [128, DI], BF16)
        yT = sb("yT", [128, DI], BF16)
        xn = sb("xn", [128, D], F32)
        wo = W["c_w_out"][0].rearrange("(kc p) f -> p kc f", p=128)
        for kc in range(0, 16, 4):
            P.add("pool", lambda e, kc=kc: e.dma_start(out=Wout[:, kc:kc + 4, :], in_=wo[:, kc:kc + 4, :]), w=[(L, "Wout", kc)], dma=(L, "W", 0))
        Wout_bufs = [(L, "Wout", kc) for kc in range(0, 16, 4)]
        P.add("sp", lambda e: e.dma_start(out=ogtab[:], in_=W["c_o_g"][0:1, :].partition_broadcast(128)), w=[L + "ogtab"], dma=L + "ogtab")

        def p3(ti):
            slot = ti % 2
            rows = slice(ti * 128, (ti + 1) * 128)
            xt = C.xt[slot]
            P.add("sp", lambda e: e.dma_start(out=xt[:], in_=x_src[rows, :]), w=[("xt", slot)], dma=("xt", slot))
            P.add("sp", lambda e: e.dma_start(out=of[:], in_=st_o[rows, :]), w=[L + "of"], dma=L + "lof")
            P.add("sp", lambda e: e.dma_start(out=szb3[:], in_=st_sz[rows, :]), w=[L + "szb3"], dma=L + "lsz")
            P.add("pool", lambda e: e.tensor_tensor(out=sq3[:], in0=of[:], in1=of[:], op=ALU.mult), r=[L + "of"], w=[L + "sq3"])
            P.add("dve", lambda e: e.tensor_reduce(out=st[:, 0:8], in_=sq3[:].rearrange("p (g d) -> p g d", g=8), axis=AX.X, op=ALU.add),
                  r=[L + "sq3"], w=[L + "st0"])
            P.add("act", lambda e: e.activation(out=st[:, 8:16], in_=st[:, 0:8], func=AF.Sqrt, scale=1.0 / 256, bias=C.epsb[:, 0:1]),
                  r=[L + "st0", "epsb"], w=[L + "st8"])
            P.add("dve", lambda e: e.reciprocal(out=st[:, 16:24], in_=st[:, 8:16]), r=[L + "st8"], w=[L + "st16"])
            P.add("dve", lambda e: e.tensor_scalar(out=st[:, 16:24], in0=st[:, 16:24], scalar1=1.0 - lambda_init, scalar2=None, op0=ALU.mult),
                  r=[L + "st16"], w=[L + "st16"])
            for g in range(8):
                P.add("dve", lambda e, g=g: e.scalar_tensor_tensor(
                    out=of[:, g * 256:(g + 1) * 256], in0=of[:, g * 256:(g + 1) * 256], scalar=st[:, 16 + g:17 + g], in1=ogtab[:],
                    op0=ALU.mult, op1=ALU.mult), r=[L + "of", L + "st16", L + "ogtab", L + "sq3"], w=[L + "of"])
            P.add("pool", lambda e: e.tensor_tensor(out=yb[:], in0=of[:], in1=szb3[:], op=ALU.mult), r=[L + "of", L + "szb3"], w=[L + "yb"])
            emit_out_proj(P, C, L, li, ti, yb, [L + "yb"] * 4, yT, Wout, Wout_bufs, xt, ("xt", slot), xn, x_dst, rows)
        for ti in range(NT):
            p3(ti)


def emit_out_proj(P, C, L, li, ti, yb, yb_bufs, yT, Wout, Wout_bufs, xt, xtb, xn, x_dst, rows):
    for kc in range(16):
        P.add("pe", lambda e, kc=kc: e.transpose(out=C.psYT[:, kc * 128:(kc + 1) * 128], in_=yb[:, kc * 128:(kc + 1) * 128],
                                                  identity=C.ident[:]),
              r=[yb_bufs[kc // 4], "ident"], w=["psYT"])
    P.add("act", lambda e: e.copy(out=yT[:], in_=C.psYT[:]), r=["psYT"], w=[L + "yT"])
    for nb in range(2):
        for kc in range(16):
            P.add("pe", lambda e, kc=kc, nb=nb: e.matmul(
                C.psO[:, nb * 512:(nb + 1) * 512], lhsT=yT[:, kc * 128:(kc + 1) * 128],
                rhs=Wout[:, kc, nb * 512:(nb + 1) * 512], start=(kc == 0), stop=(kc == 15)),
                r=[L + "yT", Wout_bufs[kc // 4]], w=[("psO", nb)])
    P.add("dve", lambda e: e.tensor_tensor(out=xn[:], in0=C.psO[:], in1=xt[:], op=ALU.add),
          r=[("psO", 0), ("psO", 1), xtb], w=[L + "xn"])
    P.add("sp", lambda e: e.dma_start(out=x_dst[rows, :], in_=xn[:]),
          r=[L + "xn"], w=[("xdram", li, ti)], dma=("xst", li, ti % 2))


def host_consts():
    ident = np.eye(128, dtype=np.float32).astype(ml_dtypes.bfloat16)
    tp = np.arange(128)[:, None]
    t = np.arange(128)[None, :]
    f = lambda m: m.astype(np.float32)
    cm = np.stack([
        f(tp <= t) - f(tp <= 64), f(tp <= t), f(tp > t),
        f(tp >= t) - f(tp >= 63), f(tp >= t), f(tp < t),
    ]).astype(np.float32) * (-1.0 / 16.0)
    sidx = np.arange(128)[:, None]
    tidx = np.arange(128)[None, :]
    mf = (tidx >= sidx).astype(np.float32)
    mb = (tidx < sidx).astype(np.float32)
    gmask = np.stack([np.tile(mf, (1, 4)), np.tile(mb, (1, 4))]).astype(ml_dtypes.bfloat16)
    p = np.arange(128, dtype=np.float64)[:, None]
    j = np.tile(np.arange(256, dtype=np.float64), 2)[None, :]
    alibi = np.zeros((8, 4, 128, 512), np.float32)
    for h in range(8):
        slope = 2.0 ** (-(h + 1))
        alibi[h, 0] = -slope * (j - p)
        alibi[h, 1] = -slope * (p - j)
        alibi[h, 2] = -slope * np.abs(j - p)
        alibi[h, 3] = -slope * np.abs(j - p - 128)
    return {"ident": ident, "cm": cm, "gmask": gmask, "alibi": alibi}


def run_model(inputs, S, depth):
    nc = build_program(S, depth)
    xall = np.concatenate([np.asarray(inputs["x_prompt"], np.float32), np.asarray(inputs["x_sample"], np.float32)], axis=0)
    consts = host_consts()
    in_maps = []
    for c in range(NCORES):
        m = {"x": np.ascontiguousarray(xall[c]) if c < 3 else np.zeros_like(xall[0])}
        m.update(consts)
        for k in WSHAPES:
            m[k] = np.ascontiguousarray(np.asarray(inputs[k], np.float32))
        in_maps.append(m)
    res = run_bass_kernel_spmd(nc, in_maps, core_ids=list(range(NCORES)))
    yall = np.stack([res.results[c]["y"] for c in range(3)], axis=0)
    return np.ascontiguousarray(yall[0:2]), np.ascontiguousarray(yall[2:3])


def kernel(**inputs):
    return run_model(inputs, 16384, 4)
```

```python
import os
import numpy as np
import ml_dtypes
import concourse.bass as bass
import concourse.mybir as mybir
from concourse.bass_utils import run_bass_kernel_spmd

F32 = mybir.dt.float32
BF16 = mybir.dt.bfloat16
AF = mybir.ActivationFunctionType
ALU = mybir.AluOpType
AX = mybir.AxisListType

NCORES = 8
D = 1024
DI = 2048
EPS = 1e-6


class _Op:
    __slots__ = ("eng", "fn", "deps", "semkey", "val", "isdma")


def _is_psum(b):
    n = b[0] if isinstance(b, tuple) else b
    return isinstance(n, str) and n.startswith("ps")


class Prog:
    ENGS = ("pe", "act", "dve", "pool", "sp")

    def __init__(self):
        self.ops = {e: [] for e in self.ENGS}
        self.ncomp = {e: 0 for e in self.ENGS}
        self.bufs = {}
        self.dma_last = {}
        self.dma_cnt = {}
        self.all_dma_keys = []
        self.bar = {e: [] for e in self.ENGS}
        self.eng_epochs = set()
        self.EPOCH = int(os.environ.get("EPOCH", "32000"))

    def barrier(self):
        deps = [ops[-1] for e, ops in self.ops.items() if ops]
        for e in self.ENGS:
            cl = [o for o in reversed(self.ops[e]) if not o.isdma]
            if cl:
                deps.append(cl[0])
        deps += list(self.dma_last.values())
        for e in self.ENGS:
            self.bar[e] = list(deps)

    def add(self, eng, fn, r=(), w=(), dma=None):
        op = _Op()
        op.eng = eng
        op.fn = fn
        op.deps = []
        if self.bar[eng]:
            op.deps.extend(self.bar[eng])
            self.bar[eng] = []
        op.isdma = dma is not None
        for b in r:
            st = self.bufs.get(b)
            if st is not None and st[0] is not None:
                op.deps.append(st[0])
            if st is not None and _is_psum(b):
                for o in st[1]:
                    if o.eng != eng:
                        op.deps.append(o)
        for b in w:
            st = self.bufs.get(b)
            if st is not None:
                if st[0] is not None:
                    op.deps.append(st[0])
                op.deps.extend(st[1])
        for b in r:
            st = self.bufs.setdefault(b, [None, []])
            st[1].append(op)
        for b in w:
            self.bufs[b] = [op, []]
        if dma is not None:
            if isinstance(dma, str):
                dma = ("misc", len(dma) % 2)
            prev = self.dma_last.get(dma)
            if prev is not None:
                op.deps.append(prev)
            else:
                self.all_dma_keys.append(dma)
            self.dma_last[dma] = op
            self.dma_cnt[dma] = self.dma_cnt.get(dma, 0) + 16
            op.semkey = ("dma", dma)
            op.val = self.dma_cnt[dma]
        else:
            n = self.ncomp[eng]
            self.ncomp[eng] += 1
            op.semkey = ("eng", eng, n // self.EPOCH)
            op.val = n % self.EPOCH + 1
            self.eng_epochs.add(op.semkey)
        self.ops[eng].append(op)
        return op

    def emit(self, nc, tail_wait_keys=()):
        import contextlib
        with contextlib.ExitStack() as es:
            sems = {}
            for k in sorted(self.eng_epochs):
                sems[k] = es.enter_context(nc.semaphore("s_%s_%d" % (k[1], k[2])))
            for i, k in enumerate(self.all_dma_keys):
                sems[("dma", k)] = es.enter_context(nc.semaphore("d%d" % i))
            block = es.enter_context(nc.Block())

            def run(engname, eng):
                waited = {}
                for op in self.ops[engname]:
                    for d in op.deps:
                        if d.eng == "pe" and engname == "pe" and not d.isdma and not op.isdma:
                            continue
                        if waited.get(d.semkey, 0) >= d.val:
                            continue
                        eng.wait_ge(sems[d.semkey], d.val)
                        waited[d.semkey] = d.val
                    ins = op.fn(eng)
                    ins.then_inc(sems[op.semkey], 16 if op.isdma else 1)
                for k in self.all_dma_keys:
                    last = self.dma_last[k]
                    if last.eng == engname and waited.get(last.semkey, 0) < last.val:
                        eng.wait_ge(sems[last.semkey], last.val)

            @block.tensor
            def _(e):
                run("pe", e)

            @block.scalar
            def _(e):
                run("act", e)

            @block.vector
            def _(e):
                run("dve", e)

            @block.gpsimd
            def _(e):
                run("pool", e)

            @block.sync
            def _(e):
                run("sp", e)


def bcast_rows(ap_row, nparts):
    return ap_row.partition_broadcast(nparts)


class Ctx:
    pass


def emit_norm_T(P, C, x_src_ap, tag, gtab, slot):
    xt = C.xt[slot]
    xb = C.xb
    hT = C.hT[slot]
    P.add("sp", lambda e: e.dma_start(out=xt[:], in_=x_src_ap), w=[("xt", slot)], dma=("xt", slot))
    P.add("act", lambda e: e.activation(out=C.junk[:, 0:D], in_=xt[:], func=AF.Square, accum_out=C.ss[:, 0:1]),
          r=[("xt", slot)], w=["junk", "ss"])
    P.add("act", lambda e: e.activation(out=C.ss[:, 1:2], in_=C.ss[:, 0:1], func=AF.Sqrt, scale=1.0 / D, bias=C.epsb[:, 0:1]),
          r=["ss", "epsb"], w=["ss1"])
    P.add("dve", lambda e: e.reciprocal(out=C.ss[:, 2:3], in_=C.ss[:, 1:2]), r=["ss1"], w=["ss2"])
    P.add("dve", lambda e: e.scalar_tensor_tensor(out=xb[:], in0=xt[:], scalar=C.ss[:, 2:3], in1=gtab[:],
                                                  op0=ALU.mult, op1=ALU.mult),
          r=[("xt", slot), "ss2", "gtab"], w=["xb"])
    for kc in range(8):
        P.add("pe", lambda e, kc=kc: e.transpose(out=C.psT[:, kc * 128:(kc + 1) * 128], in_=xb[:, kc * 128:(kc + 1) * 128],
                                                  identity=C.ident[:]),
              r=["xb", "ident"], w=["psT"])
    P.add("act", lambda e: e.copy(out=hT[:], in_=C.psT[:]), r=["psT"], w=[("hT", slot)])


def build_program(SL, depth):
    NT = SL // 128
    nc = bass.Bass("TRN2", target_bir_lowering=False)
    P = Prog()
    C = Ctx()
    C.NT = NT

    def din(name, shape, dt=F32):
        return nc.dram_tensor(name, list(shape), dt, kind="ExternalInput").ap()

    x_in = din("x", [SL, D])
    W = {}
    for k, shp in WSHAPES.items():
        W[k] = din(k, shp)
    ident_in = din("ident", [128, 128], BF16)
    C.cm_in = din("cm", [6, 128, 128])
    C.gmask_in = din("gmask", [2, 128, 512], BF16)
    C.alibi_in = din("alibi", [8, 4, 128, 512])
    y_out = nc.dram_tensor("y", [SL, D], F32, kind="ExternalOutput").ap()
    xs = [nc.dram_tensor("xs%d" % i, [SL, D], F32, kind="Internal").ap() for i in range(2)]
    C.dram = lambda name, shape, dt: nc.dram_tensor(name, list(shape), dt, kind="Internal").ap()

    import contextlib
    with contextlib.ExitStack() as es:
        def sb(name, shape, dt):
            return es.enter_context(nc.sbuf_tensor("sb_" + name, list(shape), dt))

        def ps(name, shape, dt):
            return es.enter_context(nc.psum_tensor("ps_" + name, list(shape), dt))

        C.ident = sb("ident", [128, 128], BF16)
        C.xt = [sb("xt%d" % i, [128, D], F32) for i in range(2)]
        C.xb = sb("xb", [128, D], BF16)
        C.hT = [sb("hT%d" % i, [128, D], BF16) for i in range(2)]
        C.junk = sb("junk", [128, D], BF16)
        C.ss = sb("ss", [128, 16], F32)
        C.epsb = sb("epsb", [128, 1], F32)
        C.oneb = sb("oneb", [128, 1], F32)
        C.gtab = sb("gtab", [128, D], F32)
        C.psA = [ps("psA%d" % i, [128, 512], F32) for i in range(3)]
        C.bankT = ps("bankT", [128, 512], F32)
        C.psT = C.bankT.bitcast(BF16)
        C.bankY = ps("bankY", [128, 1024], F32)
        C.psYT = C.bankY.bitcast(BF16)
        C.psO = ps("psO", [128, D], F32)
        C.nc = nc
        C.pa = [0]

        P.add("sp", lambda e: e.dma_start(out=C.ident[:], in_=ident_in), w=["ident"], dma="ident")
        P.add("pool", lambda e: e.memset(C.epsb[:], EPS), w=["epsb"])
        P.add("pool", lambda e: e.memset(C.oneb[:], 1.0), w=["oneb"])

        layer_kinds = [0, 1, 2, 0][:depth]
        cur = x_in
        for li, kind in enumerate(layer_kinds):
            dst = y_out if li == depth - 1 else xs[li % 2]
            with contextlib.ExitStack() as les:
                P.barrier()
                if kind == 0:
                    layer_A(nc, P, C, les, cur, dst, li, li // 3, NT, W["norm_g"], W["a_w_in"], W["a_v_g"], W["a_w_s"],
                            W["a_b_s"], W["a_w_out"])
                elif kind == 1:
                    layer_B(nc, P, C, les, cur, dst, li, NT, W)
                else:
                    layer_C(nc, P, C, les, cur, dst, li, NT, W)
            cur = dst
        P.emit(nc)
    return nc


WSHAPES = {
    "norm_g": [4, D], "a_w_in": [2, D, 3 * DI], "a_v_g": [2, DI], "a_w_s": [2, 8, 128, 128], "a_b_s": [2, 8, 128],
    "a_w_out": [2, DI, D], "b_w_in": [1, D, 5152], "b_w_gate": [1, 2, 16, 512], "b_gate_bias": [1, 2, 512],
    "b_o_g": [1, 512], "b_w_out": [1, DI, D], "c_w_in": [1, D, 4 * DI], "c_q_g": [1, 128], "c_k_g": [1, 128],
    "c_lam": [1, 4, 128], "c_o_g": [1, 256], "c_w_out": [1, DI, D],
}


def interleave(gens):
    gens = list(gens)
    while gens:
        nxt = []
        for g in gens:
            try:
                next(g)
                nxt.append(g)
            except StopIteration:
                pass
        gens = nxt


def layer_A(nc, P, C, es, x_src, x_dst, li, j, NT, norm_g, a_w_in, a_v_g, a_w_s, a_b_s, a_w_out):
    L = "A%d" % li

    def sb(name, shape, dt):
        return es.enter_context(nc.sbuf_tensor("sb_" + L + name, list(shape), dt))

    Win = sb("Win", [128, 8, 3 * DI], BF16)
    Wout = sb("Wout", [128, 16, D], BF16)
    t1 = [sb("t1%d" % i, [128, 512], F32) for i in range(2)]
    yb = sb("yb", [128, DI], BF16)
    wsq = sb("wsq", [128, 8, 128], F32)
    wsqb = yb[:, 0:1024].rearrange("p (g q) -> p g q", g=8)
    wsT = sb("wsT", [128, 8, 128], BF16)
    bs = sb("bs", [128, 8], F32)
    vgtab = sb("vgtab", [128, DI], F32)
    u = [sb("u%d" % i, [128, DI], BF16) for i in range(2)]
    sz = [sb("sz%d" % i, [128, DI], BF16) for i in range(2)]
    v = sb("v", [128, DI], F32)
    vs = sb("vs", [128, DI], BF16)
    yT = sb("yT", [128, DI], BF16)
    xn = sb("xn", [128, D], F32)
    st = sb("st", [128, 16], F32)

    wv = a_w_in[j].rearrange("(kc p) f -> p kc f", p=128)
    for kc in range(8):
        P.add("pool", lambda e, kc=kc: e.dma_start(out=Win[:, kc, :], in_=wv[:, kc, :]),
              w=[(L, "Win", kc)], dma=(L, "Win", kc % 2))
    wo = a_w_out[j].rearrange("(kc p) f -> p kc f", p=128)
    for kc in range(0, 16, 4):
        P.add("pool", lambda e, kc=kc: e.dma_start(out=Wout[:, kc:kc + 4, :], in_=wo[:, kc:kc + 4, :]),
              w=[(L, "Wout", kc)], dma=(L, "Wout", (kc // 4) % 2))
    P.add("sp", lambda e: e.dma_start(out=wsq[:], in_=a_w_s[j].rearrange("g q p -> q g p")), w=[L + "wsq"], dma=L + "wsq")
    P.add("sp", lambda e: e.dma_start(out=bs[:], in_=a_b_s[j].rearrange("g q -> q g"), allow_slow_non_contiguous=True),
          w=[L + "bs"], dma=L + "bs")
    P.add("sp", lambda e: e.dma_start(out=C.gtab[:], in_=norm_g[li:li + 1, :].partition_broadcast(128)),
          w=["gtab"], dma="gtab")
    P.add("sp", lambda e: e.dma_start(out=vgtab[:], in_=a_v_g[j:j + 1, :].partition_broadcast(128)),
          w=[L + "vgtab"], dma=L + "vgtab")
    P.add("dve", lambda e: e.tensor_copy(out=wsqb, in_=wsq[:]), r=[L + "wsq"], w=[(L, "yb", 0), (L, "yb", 1)])
    for g in range(8):
        P.add("pe", lambda e, g=g: e.transpose(out=C.psT[:, g * 128:(g + 1) * 128], in_=wsqb[:, g, :], identity=C.ident[:]),
              r=[(L, "yb", 0), (L, "yb", 1), "ident"], w=["psT"])
    P.add("act", lambda e: e.copy(out=wsT[:].rearrange("p g q -> p (g q)"), in_=C.psT[:]), r=["psT"], w=[L + "wsT"])

    Win_bufs = [(L, "Win", kc) for kc in range(8)]
    Wout_bufs = [(L, "Wout", kc) for kc in range(0, 16, 4)]

    def next_psA():
        i = C.pa[0] % 3
        C.pa[0] += 1
        return i

    def front(ti):
        slot = ti % 2
        rows = slice(ti * 128, (ti + 1) * 128)
        emit_norm_T(P, C, x_src[rows, :], L, C.gtab, slot)
        hT = C.hT[slot]
        u_, sz_ = u[slot], sz[slot]
        yield
        for cb in range(12):
            pi = next_psA()
            pst = C.psA[pi]
            for kc in range(8):
                P.add("pe", lambda e, kc=kc, cb=cb, pst=pst: e.matmul(
                    pst[:], lhsT=hT[:, kc * 128:(kc + 1) * 128], rhs=Win[:, kc, cb * 512:(cb + 1) * 512],
                    start=(kc == 0), stop=(kc == 7)),
                    r=[("hT", slot), Win_bufs[kc]], w=[("psA", pi)])
            c0 = (cb % 4) * 512
            if cb < 4:
                P.add("act", lambda e, pst=pst, c0=c0: e.copy(out=u_[:, c0:c0 + 512], in_=pst[:]),
                      r=[("psA", pi)], w=[(L, "u", slot, cb)])
            elif cb < 8:
                P.add("dve", lambda e, pst=pst, c0=c0: e.tensor_copy(out=v[:, c0:c0 + 512], in_=pst[:]),
                      r=[("psA", pi)], w=[(L, "v", cb - 4)])
                P.add("act", lambda e, pst=pst, cb=cb: e.activation(out=C.junk[:, 0:512], in_=pst[:], func=AF.Square,
                                                                      accum_out=st[:, cb - 4:cb - 3]),
                      r=[("psA", pi)], w=["junk", (L, "ssv", cb - 4)])
            else:
                P.add("act", lambda e, pst=pst, c0=c0: e.activation(out=sz_[:, c0:c0 + 512], in_=pst[:], func=AF.Silu),
                      r=[("psA", pi)], w=[(L, "sz", slot, cb - 8)])
            yield

    def back(ti):
        slot = ti % 2
        rows = slice(ti * 128, (ti + 1) * 128)
        xt = C.xt[slot]
        u_, sz_ = u[slot], sz[slot]
        P.add("dve", lambda e: e.tensor_reduce(out=st[:, 4:5], in_=st[:, 0:4], axis=AX.X, op=ALU.add),
              r=[(L, "ssv", i) for i in range(4)], w=[L + "st4"])
        P.add("act", lambda e: e.activation(out=st[:, 5:6], in_=st[:, 4:5], func=AF.Sqrt, scale=1.0 / DI, bias=C.epsb[:, 0:1]),
              r=[L + "st4", "epsb"], w=[L + "st5"])
        P.add("dve", lambda e: e.reciprocal(out=st[:, 6:7], in_=st[:, 5:6]), r=[L + "st5"], w=[L + "st6"])
        for b in range(4):
            P.add("dve", lambda e, b=b: e.tensor_scalar(out=vs[:, b * 512:(b + 1) * 512], in0=v[:, b * 512:(b + 1) * 512],
                                                         scalar1=st[:, 6:7], scalar2=None, op0=ALU.mult),
                  r=[(L, "v", b), L + "st6"], w=[(L, "vs", b)])
        yield
        for b in range(4):
            pi = next_psA()
            pst = C.psA[pi]
            for gg in range(2):
                g = 2 * b + gg
                P.add("pe", lambda e, g=g, gg=gg, pst=pst: e.matmul(
                    pst[:, gg * 256:(gg + 1) * 256], lhsT=wsT[:, g, :], rhs=vs[:, g * 256:(g + 1) * 256],
                    start=True, stop=True),
                    r=[L + "wsT", (L, "vs", b)], w=[("psA", pi)])
            tt = t1[b % 2]
            P.add("dve", lambda e, pst=pst, tt=tt, b=b: e.tensor_tensor(out=tt[:], in0=pst[:], in1=vgtab[:, b * 512:(b + 1) * 512],
                                                                        op=ALU.mult),
                  r=[("psA", pi), L + "vgtab"], w=[(L, "t1", b % 2)])
            for gg in range(2):
                g = 2 * b + gg
                P.add("dve", lambda e, tt=tt, g=g, gg=gg: e.scalar_tensor_tensor(
                    out=tt[:, gg * 256:(gg + 1) * 256], in0=tt[:, gg * 256:(gg + 1) * 256], scalar=bs[:, g:g + 1],
                    in1=u_[:, g * 256:(g + 1) * 256], op0=ALU.add, op1=ALU.mult),
                    r=[(L, "t1", b % 2), L + "bs", (L, "u", slot, b)], w=[(L, "t1", b % 2)])
            P.add("pool", lambda e, tt=tt, b=b: e.tensor_tensor(out=yb[:, b * 512:(b + 1) * 512], in0=tt[:],
                                                                in1=sz_[:, b * 512:(b + 1) * 512], op=ALU.mult),
                  r=[(L, "t1", b % 2), (L, "sz", slot, b)], w=[(L, "yb", b // 2)])
            yield
        for kc in range(16):
            P.add("pe", lambda e, kc=kc: e.transpose(out=C.psYT[:, kc * 128:(kc + 1) * 128], in_=yb[:, kc * 128:(kc + 1) * 128],
                                                      identity=C.ident[:]),
                  r=[(L, "yb", kc // 8), "ident"], w=["psYT"])
        P.add("act", lambda e: e.copy(out=yT[:], in_=C.psYT[:]), r=["psYT"], w=[L + "yT"])
        yield
        for nb in range(2):
            for kc in range(16):
                P.add("pe", lambda e, kc=kc, nb=nb: e.matmul(
                    C.psO[:, nb * 512:(nb + 1) * 512], lhsT=yT[:, kc * 128:(kc + 1) * 128],
                    rhs=Wout[:, kc, nb * 512:(nb + 1) * 512], start=(kc == 0), stop=(kc == 15)),
                    r=[L + "yT", Wout_bufs[kc // 4]], w=[("psO", nb)])
            yield
        P.add("dve", lambda e: e.tensor_tensor(out=xn[:], in0=C.psO[:], in1=xt[:], op=ALU.add),
              r=[("psO", 0), ("psO", 1), ("xt", slot)], w=[L + "xn"])
        P.add("sp", lambda e, rows=rows: e.dma_start(out=x_dst[rows, :], in_=xn[:]),
              r=[L + "xn"], w=[("xdram", li, ti)], dma=("xst", li, ti % 2))
        yield

    interleave([front(0)])
    for ti in range(NT):
        gens = [back(ti)]
        if ti + 1 < NT:
            gens.append(front(ti + 1))
        interleave(gens)


def layer_B(nc, P, C, es, x_src, x_dst, li, NT, W):
    L = "B%d" % li
    w_in = W["b_w_in"][0]

    def sb(name, shape, dt):
        return es.enter_context(nc.sbuf_tensor("sb_" + L + name, list(shape), dt))

    Wqk = sb("Wqk", [128, 8, 1024], BF16)
    Wvg = sb("Wvg", [128, 8, 4096], BF16)
    Waf = sb("Waf", [128, 8, 16], BF16)
    Wab = sb("Wab", [128, 8, 16], BF16)
    Wout = Wvg[:, 0:4, :].rearrange("p a (b f) -> p (a b) f", b=4)
    wg = [sb("wg%d" % d, [32, 512], BF16) for d in range(2)]
    aT = [sb("aT%d" % d, [32, 128], BF16) for d in range(2)]
    cm = sb("cm", [128, 6, 128], F32)
    gmask = sb("gmask", [128, 2, 512], BF16)
    ogtab = sb("ogtab", [128, 512], F32)
    qT2 = [sb("qT%d" % i, [128, 512], F32) for i in range(2)]
    kT2 = [sb("kT%d" % i, [128, 512], F32) for i in range(2)]
    ex = sb("ex", [128, 512], F32)
    sp2 = [[sb("sp%d_%d" % (i, d), [128, 512], F32) for d in range(2)] for i in range(2)]
    E = [sb("E%d" % i, [128, 512], F32) for i in range(2)]
    qq = [sb("qq%d" % d, [128, 512], BF16) for d in range(2)]
    kk = [sb("kk%d" % d, [128, 512], BF16) for d in range(2)]
    qi = [sb("qi%d" % d, [128, 512], BF16) for d in range(2)]
    ki = [sb("ki%d" % d, [128, 512], BF16) for d in range(2)]
    kiT = sb("kiT", [128, 1024], BF16)
    scT = [sb("scT%d" % d, [128, 512], BF16) for d in range(2)]
    vb2 = [sb("vb%d" % i, [128, DI], BF16) for i in range(2)]
    sg2 = [sb("sg%d" % i, [128, DI], BF16) for i in range(2)]
    opart = sb("opart", [128, DI], F32)
    dec = sb("dec", [128, 8], F32)
    St = sb("St", [128, DI], F32)
    Stb = sb("Stb", [128, DI], BF16)
    yb = sb("yb", [128, DI], BF16)
    yT = sb("yT", [128, DI], BF16)
    xn = sb("xn", [128, D], F32)
    st = sb("st", [128, 16], F32)
    qi2 = sb("qi2", [128, 512], BF16)
    kiT2 = sb("kiT2", [128, 512], BF16)
    dec2 = sb("dec2", [128, 4], F32)

    st_o = C.dram(L + "st_o", [NT * 128, DI], F32)
    st_v = C.dram(L + "st_v", [NT * 128, DI], BF16)
    st_sg = C.dram(L + "st_sg", [NT * 128, DI], BF16)
    st_qi = C.dram(L + "st_qi", [NT * 128, 512], BF16)
    st_ki = C.dram(L + "st_ki", [NT * 128, 512], BF16)
    st_df = C.dram(L + "st_df", [NT * 128, 4], F32)

    def next_psA():
        i = C.pa[0] % 3
        C.pa[0] += 1
        return i

    wv = w_in.rearrange("(kc p) f -> p kc f", p=128)
    for kc in range(8):
        P.add("pool", lambda e, kc=kc: e.dma_start(out=Wqk[:, kc, :], in_=wv[:, kc, 0:1024]), w=[(L, "Wqk", kc)], dma=(L, "W", 0))
        P.add("pool", lambda e, kc=kc: e.dma_start(out=Wvg[:, kc, :], in_=wv[:, kc, 1024:5120]), w=[(L, "Wvg", kc)], dma=(L, "W", 1))
        P.add("pool", lambda e, kc=kc: e.dma_start(out=Waf[:, kc, :], in_=wv[:, kc, 5120:5136]), w=[(L, "Waf")], dma=(L, "W", 2))
        P.add("pool", lambda e, kc=kc: e.dma_start(out=Wab[:, kc, :], in_=wv[:, kc, 5136:5152]), w=[(L, "Wab")], dma=(L, "W", 3))
    for d in range(2):
        P.add("pool", lambda e, d=d: e.dma_start(out=wg[d][0:16, :], in_=W["b_w_gate"][0, d]), w=[(L, "wg", d)], dma=(L, "W", 2))
        P.add("pool", lambda e, d=d: e.dma_start(out=wg[d][16:17, :], in_=W["b_gate_bias"][0, d:d + 1, :]), w=[(L, "wgb", d)], dma=(L, "W", 3))
        P.add("pool", lambda e, d=d: e.memset(aT[d][:], 1.0), w=[(L, "aT", d)])
    P.add("sp", lambda e: e.dma_start(out=cm[:], in_=C.cm_in.rearrange("m a b -> a m b")), w=[L + "cm"], dma=L + "cm")
    P.add("sp", lambda e: e.dma_start(out=gmask[:], in_=C.gmask_in.rearrange("m a b -> a m b")), w=[L + "gmask"], dma=L + "gmask")
    P.add("sp", lambda e: e.dma_start(out=C.gtab[:], in_=W["norm_g"][li:li + 1, :].partition_broadcast(128)), w=["gtab"], dma="gtab")
    P.add("sp", lambda e: e.dma_start(out=ogtab[:], in_=W["b_o_g"][0:1, :].partition_broadcast(128)), w=[L + "ogtab"], dma=L + "ogtab")
    P.add("dve", lambda e: e.memset(St[:], 0.0), w=[L + "St"])
    P.add("pool", lambda e: e.memset(Stb[:], 0.0), w=[(L, "Stb", h) for h in range(4)])
    Wout_bufs = [(L, "Wout", kc) for kc in range(0, 16, 4)]

    def front1(ti, par):
        slot = par
        rows = slice(ti * 128, (ti + 1) * 128)
        emit_norm_T(P, C, x_src[rows, :], L, C.gtab, slot)
        hT = C.hT[slot]
        hTb = ("hT", slot)
        qT, kT, sp, vb, sg = qT2[par], kT2[par], sp2[par], vb2[par], sg2[par]
        yield
        for qk in range(2):
            pi = next_psA()
            pst = C.psA[pi]
            for h in range(4):
                blk = qk * 4 + h
                for kc in range(8):
                    P.add("pe", lambda e, kc=kc, blk=blk, h=h, pst=pst: e.matmul(
                        pst[:, h * 128:(h + 1) * 128], lhsT=Wqk[:, kc, blk * 128:(blk + 1) * 128], rhs=hT[:, kc * 128:(kc + 1) * 128],
                        start=(kc == 0), stop=(kc == 7)), r=[hTb, (L, "Wqk", kc)], w=[("psA", pi)])
            if qk == 0:
                P.add("act", lambda e, pst=pst: e.activation(out=qT[:], in_=pst[:], func=AF.Copy, scale=128.0 ** -0.5),
                      r=[("psA", pi)], w=[(L, "qT", par)])
            else:
                P.add("act", lambda e, pst=pst: e.copy(out=kT[:], in_=pst[:]), r=[("psA", pi)], w=[(L, "kT", par)])
            yield
        for d, Wa in enumerate((Waf, Wab)):
            pi = next_psA()
            pst = C.psA[pi]
            for kc in range(8):
                P.add("pe", lambda e, kc=kc, Wa=Wa, pst=pst: e.matmul(
                    pst[0:16, 0:128], lhsT=Wa[:, kc, :], rhs=hT[:, kc * 128:(kc + 1) * 128], start=(kc == 0), stop=(kc == 7)),
                    r=[hTb, (L, "Waf"), (L, "Wab")], w=[("psA", pi)])
            P.add("dve", lambda e, d=d, pst=pst: e.tensor_copy(out=aT[d][0:16, :], in_=pst[0:16, 0:128]),
                  r=[("psA", pi)], w=[(L, "aT", d)])
        for d in range(2):
            pi = next_psA()
            pst = C.psA[pi]
            P.add("pe", lambda e, d=d, pst=pst: e.matmul(pst[:], lhsT=aT[d][0:17, :], rhs=wg[d][0:17, :], start=True, stop=True),
                  r=[(L, "aT", d), (L, "wg", d), (L, "wgb", d)], w=[("psA", pi)])
            P.add("act", lambda e, pst=pst: e.activation(out=ex[:], in_=pst[:], func=AF.Exp, scale=-1.0),
                  r=[("psA", pi)], w=[L + "ex"])
            P.add("act", lambda e, d=d: e.activation(out=sp[d][:], in_=ex[:], func=AF.Ln, bias=C.oneb[:, 0:1]),
                  r=[L + "ex", "oneb"], w=[(L, "sp", par, d)])
            yield
        for cb in range(8):
            pi = next_psA()
            pst = C.psA[pi]
            for kc in range(8):
                P.add("pe", lambda e, kc=kc, cb=cb, pst=pst: e.matmul(
                    pst[:], lhsT=hT[:, kc * 128:(kc + 1) * 128], rhs=Wvg[:, kc, cb * 512:(cb + 1) * 512],
                    start=(kc == 0), stop=(kc == 7)), r=[hTb, (L, "Wvg", kc)], w=[("psA", pi)])
            c0 = (cb % 4) * 512
            if cb < 4:
                P.add("dve", lambda e, pst=pst, c0=c0: e.tensor_copy(out=vb[:, c0:c0 + 512], in_=pst[:]),
                      r=[("psA", pi)], w=[(L, "vb", par, cb)])
            else:
                P.add("act", lambda e, pst=pst, c0=c0: e.activation(out=sg[:, c0:c0 + 512], in_=pst[:], func=AF.Silu),
                      r=[("psA", pi)], w=[(L, "sg", par, cb - 4)])
            yield

    def back1(ti, par):
        rows = slice(ti * 128, (ti + 1) * 128)
        qT, kT, sp, vb, sg = qT2[par], kT2[par], sp2[par], vb2[par], sg2[par]
        for d in range(2):
            for m in range(3):
                pi = next_psA()
                pst = C.psA[pi]
                for h in range(4):
                    P.add("pe", lambda e, d=d, m=m, h=h, pst=pst: e.matmul(
                        pst[:, h * 128:(h + 1) * 128], lhsT=sp[d][:, h * 128:(h + 1) * 128], rhs=cm[:, d * 3 + m, :],
                        start=True, stop=True), r=[(L, "sp", par, d), L + "cm"], w=[("psA", pi)])
                if m == 0:
                    P.add("act", lambda e, pst=pst: e.activation(out=E[0][:], in_=pst[:], func=AF.Exp), r=[("psA", pi)], w=[(L, "E", 0)])
                    P.add("dve", lambda e, d=d: e.tensor_tensor(out=qq[d][:], in0=qT[:], in1=E[0][:], op=ALU.mult),
                          r=[(L, "qT", par), (L, "E", 0)], w=[(L, "qq", d)])
                    P.add("act", lambda e, pst=pst: e.activation(out=E[1][:], in_=pst[:], func=AF.Exp, scale=-1.0),
                          r=[("psA", pi)], w=[(L, "E", 1)])
                    P.add("dve", lambda e, d=d: e.tensor_tensor(out=kk[d][:], in0=kT[:], in1=E[1][:], op=ALU.mult),
                          r=[(L, "kT", par), (L, "E", 1)], w=[(L, "kk", d)])
                elif m == 1:
                    P.add("act", lambda e, pst=pst: e.activation(out=E[0][:], in_=pst[:], func=AF.Exp), r=[("psA", pi)], w=[(L, "E", 0)])
                    P.add("dve", lambda e, d=d: e.tensor_tensor(out=qi[d][:], in0=qT[:], in1=E[0][:], op=ALU.mult),
                          r=[(L, "qT", par), (L, "E", 0)], w=[(L, "qi", d)])
                    col = 127 if d == 0 else 0
                    P.add("dve", lambda e, d=d, col=col: e.tensor_copy(
                        out=dec[:, d * 4:(d + 1) * 4], in_=E[0][:].rearrange("p (h t) -> p h t", h=4)[:, :, col]),
                        r=[(L, "E", 0)], w=[(L, "dec", d)])
                else:
                    P.add("act", lambda e, pst=pst: e.activation(out=E[1][:], in_=pst[:], func=AF.Exp), r=[("psA", pi)], w=[(L, "E", 1)])
                    P.add("dve", lambda e, d=d: e.tensor_tensor(out=ki[d][:], in0=kT[:], in1=E[1][:], op=ALU.mult),
                          r=[(L, "kT", par), (L, "E", 1)], w=[(L, "ki", d)])
                yield
        for d in range(2):
            for h in range(4):
                blk = d * 4 + h
                P.add("pe", lambda e, d=d, h=h, blk=blk: e.transpose(out=C.psT[:, blk * 128:(blk + 1) * 128],
                                                                      in_=ki[d][:, h * 128:(h + 1) * 128], identity=C.ident[:]),
                      r=[(L, "ki", d), "ident"], w=["psT"])
        P.add("act", lambda e: e.copy(out=kiT[:], in_=C.psT[:]), r=["psT"], w=[L + "kiT"])
        yield
        for d in range(2):
            pi = next_psA()
            pst = C.psA[pi]
            for h in range(4):
                P.add("pe", lambda e, d=d, h=h, pst=pst: e.matmul(
                    pst[:, h * 128:(h + 1) * 128], lhsT=kk[d][:, h * 128:(h + 1) * 128], rhs=qq[d][:, h * 128:(h + 1) * 128],
                    start=True, stop=True), r=[(L, "kk", d), (L, "qq", d)], w=[("psA", pi)])
            P.add("dve", lambda e, d=d, pst=pst: e.tensor_tensor(out=scT[d][:], in0=pst[:], in1=gmask[:, d, :], op=ALU.mult),
                  r=[("psA", pi), L + "gmask"], w=[(L, "scT", d)])
            yield
        for h in range(4):
            pi = next_psA()
            pst = C.psA[pi]
            hs = slice(h * 128, (h + 1) * 128)
            vs_ = slice(h * 512, (h + 1) * 512)
            P.add("pe", lambda e, pst=pst, hs=hs, vs_=vs_: e.matmul(pst[:], lhsT=scT[0][:, hs], rhs=vb[:, vs_], start=True, stop=False),
                  r=[(L, "scT", 0), (L, "vb", par, h)], w=[("psA", pi)])
            P.add("pe", lambda e, pst=pst, hs=hs, vs_=vs_: e.matmul(pst[:], lhsT=scT[1][:, hs], rhs=vb[:, vs_], start=False, stop=False),
                  r=[(L, "scT", 1), (L, "vb", par, h)], w=[("psA", pi)])
            P.add("pe", lambda e, pst=pst, hs=hs, vs_=vs_: e.matmul(pst[:], lhsT=qi[1][:, hs], rhs=Stb[:, vs_], start=False, stop=True),
                  r=[(L, "qi", 1), (L, "Stb", h)], w=[("psA", pi)])
            P.add("act", lambda e, pst=pst, vs_=vs_: e.copy(out=opart[:, vs_], in_=pst[:]), r=[("psA", pi)], w=[(L, "opart", h)])
            yield
        for h in range(4):
            pi = next_psA()
            pst = C.psA[pi]
            hs2 = slice((4 + h) * 128, (5 + h) * 128)
            vs_ = slice(h * 512, (h + 1) * 512)
            P.add("pe", lambda e, pst=pst, hs2=hs2, vs_=vs_: e.matmul(pst[:], lhsT=kiT[:, hs2], rhs=vb[:, vs_], start=True, stop=True),
                  r=[L + "kiT", (L, "vb", par, h)], w=[("psA", pi)])
            P.add("dve", lambda e, pst=pst, vs_=vs_, h=h: e.scalar_tensor_tensor(
                out=St[:, vs_], in0=St[:, vs_], scalar=dec[:, 4 + h:5 + h], in1=pst[:], op0=ALU.mult, op1=ALU.add),
                r=[("psA", pi), (L, "dec", 1), L + "St"], w=[L + "St"])
            P.add("act", lambda e, vs_=vs_: e.copy(out=Stb[:, vs_], in_=St[:, vs_]), r=[L + "St"], w=[(L, "Stb", h)])
            yield
        k2 = ti % 2
        P.add("sp", lambda e: e.dma_start(out=st_o[rows, :], in_=opart[:]), r=[(L, "opart", h) for h in range(4)],
              w=[(L, "d_o", ti)], dma=(L, "s0", k2))
        P.add("sp", lambda e: e.dma_start(out=st_v[rows, :], in_=vb[:]), r=[(L, "vb", par, h) for h in range(4)],
              w=[(L, "d_v", ti)], dma=(L, "s1", k2))
        P.add("sp", lambda e: e.dma_start(out=st_sg[rows, :], in_=sg[:]), r=[(L, "sg", par, h) for h in range(4)],
              w=[(L, "d_sg", ti)], dma=(L, "s2", k2))
        P.add("sp", lambda e: e.dma_start(out=st_qi[rows, :], in_=qi[0][:]), r=[(L, "qi", 0)], w=[(L, "d_qi", ti)], dma=(L, "s3", k2))
        P.add("sp", lambda e: e.dma_start(out=st_ki[rows, :], in_=kiT[:, 0:512]), r=[L + "kiT"], w=[(L, "d_ki", ti)], dma=(L, "s4", k2))
        P.add("sp", lambda e: e.dma_start(out=st_df[rows, :], in_=dec[:, 0:4]), r=[(L, "dec", 0)], w=[(L, "d_df", ti)], dma=(L, "s5", k2))

        yield

    order = list(reversed(range(NT)))
    interleave([front1(order[0], 0)])
    for n, ti in enumerate(order):
        gens = [back1(ti, n % 2)]
        if n + 1 < NT:
            gens.append(front1(order[n + 1], (n + 1) % 2))
        interleave(gens)

    P.add("dve", lambda e: e.memset(St[:], 0.0), r=[L + "St"], w=[L + "St"])
    P.add("pool", lambda e: e.memset(Stb[:], 0.0), w=[(L, "Stb", h) for h in range(4)])
    wo = W["b_w_out"][0].rearrange("(kc p) f -> p kc f", p=128)
    for kc in range(0, 16, 4):
        P.add("pool", lambda e, kc=kc: e.dma_start(out=Wout[:, kc:kc + 4, :], in_=wo[:, kc:kc + 4, :]),
              w=[(L, "Wout", kc), (L, "Wvg", kc // 4)], dma=(L, "W", 0))

    yb2 = [yb, sb("yb2", [128, DI], BF16)]
    qi22 = [qi2, sb("qi2b", [128, 512], BF16)]
    kiT22 = [kiT2, sb("kiT2b", [128, 512], BF16)]
    dec22 = [dec2, sb("dec2b", [128, 4], F32)]
    st22 = [st, sb("stb", [128, 16], F32)]

    def front2(ti):
        slot = ti % 2
        rows = slice(ti * 128, (ti + 1) * 128)
        xt = C.xt[slot]
        vb, sg, yb_, qi2_, kiT2_, dec2_, st_ = vb2[slot], sg2[slot], yb2[slot], qi22[slot], kiT22[slot], dec22[slot], st22[slot]
        P.add("sp", lambda e: e.dma_start(out=xt[:], in_=x_src[rows, :]), w=[("xt", slot)], dma=("xt", slot))
        P.add("sp", lambda e: e.dma_start(out=opart[:], in_=st_o[rows, :]), r=[(L, "d_o", ti)], w=[(L, "opart", h) for h in range(4)], dma=(L, "l0"))
        P.add("sp", lambda e: e.dma_start(out=vb[:], in_=st_v[rows, :]), r=[(L, "d_v", ti)], w=[(L, "vb", slot, h) for h in range(4)], dma=(L, "l1", slot))
        P.add("sp", lambda e: e.dma_start(out=sg[:], in_=st_sg[rows, :]), r=[(L, "d_sg", ti)], w=[(L, "sg", slot, h) for h in range(4)], dma=(L, "l2", slot))
        P.add("sp", lambda e: e.dma_start(out=qi2_[:], in_=st_qi[rows, :]), r=[(L, "d_qi", ti)], w=[(L, "qi2", slot)], dma=(L, "l3", slot))
        P.add("sp", lambda e: e.dma_start(out=kiT2_[:], in_=st_ki[rows, :]), r=[(L, "d_ki", ti)], w=[(L, "kiT2", slot)], dma=(L, "l4", slot))
        P.add("sp", lambda e: e.dma_start(out=dec2_[:], in_=st_df[rows, :]), r=[(L, "d_df", ti)], w=[(L, "dec2", slot)], dma=(L, "l5", slot))
        yield
        for h in range(4):
            pi = next_psA()
            pst = C.psA[pi]
            hs = slice(h * 128, (h + 1) * 128)
            vs_ = slice(h * 512, (h + 1) * 512)
            P.add("pe", lambda e, pst=pst, hs=hs, vs_=vs_: e.matmul(pst[:], lhsT=qi2_[:, hs], rhs=Stb[:, vs_], start=True, stop=True),
                  r=[(L, "qi2", slot), (L, "Stb", h)], w=[("psA", pi)])
            P.add("dve", lambda e, pst=pst, vs_=vs_: e.tensor_tensor(out=opart[:, vs_], in0=pst[:], in1=opart[:, vs_], op=ALU.add),
                  r=[("psA", pi), (L, "opart", h)], w=[(L, "opart", h)])
            P.add("act", lambda e, vs_=vs_, h=h: e.activation(out=C.junk[:, 0:512], in_=opart[:, vs_], func=AF.Square,
                                                               accum_out=st_[:, h:h + 1]),
                  r=[(L, "opart", h)], w=["junk", (L, "sso", slot, h)])
            yield
        for h in range(4):
            pi = next_psA()
            pst = C.psA[pi]
            hs = slice(h * 128, (h + 1) * 128)
            vs_ = slice(h * 512, (h + 1) * 512)
            P.add("pe", lambda e, pst=pst, hs=hs, vs_=vs_: e.matmul(pst[:], lhsT=kiT2_[:, hs], rhs=vb[:, vs_], start=True, stop=True),
                  r=[(L, "kiT2", slot), (L, "vb", slot, h)], w=[("psA", pi)])
            P.add("dve", lambda e, pst=pst, vs_=vs_, h=h: e.scalar_tensor_tensor(
                out=St[:, vs_], in0=St[:, vs_], scalar=dec2_[:, h:h + 1], in1=pst[:], op0=ALU.mult, op1=ALU.add),
                r=[("psA", pi), (L, "dec2", slot), L + "St"], w=[L + "St"])
            P.add("act", lambda e, vs_=vs_: e.copy(out=Stb[:, vs_], in_=St[:, vs_]), r=[L + "St"], w=[(L, "Stb", h)])
            yield
        P.add("act", lambda e: e.activation(out=st_[:, 4:8], in_=st_[:, 0:4], func=AF.Sqrt, scale=1.0 / 512, bias=C.epsb[:, 0:1]),
              r=[(L, "sso", slot, h) for h in range(4)] + ["epsb"], w=[(L, "st4", slot)])
        P.add("dve", lambda e: e.reciprocal(out=st_[:, 8:12], in_=st_[:, 4:8]), r=[(L, "st4", slot)], w=[(L, "st8", slot)])
        for h in range(4):
            vs_ = slice(h * 512, (h + 1) * 512)
            P.add("dve", lambda e, vs_=vs_, h=h: e.scalar_tensor_tensor(
                out=opart[:, vs_], in0=opart[:, vs_], scalar=st_[:, 8 + h:9 + h], in1=ogtab[:], op0=ALU.mult, op1=ALU.mult),
                r=[(L, "opart", h), (L, "st8", slot), L + "ogtab"], w=[(L, "opart", h)])
            P.add("pool", lambda e, vs_=vs_: e.tensor_tensor(out=yb_[:, vs_], in0=opart[:, vs_], in1=sg[:, vs_], op=ALU.mult),
                  r=[(L, "opart", h), (L, "sg", slot, h)], w=[(L, "yb", slot, h)])
            yield

    def back2(ti):
        slot = ti % 2
        rows = slice(ti * 128, (ti + 1) * 128)
        yield from emit_out_proj(P, C, L, li, ti, yb2[slot], [(L, "yb", slot, h) for h in range(4)], yT, Wout, Wout_bufs, C.xt[slot],
                                 ("xt", slot), xn, x_dst, rows, gen=True)

    interleave([front2(0)])
    for ti in range(NT):
        gens = [back2(ti)]
        if ti + 1 < NT:
            gens.append(front2(ti + 1))
        interleave(gens)


BAND = 140.0


def layer_C(nc, P, C, es0, x_src, x_dst, li, NT, W):
    import contextlib
    import math
    L = "C%d" % li
    S = NT * 128
    lambda_init = 0.8 - 0.6 * math.exp(-0.3 * li)
    w_in = W["c_w_in"][0]
    wv = w_in.rearrange("(kc p) f -> p kc f", p=128)

    st_qT = C.dram(L + "st_qT", [16, 128, S], BF16)
    st_kT = C.dram(L + "st_kT", [16, 128, S], BF16)
    st_v = C.dram(L + "st_v", [S, DI], BF16)
    st_sz = C.dram(L + "st_sz", [S, DI], BF16)
    st_o = C.dram(L + "st_o", [S, DI], F32)

    def next_psA():
        i = C.pa[0] % 3
        C.pa[0] += 1
        return i

    P.add("sp", lambda e: e.dma_start(out=C.gtab[:], in_=W["norm_g"][li:li + 1, :].partition_broadcast(128)), w=["gtab"], dma="gtab")

    with contextlib.ExitStack() as es:
        def sb(name, shape, dt):
            return es.enter_context(nc.sbuf_tensor("sb_" + L + "a" + name, list(shape), dt))
        Wqk = sb("Wqk", [128, 8, 4096], BF16)
        gt = [sb("gt%d" % i, [128, 128], F32) for i in range(2)]
        qf2 = [sb("qf%d" % i, [128, DI], F32) for i in range(2)]
        sq = sb("sq", [128, 512], F32)
        ssq2 = [sb("ssq%d" % i, [128, 48], F32) for i in range(2)]
        qn2 = [sb("qn%d" % i, [128, DI], BF16) for i in range(2)]
        qTt2 = [sb("qTt%d" % i, [128, DI], BF16) for i in range(2)]
        for kc in range(8):
            P.add("pool", lambda e, kc=kc: e.dma_start(out=Wqk[:, kc, :], in_=wv[:, kc, 0:4096]), w=[(L, "Wqk", kc)], dma=(L, "W", kc % 2))
        P.add("sp", lambda e: e.dma_start(out=gt[0][:], in_=W["c_q_g"][0:1, :].partition_broadcast(128)), w=[(L, "gt", 0)], dma=L + "gt0")
        P.add("sp", lambda e: e.dma_start(out=gt[1][:], in_=W["c_k_g"][0:1, :].partition_broadcast(128)), w=[(L, "gt", 1)], dma=L + "gt1")

        def f1a(ti, qk, par):
            slot = ti % 2
            rows = slice(ti * 128, (ti + 1) * 128)
            if qk == 0:
                emit_norm_T(P, C, x_src[rows, :], L, C.gtab, slot)
                yield
            hT = C.hT[slot]
            qf, ssq = qf2[par], ssq2[par]
            for cb in range(4):
                pi = next_psA()
                pst = C.psA[pi]
                col = qk * 2048 + cb * 512
                for kc in range(8):
                    P.add("pe", lambda e, kc=kc, col=col, pst=pst: e.matmul(
                        pst[:], lhsT=hT[:, kc * 128:(kc + 1) * 128], rhs=Wqk[:, kc, col:col + 512],
                        start=(kc == 0), stop=(kc == 7)), r=[("hT", slot), (L, "Wqk", kc)], w=[("psA", pi)])
                P.add("act", lambda e, pst=pst, cb=cb: e.copy(out=qf[:, cb * 512:(cb + 1) * 512], in_=pst[:]),
                      r=[("psA", pi)], w=[(L, "qf", par, cb)])
                P.add("pool", lambda e, cb=cb: e.tensor_tensor(out=sq[:], in0=qf[:, cb * 512:(cb + 1) * 512],
                                                               in1=qf[:, cb * 512:(cb + 1) * 512], op=ALU.mult),
                      r=[(L, "qf", par, cb)], w=[L + "sq"])
                P.add("dve", lambda e, cb=cb: e.tensor_reduce(out=ssq[:, cb * 4:(cb + 1) * 4],
                                                               in_=sq[:].rearrange("p (g d) -> p g d", g=4), axis=AX.X, op=ALU.add),
                      r=[L + "sq"], w=[(L, "ssq", par, cb)])
                yield

        def b1a(ti, qk, par):
            qf, ssq, qn, qTt = qf2[par], ssq2[par], qn2[par], qTt2[par]
            P.add("act", lambda e: e.activation(out=ssq[:, 16:32], in_=ssq[:, 0:16], func=AF.Sqrt, scale=1.0 / 128, bias=C.epsb[:, 0:1]),
                  r=[(L, "ssq", par, cb) for cb in range(4)] + ["epsb"], w=[(L, "ssq16", par)])
            P.add("dve", lambda e: e.reciprocal(out=ssq[:, 32:48], in_=ssq[:, 16:32]), r=[(L, "ssq16", par)], w=[(L, "ssq32", par)])
            if qk == 0:
                P.add("dve", lambda e: e.tensor_scalar(out=ssq[:, 32:48], in0=ssq[:, 32:48], scalar1=128.0 ** -0.5, scalar2=None, op0=ALU.mult),
                      r=[(L, "ssq32", par)], w=[(L, "ssq32", par)])
            yield
            for g in range(16):
                P.add("dve", lambda e, g=g: e.scalar_tensor_tensor(
                    out=qn[:, g * 128:(g + 1) * 128], in0=qf[:, g * 128:(g + 1) * 128], scalar=ssq[:, 32 + g:33 + g],
                    in1=gt[qk][:], op0=ALU.mult, op1=ALU.mult),
                    r=[(L, "qf", par, g // 4), (L, "ssq32", par), (L, "gt", qk)], w=[(L, "qn", par, g // 4)])
                if g % 4 == 3:
                    yield
            for half in range(2):
                for g8 in range(8):
                    g = half * 8 + g8
                    P.add("pe", lambda e, g=g, g8=g8: e.transpose(out=C.psT[:, g8 * 128:(g8 + 1) * 128], in_=qn[:, g * 128:(g + 1) * 128],
                                                                  identity=C.ident[:]),
                          r=[(L, "qn", par, g // 4), "ident"], w=["psT"])
                P.add("act", lambda e, half=half: e.copy(out=qTt[:, half * 1024:(half + 1) * 1024], in_=C.psT[:]),
                      r=["psT"], w=[(L, "qTt", par, half)])
                yield
            dst = st_qT if qk == 0 else st_kT
            P.add("sp", lambda e, dst=dst: e.dma_start(out=dst.rearrange("g d s -> d g s")[:, :, ti * 128:(ti + 1) * 128],
                                                       in_=qTt[:].rearrange("p (g t) -> p g t", g=16)),
                  r=[(L, "qTt", par, 0), (L, "qTt", par, 1)], w=[(L, "d_qk", qk, ti)], dma=(L, "sq", qk))
            yield

        jobs = [(ti, qk) for ti in range(NT) for qk in range(2)]
        interleave([f1a(jobs[0][0], jobs[0][1], 0)])
        for n, (ti, qk) in enumerate(jobs):
            gens = [b1a(ti, qk, n % 2)]
            if n + 1 < len(jobs):
                gens.append(f1a(jobs[n + 1][0], jobs[n + 1][1], (n + 1) % 2))
            interleave(gens)
    P.barrier()

    with contextlib.ExitStack() as es:
        def sb(name, shape, dt):
            return es.enter_context(nc.sbuf_tensor("sb_" + L + "b" + name, list(shape), dt))
        Wvz = sb("Wvz", [128, 8, 4096], BF16)
        vbb = [sb("vb%d" % i, [128, DI], BF16) for i in range(2)]
        szbb = [sb("szb%d" % i, [128, DI], BF16) for i in range(2)]
        for kc in range(8):
            P.add("pool", lambda e, kc=kc: e.dma_start(out=Wvz[:, kc, :], in_=wv[:, kc, 4096:8192]), w=[(L, "Wvz", kc)], dma=(L, "W", kc % 2))

        def p1b(ti):
            slot = ti % 2
            vb, szb = vbb[slot], szbb[slot]
            rows = slice(ti * 128, (ti + 1) * 128)
            emit_norm_T(P, C, x_src[rows, :], L, C.gtab, slot)
            hT = C.hT[slot]
            for cb in range(8):
                pi = next_psA()
                pst = C.psA[pi]
                for kc in range(8):
                    P.add("pe", lambda e, kc=kc, cb=cb, pst=pst: e.matmul(
                        pst[:], lhsT=hT[:, kc * 128:(kc + 1) * 128], rhs=Wvz[:, kc, cb * 512:(cb + 1) * 512],
                        start=(kc == 0), stop=(kc == 7)), r=[("hT", slot), (L, "Wvz", kc)], w=[("psA", pi)])
                c0 = (cb % 4) * 512
                if cb < 4:
                    P.add("dve", lambda e, pst=pst, c0=c0: e.tensor_copy(out=vb[:, c0:c0 + 512], in_=pst[:]),
                          r=[("psA", pi)], w=[(L, "vb", slot, cb)])
                else:
                    P.add("act", lambda e, pst=pst, c0=c0: e.activation(out=szb[:, c0:c0 + 512], in_=pst[:], func=AF.Silu),
                          r=[("psA", pi)], w=[(L, "szb", slot, cb - 4)])
            P.add("sp", lambda e: e.dma_start(out=st_v[rows, :], in_=vb[:]), r=[(L, "vb", slot, i) for i in range(4)], w=[(L, "d_v", ti)], dma=(L, "sv", ti % 2))
            P.add("sp", lambda e: e.dma_start(out=st_sz[rows, :], in_=szb[:]), r=[(L, "szb", slot, i) for i in range(4)], w=[(L, "d_sz", ti)], dma=(L, "ssz", ti % 2))
        for ti in range(NT):
            p1b(ti)
    P.barrier()

    with contextlib.ExitStack() as es:
        def sb(name, shape, dt):
            return es.enter_context(nc.sbuf_tensor("sb_" + L + "c" + name, list(shape), dt))
        kS = sb("kS", [128, 2, S], BF16)
        vS = sb("vS", [128, NT, 257], BF16)
        tab = sb("tab", [128, 4, 512], F32)
        qS = [sb("qS%d" % i, [128, 2, 256], BF16) for i in range(2)]
        tmp = [sb("tmp%d" % i, [128, 512], F32) for i in range(4)]
        pT = [sb("pT%d" % i, [128, 512], BF16) for i in range(4)]
        sbank = [C.psA[0][:, :], C.psA[1][:, :], C.psO[:, 0:512], C.psO[:, 512:1024]]
        sbankb = [("psA", 0), ("psA", 1), ("psO", 0), ("psO", 1)]
        lam = sb("lam", [128, 4, 128], F32)
        lw = sb("lw", [128, 2, 128], F32)
        lv = sb("lv", [128, 8], F32)
        rr = sb("rr", [128, 8], F32)
        ot = [sb("ot%d" % i, [128, 256], F32) for i in range(2)]
        oo = [sb("oo%d" % i, [128, 256], F32) for i in range(2)]
        acc = [C.psA[2][:, 0:257], C.bankT[:, 0:257], C.bankY[:, 0:257], C.bankY[:, 512:769]]
        accb = [("psA", 2), "psT", ("psYT", 0), ("psYT", 1)]

        P.add("sp", lambda e: e.dma_start(out=lam[:].rearrange("p a b -> p (a b)"),
                                          in_=W["c_lam"][0:1].rearrange("o a b -> o (a b)").partition_broadcast(128)),
              w=[L + "lam"], dma=L + "lam")
        P.add("dve", lambda e: e.tensor_tensor(out=lw[:, 0, :], in0=lam[:, 0, :], in1=lam[:, 1, :], op=ALU.mult), r=[L + "lam"], w=[L + "lw0"])
        P.add("dve", lambda e: e.tensor_tensor(out=lw[:, 1, :], in0=lam[:, 2, :], in1=lam[:, 3, :], op=ALU.mult), r=[L + "lam"], w=[L + "lw1"])
        P.add("dve", lambda e: e.tensor_reduce(out=lv[:, 0:2], in_=lw[:], axis=AX.X, op=ALU.add), r=[L + "lw0", L + "lw1"], w=[L + "lv0"])
        P.add("act", lambda e: e.activation(out=lv[:, 2:4], in_=lv[:, 0:2], func=AF.Exp), r=[L + "lv0"], w=[L + "lv2"])
        P.add("dve", lambda e: e.tensor_tensor(out=lv[:, 4:5], in0=lv[:, 2:3], in1=lv[:, 3:4], op=ALU.subtract), r=[L + "lv2"], w=[L + "lv4"])
        P.add("dve", lambda e: e.tensor_scalar(out=lv[:, 5:6], in0=lv[:, 4:5], scalar1=-1.0, scalar2=-lambda_init, op0=ALU.mult, op1=ALU.add),
              r=[L + "lv4"], w=[L + "neglam"])
        P.add("pool", lambda e: e.memset(vS[:], 1.0), w=[L + "vS"])

        QT = 256
        NQ = S // QT
        cnt = [0]
        for h in range(8):
            slope = 2.0 ** (-(h + 1))
            dmax = BAND / slope
            for m in range(2):
                P.add("sp", lambda e, h=h, m=m: e.dma_start(out=kS[:, m, :], in_=st_kT[2 * h + m]), w=[(L, "kS", m)], dma=(L, "kS", m))
            NPART = max(4, NT // 8)
            for part in range(NPART):
                n0 = part * NT // NPART
                n1 = (part + 1) * NT // NPART
                if n1 > n0:
                    P.add("pool", lambda e, h=h, n0=n0, n1=n1: e.dma_start(
                        out=vS[:, n0:n1, 0:256], in_=st_v[n0 * 128:n1 * 128, h * 256:(h + 1) * 256].rearrange("(n p) c -> p n c", p=128)),
                        r=[], w=[L + "vS"], dma=(L, "vS", part % 4))
            P.add("sp", lambda e, h=h: e.dma_start(out=tab[:], in_=C.alibi_in[h].rearrange("a p j -> p a j")), w=[L + "tab"], dma=L + "tab")

            units = []
            for qi_ in range(NQ):
                q0 = qi_ * QT
                kbs = []
                for kb in range(NT):
                    k0 = kb * 128
                    if k0 >= q0 + QT:
                        dist = k0 - (q0 + QT - 1)
                    elif k0 + 127 < q0:
                        dist = q0 - (k0 + 127)
                    else:
                        dist = 0
                    if dist <= dmax:
                        kbs.append(kb)
                for ik, kb in enumerate(kbs):
                    units.append((qi_, kb, ik, len(kbs)))

            def front(un, h=h, slope=slope):
                qi_, kb, ik, nk = un
                q0 = qi_ * QT
                qslot = qi_ % 2
                qs_ = qS[qslot]
                if ik == 0:
                    P.add("sp", lambda e: e.dma_start(out=qs_[:], in_=st_qT[2 * h:2 * h + 2, :, q0:q0 + QT].rearrange("m d s -> d m s")),
                          w=[(L, "qS", qslot)], dma=(L, "qS", qslot))
                k0 = kb * 128
                delta = q0 - k0
                if delta >= 128:
                    tsel, cc = 0, -slope * delta
                elif delta <= -256:
                    tsel, cc = 1, slope * delta
                elif delta == 0:
                    tsel, cc = 2, 0.0
                else:
                    assert delta == -128
                    tsel, cc = 3, 0.0
                u = cnt[0] % 4
                cnt[0] += 1
                pst = sbank[u]
                for m in range(2):
                    P.add("pe", lambda e, m=m: e.matmul(
                        pst[:, m * 256:(m + 1) * 256], lhsT=kS[:, m, k0:k0 + 128], rhs=qs_[:, m, :], start=True, stop=True),
                        r=[(L, "kS", m), (L, "qS", qslot)], w=[sbankb[u]])
                tm = tmp[u]
                P.add("dve", lambda e: e.scalar_tensor_tensor(
                    out=tm[:], in0=pst, scalar=float(cc), in1=tab[:, tsel, :], op0=ALU.add, op1=ALU.add),
                    r=[sbankb[u], L + "tab"], w=[(L, "tmp", u)])
                pt = pT[u]
                P.add("act", lambda e: e.activation(out=pt[:], in_=tm[:], func=AF.Exp),
                      r=[(L, "tmp", u)], w=[(L, "pT", u)])
                return u

            def back(un, u, h=h):
                qi_, kb, ik, nk = un
                q0 = qi_ * QT
                pt = pT[u]
                for m in range(2):
                    for qh in range(2):
                        a = m * 2 + qh
                        P.add("pe", lambda e, a=a, m=m, qh=qh: e.matmul(
                            acc[a], lhsT=pt[:, m * 256 + qh * 128:m * 256 + (qh + 1) * 128], rhs=vS[:, kb, :],
                            start=(ik == 0), stop=(ik == nk - 1)),
                            r=[(L, "pT", u), L + "vS"], w=[accb[a]])
                if ik != nk - 1:
                    return
                for a in range(4):
                    P.add("dve", lambda e, a=a: e.reciprocal(out=rr[:, a:a + 1], in_=acc[a][:, 256:257]), r=[accb[a]], w=[(L, "rr", a)])
                for qh in range(2):
                    a0, a1 = qh, 2 + qh
                    P.add("dve", lambda e, a1=a1: e.tensor_tensor(out=rr[:, 4 + a1:5 + a1], in0=rr[:, a1:a1 + 1], in1=lv[:, 5:6], op=ALU.mult),
                          r=[(L, "rr", a1), L + "neglam"], w=[(L, "rl", a1)])
                    P.add("act", lambda e, a0=a0, qh=qh: e.activation(out=ot[qh][:], in_=acc[a0][:, 0:256], func=AF.Copy, scale=rr[:, a0:a0 + 1]),
                          r=[accb[a0], (L, "rr", a0)], w=[(L, "ot", qh)])
                    o_ = oo[qh]
                    P.add("dve", lambda e, a1=a1, o_=o_, qh=qh: e.scalar_tensor_tensor(
                        out=o_[:], in0=acc[a1][:, 0:256], scalar=rr[:, 4 + a1:5 + a1], in1=ot[qh][:], op0=ALU.mult, op1=ALU.add),
                        r=[accb[a1], (L, "rl", a1), (L, "ot", qh)], w=[(L, "oo", qh)])
                    r0 = q0 + qh * 128
                    P.add("sp", lambda e, o_=o_, r0=r0: e.dma_start(out=st_o[r0:r0 + 128, h * 256:(h + 1) * 256], in_=o_[:]),
                          r=[(L, "oo", qh)], w=[(L, "d_o", h, qi_, qh)], dma=(L, "so", qh))

            LAG = 3
            ubuf = {}
            for idx in range(len(units) + LAG):
                if idx < len(units):
                    ubuf[idx] = front(units[idx])
                if idx - LAG >= 0:
                    back(units[idx - LAG], ubuf.pop(idx - LAG))
    P.barrier()

    with contextlib.ExitStack() as es:
        def sb(name, shape, dt):
            return es.enter_context(nc.sbuf_tensor("sb_" + L + "d" + name, list(shape), dt))
        Wout = sb("Wout", [128, 16, D], BF16)
        ogtab = sb("ogtab", [128, 256], F32)
        of2 = [sb("of%d" % i, [128, DI], F32) for i in range(2)]
        sq3 = sb("sq", [128, DI], F32)
        szb32 = [sb("szb%d" % i, [128, DI], BF16) for i in range(2)]
        st2 = [sb("st%d" % i, [128, 32], F32) for i in range(2)]
        yb2 = [sb("yb%d" % i, [128, DI], BF16) for i in range(2)]
        yT = sb("yT", [128, DI], BF16)
        xn = sb("xn", [128, D], F32)
        wo = W["c_w_out"][0].rearrange("(kc p) f -> p kc f", p=128)
        for kc in range(0, 16, 4):
            P.add("pool", lambda e, kc=kc: e.dma_start(out=Wout[:, kc:kc + 4, :], in_=wo[:, kc:kc + 4, :]), w=[(L, "Wout", kc)], dma=(L, "W", 0))
        Wout_bufs = [(L, "Wout", kc) for kc in range(0, 16, 4)]
        P.add("sp", lambda e: e.dma_start(out=ogtab[:], in_=W["c_o_g"][0:1, :].partition_broadcast(128)), w=[L + "ogtab"], dma=L + "ogtab")

        def f3(ti):
            slot = ti % 2
            rows = slice(ti * 128, (ti + 1) * 128)
            xt = C.xt[slot]
            of, szb3, st, yb = of2[slot], szb32[slot], st2[slot], yb2[slot]
            P.add("sp", lambda e: e.dma_start(out=xt[:], in_=x_src[rows, :]), w=[("xt", slot)], dma=("xt", slot))
            P.add("sp", lambda e: e.dma_start(out=of[:], in_=st_o[rows, :]), w=[(L, "of", slot)], dma=(L, "lof", slot))
            P.add("sp", lambda e: e.dma_start(out=szb3[:], in_=st_sz[rows, :]), w=[(L, "szb3", slot)], dma=(L, "lsz", slot))
            yield
            P.add("pool", lambda e: e.tensor_tensor(out=sq3[:], in0=of[:], in1=of[:], op=ALU.mult), r=[(L, "of", slot)], w=[L + "sq3"])
            P.add("dve", lambda e: e.tensor_reduce(out=st[:, 0:8], in_=sq3[:].rearrange("p (g d) -> p g d", g=8), axis=AX.X, op=ALU.add),
                  r=[L + "sq3"], w=[(L, "st0", slot)])
            P.add("act", lambda e: e.activation(out=st[:, 8:16], in_=st[:, 0:8], func=AF.Sqrt, scale=1.0 / 256, bias=C.epsb[:, 0:1]),
                  r=[(L, "st0", slot), "epsb"], w=[(L, "st8", slot)])
            P.add("dve", lambda e: e.reciprocal(out=st[:, 16:24], in_=st[:, 8:16]), r=[(L, "st8", slot)], w=[(L, "st16", slot)])
            P.add("dve", lambda e: e.tensor_scalar(out=st[:, 16:24], in0=st[:, 16:24], scalar1=1.0 - lambda_init, scalar2=None, op0=ALU.mult),
                  r=[(L, "st16", slot)], w=[(L, "st16", slot)])
            yield
            for g in range(8):
                P.add("dve", lambda e, g=g: e.scalar_tensor_tensor(
                    out=of[:, g * 256:(g + 1) * 256], in0=of[:, g * 256:(g + 1) * 256], scalar=st[:, 16 + g:17 + g], in1=ogtab[:],
                    op0=ALU.mult, op1=ALU.mult), r=[(L, "of", slot), (L, "st16", slot), L + "ogtab", L + "sq3"], w=[(L, "of", slot)])
                if g % 4 == 3:
                    yield
            P.add("pool", lambda e: e.tensor_tensor(out=yb[:], in0=of[:], in1=szb3[:], op=ALU.mult),
                  r=[(L, "of", slot), (L, "szb3", slot)], w=[(L, "yb", slot)])
            yield

        def b3(ti):
            slot = ti % 2
            rows = slice(ti * 128, (ti + 1) * 128)
            yield from emit_out_proj(P, C, L, li, ti, yb2[slot], [(L, "yb", slot)] * 4, yT, Wout, Wout_bufs, C.xt[slot], ("xt", slot),
                                     xn, x_dst, rows, gen=True)

        interleave([f3(0)])
        for ti in range(NT):
            gens = [b3(ti)]
            if ti + 1 < NT:
                gens.append(f3(ti + 1))
            interleave(gens)


def emit_out_proj(P, C, L, li, ti, yb, yb_bufs, yT, Wout, Wout_bufs, xt, xtb, xn, x_dst, rows, gen=False):
    g = _emit_out_proj(P, C, L, li, ti, yb, yb_bufs, yT, Wout, Wout_bufs, xt, xtb, xn, x_dst, rows)
    if gen:
        return g
    for _ in g:
        pass


def _emit_out_proj(P, C, L, li, ti, yb, yb_bufs, yT, Wout, Wout_bufs, xt, xtb, xn, x_dst, rows):
    for kc in range(16):
        P.add("pe", lambda e, kc=kc: e.transpose(out=C.psYT[:, kc * 128:(kc + 1) * 128], in_=yb[:, kc * 128:(kc + 1) * 128],
                                                  identity=C.ident[:]),
              r=[yb_bufs[kc // 4], "ident"], w=["psYT"])
    P.add("act", lambda e: e.copy(out=yT[:], in_=C.psYT[:]), r=["psYT"], w=[L + "yT"])
    yield
    for nb in range(2):
        for kc in range(16):
            P.add("pe", lambda e, kc=kc, nb=nb: e.matmul(
                C.psO[:, nb * 512:(nb + 1) * 512], lhsT=yT[:, kc * 128:(kc + 1) * 128],
                rhs=Wout[:, kc, nb * 512:(nb + 1) * 512], start=(kc == 0), stop=(kc == 15)),
                r=[L + "yT", Wout_bufs[kc // 4]], w=[("psO", nb)])
        yield
    P.add("dve", lambda e: e.tensor_tensor(out=xn[:], in0=C.psO[:], in1=xt[:], op=ALU.add),
          r=[("psO", 0), ("psO", 1), xtb], w=[L + "xn"])
    P.add("sp", lambda e: e.dma_start(out=x_dst[rows, :], in_=xn[:]),
          r=[L + "xn"], w=[("xdram", li, ti)], dma=("xst", li, ti % 2))
    yield


def host_consts():
    ident = np.eye(128, dtype=np.float32).astype(ml_dtypes.bfloat16)
    tp = np.arange(128)[:, None]
    t = np.arange(128)[None, :]
    f = lambda m: m.astype(np.float32)
    cm = np.stack([
        f(tp <= t) - f(tp <= 64), f(tp <= t), f(tp > t),
        f(tp >= t) - f(tp >= 63), f(tp >= t), f(tp < t),
    ]).astype(np.float32) * (-1.0 / 16.0)
    sidx = np.arange(128)[:, None]
    tidx = np.arange(128)[None, :]
    mf = (tidx >= sidx).astype(np.float32)
    mb = (tidx < sidx).astype(np.float32)
    gmask = np.stack([np.tile(mf, (1, 4)), np.tile(mb, (1, 4))]).astype(ml_dtypes.bfloat16)
    p = np.arange(128, dtype=np.float64)[:, None]
    j = np.tile(np.arange(256, dtype=np.float64), 2)[None, :]
    alibi = np.zeros((8, 4, 128, 512), np.float32)
    for h in range(8):
        slope = 2.0 ** (-(h + 1))
        alibi[h, 0] = -slope * (j - p)
        alibi[h, 1] = -slope * (p - j)
        alibi[h, 2] = -slope * np.abs(j - p)
        alibi[h, 3] = -slope * np.abs(j - p - 128)
    return {"ident": ident, "cm": cm, "gmask": gmask, "alibi": alibi}


def run_model(inputs, S, depth):
    nc = build_program(S, depth)
    xall = np.concatenate([np.asarray(inputs["x_prompt"], np.float32), np.asarray(inputs["x_sample"], np.float32)], axis=0)
    consts = host_consts()
    in_maps = []
    for c in range(NCORES):
        m = {"x": np.ascontiguousarray(xall[c]) if c < 3 else np.zeros_like(xall[0])}
        m.update(consts)
        for k in WSHAPES:
            m[k] = np.ascontiguousarray(np.asarray(inputs[k], np.float32))
        in_maps.append(m)
    res = run_bass_kernel_spmd(nc, in_maps, core_ids=list(range(NCORES)))
    yall = np.stack([res.results[c]["y"] for c in range(3)], axis=0)
    return np.ascontiguousarray(yall[0:2]), np.ascontiguousarray(yall[2:3])


def kernel(**inputs):
    return run_model(inputs, 16384, 4)
```

```python
import os
import numpy as np
import ml_dtypes
import concourse.bass as bass
import concourse.mybir as mybir
from concourse.bass_utils import run_bass_kernel_spmd

F32 = mybir.dt.float32
BF16 = mybir.dt.bfloat16
AF = mybir.ActivationFunctionType
ALU = mybir.AluOpType
AX = mybir.AxisListType

NCORES = 8
D = 1024
DI = 2048
EPS = 1e-6


class _Op:
    __slots__ = ("eng", "fn", "deps", "semkey", "val", "isdma")


def _is_psum(b):
    n = b[0] if isinstance(b, tuple) else b
    return isinstance(n, str) and n.startswith("ps")


class Prog:
    ENGS = ("pe", "act", "dve", "pool", "sp")

    def __init__(self):
        self.ops = {e: [] for e in self.ENGS}
        self.ncomp = {e: 0 for e in self.ENGS}
        self.bufs = {}
        self.dma_last = {}
        self.dma_cnt = {}
        self.all_dma_keys = []
        self.bar = {e: [] for e in self.ENGS}
        self.eng_epochs = set()
        self.EPOCH = int(os.environ.get("EPOCH", "32000"))

    def barrier(self):
        deps = [ops[-1] for e, ops in self.ops.items() if ops]
        for e in self.ENGS:
            cl = [o for o in reversed(self.ops[e]) if not o.isdma]
            if cl:
                deps.append(cl[0])
        deps += list(self.dma_last.values())
        for e in self.ENGS:
            self.bar[e] = list(deps)

    def add(self, eng, fn, r=(), w=(), dma=None):
        op = _Op()
        op.eng = eng
        op.fn = fn
        op.deps = []
        if self.bar[eng]:
            op.deps.extend(self.bar[eng])
            self.bar[eng] = []
        op.isdma = dma is not None
        for b in r:
            st = self.bufs.get(b)
            if st is not None and st[0] is not None:
                op.deps.append(st[0])
            if st is not None and _is_psum(b):
                for o in st[1]:
                    if o.eng != eng:
                        op.deps.append(o)
        for b in w:
            st = self.bufs.get(b)
            if st is not None:
                if st[0] is not None:
                    op.deps.append(st[0])
                op.deps.extend(st[1])
        for b in r:
            st = self.bufs.setdefault(b, [None, []])
            st[1].append(op)
        for b in w:
            self.bufs[b] = [op, []]
        if dma is not None:
            if isinstance(dma, str):
                dma = ("misc", len(dma) % 2)
            prev = self.dma_last.get(dma)
            if prev is not None:
                op.deps.append(prev)
            else:
                self.all_dma_keys.append(dma)
            self.dma_last[dma] = op
            self.dma_cnt[dma] = self.dma_cnt.get(dma, 0) + 16
            op.semkey = ("dma", dma)
            op.val = self.dma_cnt[dma]
        else:
            n = self.ncomp[eng]
            self.ncomp[eng] += 1
            op.semkey = ("eng", eng, n // self.EPOCH)
            op.val = n % self.EPOCH + 1
            self.eng_epochs.add(op.semkey)
        self.ops[eng].append(op)
        return op

    def emit(self, nc, tail_wait_keys=()):
        import contextlib
        with contextlib.ExitStack() as es:
            sems = {}
            for k in sorted(self.eng_epochs):
                sems[k] = es.enter_context(nc.semaphore("s_%s_%d" % (k[1], k[2])))
            for i, k in enumerate(self.all_dma_keys):
                sems[("dma", k)] = es.enter_context(nc.semaphore("d%d" % i))
            block = es.enter_context(nc.Block())

            def run(engname, eng):
                waited = {}
                for op in self.ops[engname]:
                    for d in op.deps:
                        if d.eng == "pe" and engname == "pe" and not d.isdma and not op.isdma:
                            continue
                        if waited.get(d.semkey, 0) >= d.val:
                            continue
                        eng.wait_ge(sems[d.semkey], d.val)
                        waited[d.semkey] = d.val
                    ins = op.fn(eng)
                    ins.then_inc(sems[op.semkey], 16 if op.isdma else 1)
                for k in self.all_dma_keys:
                    last = self.dma_last[k]
                    if last.eng == engname and waited.get(last.semkey, 0) < last.val:
                        eng.wait_ge(sems[last.semkey], last.val)

            @block.tensor
            def _(e):
                run("pe", e)

            @block.scalar
            def _(e):
                run("act", e)

            @block.vector
            def _(e):
                run("dve", e)

            @block.gpsimd
            def _(e):
                run("pool", e)

            @block.sync
            def _(e):
                run("sp", e)


def bcast_rows(ap_row, nparts):
    return ap_row.partition_broadcast(nparts)


class Ctx:
    pass


def emit_norm_T(P, C, x_src_ap, tag, gtab, slot):
    xt = C.xt[slot]
    xb = C.xb
    hT = C.hT[slot]
    P.add("sp", lambda e: e.dma_start(out=xt[:], in_=x_src_ap), w=[("xt", slot)], dma=("xt", slot))
    P.add("act", lambda e: e.activation(out=C.junk[:, 0:D], in_=xt[:], func=AF.Square, accum_out=C.ss[:, 0:1]),
          r=[("xt", slot)], w=["junk", "ss"])
    P.add("act", lambda e: e.activation(out=C.ss[:, 1:2], in_=C.ss[:, 0:1], func=AF.Sqrt, scale=1.0 / D, bias=C.epsb[:, 0:1]),
          r=["ss", "epsb"], w=["ss1"])
    P.add("dve", lambda e: e.reciprocal(out=C.ss[:, 2:3], in_=C.ss[:, 1:2]), r=["ss1"], w=["ss2"])
    P.add("dve", lambda e: e.scalar_tensor_tensor(out=xb[:], in0=xt[:], scalar=C.ss[:, 2:3], in1=gtab[:],
                                                  op0=ALU.mult, op1=ALU.mult),
          r=[("xt", slot), "ss2", "gtab"], w=["xb"])
    for kc in range(8):
        P.add("pe", lambda e, kc=kc: e.transpose(out=C.psT[:, kc * 128:(kc + 1) * 128], in_=xb[:, kc * 128:(kc + 1) * 128],
                                                  identity=C.ident[:]),
              r=["xb", "ident"], w=["psT"])
    P.add("act", lambda e: e.copy(out=hT[:], in_=C.psT[:]), r=["psT"], w=[("hT", slot)])


def build_program(SL, depth):
    NT = SL // 128
    nc = bass.Bass("TRN2", target_bir_lowering=False)
    P = Prog()
    C = Ctx()
    C.NT = NT

    def din(name, shape, dt=F32):
        return nc.dram_tensor(name, list(shape), dt, kind="ExternalInput").ap()

    x_in = din("x", [SL, D])
    W = {}
    for k, shp in WSHAPES.items():
        W[k] = din(k, shp)
    ident_in = din("ident", [128, 128], BF16)
    C.cm_in = din("cm", [6, 128, 128])
    C.gmask_in = din("gmask", [2, 128, 512], BF16)
    C.alibi_in = din("alibi", [8, 4, 128, 512])
    y_out = nc.dram_tensor("y", [SL, D], F32, kind="ExternalOutput").ap()
    xs = [nc.dram_tensor("xs%d" % i, [SL, D], F32, kind="Internal").ap() for i in range(2)]
    C.dram = lambda name, shape, dt: nc.dram_tensor(name, list(shape), dt, kind="Internal").ap()

    import contextlib
    with contextlib.ExitStack() as es:
        def sb(name, shape, dt):
            return es.enter_context(nc.sbuf_tensor("sb_" + name, list(shape), dt))

        def ps(name, shape, dt):
            return es.enter_context(nc.psum_tensor("ps_" + name, list(shape), dt))

        C.ident = sb("ident", [128, 128], BF16)
        C.xt = [sb("xt%d" % i, [128, D], F32) for i in range(2)]
        C.xb = sb("xb", [128, D], BF16)
        C.hT = [sb("hT%d" % i, [128, D], BF16) for i in range(2)]
        C.junk = sb("junk", [128, D], BF16)
        C.ss = sb("ss", [128, 16], F32)
        C.epsb = sb("epsb", [128, 1], F32)
        C.oneb = sb("oneb", [128, 1], F32)
        C.gtab = sb("gtab", [128, D], F32)
        C.psA = [ps("psA%d" % i, [128, 512], F32) for i in range(3)]
        C.bankT = ps("bankT", [128, 512], F32)
        C.psT = C.bankT.bitcast(BF16)
        C.bankY = ps("bankY", [128, 1024], F32)
        C.psYT = C.bankY.bitcast(BF16)
        C.psO = ps("psO", [128, D], F32)
        C.nc = nc
        C.pa = [0]

        P.add("sp", lambda e: e.dma_start(out=C.ident[:], in_=ident_in), w=["ident"], dma="ident")
        P.add("pool", lambda e: e.memset(C.epsb[:], EPS), w=["epsb"])
        P.add("pool", lambda e: e.memset(C.oneb[:], 1.0), w=["oneb"])

        layer_kinds = [0, 1, 2, 0][:depth]
        cur = x_in
        for li, kind in enumerate(layer_kinds):
            dst = y_out if li == depth - 1 else xs[li % 2]
            with contextlib.ExitStack() as les:
                P.barrier()
                if kind == 0:
                    layer_A(nc, P, C, les, cur, dst, li, li // 3, NT, W["norm_g"], W["a_w_in"], W["a_v_g"], W["a_w_s"],
                            W["a_b_s"], W["a_w_out"])
                elif kind == 1:
                    layer_B(nc, P, C, les, cur, dst, li, NT, W)
                else:
                    layer_C(nc, P, C, les, cur, dst, li, NT, W)
            cur = dst
        P.emit(nc)
    return nc


WSHAPES = {
    "norm_g": [4, D], "a_w_in": [2, D, 3 * DI], "a_v_g": [2, DI], "a_w_s": [2, 8, 128, 128], "a_b_s": [2, 8, 128],
    "a_w_out": [2, DI, D], "b_w_in": [1, D, 5152], "b_w_gate": [1, 2, 16, 512], "b_gate_bias": [1, 2, 512],
    "b_o_g": [1, 512], "b_w_out": [1, DI, D], "c_w_in": [1, D, 4 * DI], "c_q_g": [1, 128], "c_k_g": [1, 128],
    "c_lam": [1, 4, 128], "c_o_g": [1, 256], "c_w_out": [1, DI, D],
}


def interleave(gens):
    gens = list(gens)
    while gens:
        nxt = []
        for g in gens:
            try:
                next(g)
                nxt.append(g)
            except StopIteration:
                pass
        gens = nxt


def layer_A(nc, P, C, es, x_src, x_dst, li, j, NT, norm_g, a_w_in, a_v_g, a_w_s, a_b_s, a_w_out):
    L = "A%d" % li

    def sb(name, shape, dt):
        return es.enter_context(nc.sbuf_tensor("sb_" + L + name, list(shape), dt))

    Win = sb("Win", [128, 8, 3 * DI], BF16)
    Wout = sb("Wout", [128, 16, D], BF16)
    t1 = [sb("t1%d" % i, [128, 512], F32) for i in range(2)]
    yb = sb("yb", [128, DI], BF16)
    wsq = sb("wsq", [128, 8, 128], F32)
    wsqb = yb[:, 0:1024].rearrange("p (g q) -> p g q", g=8)
    wsT = sb("wsT", [128, 8, 128], BF16)
    bs = sb("bs", [128, 8], F32)
    vgtab = sb("vgtab", [128, DI], F32)
    u = [sb("u%d" % i, [128, DI], BF16) for i in range(2)]
    sz = [sb("sz%d" % i, [128, DI], BF16) for i in range(2)]
    v = sb("v", [128, DI], F32)
    vs = sb("vs", [128, DI], BF16)
    yT = sb("yT", [128, DI], BF16)
    xn = sb("xn", [128, D], F32)
    st = sb("st", [128, 16], F32)

    wv = a_w_in[j].rearrange("(kc p) f -> p kc f", p=128)
    for kc in range(8):
        P.add("pool", lambda e, kc=kc: e.dma_start(out=Win[:, kc, :], in_=wv[:, kc, :]),
              w=[(L, "Win", kc)], dma=(L, "Win", kc % 2))
    wo = a_w_out[j].rearrange("(kc p) f -> p kc f", p=128)
    for kc in range(0, 16, 4):
        P.add("pool", lambda e, kc=kc: e.dma_start(out=Wout[:, kc:kc + 4, :], in_=wo[:, kc:kc + 4, :]),
              w=[(L, "Wout", kc)], dma=(L, "Wout", (kc // 4) % 2))
    P.add("sp", lambda e: e.dma_start(out=wsq[:], in_=a_w_s[j].rearrange("g q p -> q g p")), w=[L + "wsq"], dma=L + "wsq")
    P.add("sp", lambda e: e.dma_start(out=bs[:], in_=a_b_s[j].rearrange("g q -> q g"), allow_slow_non_contiguous=True),
          w=[L + "bs"], dma=L + "bs")
    P.add("sp", lambda e: e.dma_start(out=C.gtab[:], in_=norm_g[li:li + 1, :].partition_broadcast(128)),
          w=["gtab"], dma="gtab")
    P.add("sp", lambda e: e.dma_start(out=vgtab[:], in_=a_v_g[j:j + 1, :].partition_broadcast(128)),
          w=[L + "vgtab"], dma=L + "vgtab")
    P.add("dve", lambda e: e.tensor_copy(out=wsqb, in_=wsq[:]), r=[L + "wsq"], w=[(L, "yb", 0), (L, "yb", 1)])
    for g in range(8):
        P.add("pe", lambda e, g=g: e.transpose(out=C.psT[:, g * 128:(g + 1) * 128], in_=wsqb[:, g, :], identity=C.ident[:]),
              r=[(L, "yb", 0), (L, "yb", 1), "ident"], w=["psT"])
    P.add("act", lambda e: e.copy(out=wsT[:].rearrange("p g q -> p (g q)"), in_=C.psT[:]), r=["psT"], w=[L + "wsT"])

    Win_bufs = [(L, "Win", kc) for kc in range(8)]
    Wout_bufs = [(L, "Wout", kc) for kc in range(0, 16, 4)]

    def next_psA():
        i = C.pa[0] % 3
        C.pa[0] += 1
        return i

    def front(ti):
        slot = ti % 2
        rows = slice(ti * 128, (ti + 1) * 128)
        emit_norm_T(P, C, x_src[rows, :], L, C.gtab, slot)
        hT = C.hT[slot]
        u_, sz_ = u[slot], sz[slot]
        yield
        for cb in range(12):
            pi = next_psA()
            pst = C.psA[pi]
            for kc in range(8):
                P.add("pe", lambda e, kc=kc, cb=cb, pst=pst: e.matmul(
                    pst[:], lhsT=hT[:, kc * 128:(kc + 1) * 128], rhs=Win[:, kc, cb * 512:(cb + 1) * 512],
                    start=(kc == 0), stop=(kc == 7)),
                    r=[("hT", slot), Win_bufs[kc]], w=[("psA", pi)])
            c0 = (cb % 4) * 512
            if cb < 4:
                P.add("act", lambda e, pst=pst, c0=c0: e.copy(out=u_[:, c0:c0 + 512], in_=pst[:]),
                      r=[("psA", pi)], w=[(L, "u", slot, cb)])
            elif cb < 8:
                P.add("dve", lambda e, pst=pst, c0=c0: e.tensor_copy(out=v[:, c0:c0 + 512], in_=pst[:]),
                      r=[("psA", pi)], w=[(L, "v", cb - 4)])
                P.add("act", lambda e, pst=pst, cb=cb: e.activation(out=C.junk[:, 0:512], in_=pst[:], func=AF.Square,
                                                                      accum_out=st[:, cb - 4:cb - 3]),
                      r=[("psA", pi)], w=["junk", (L, "ssv", cb - 4)])
            else:
                P.add("act", lambda e, pst=pst, c0=c0: e.activation(out=sz_[:, c0:c0 + 512], in_=pst[:], func=AF.Silu),
                      r=[("psA", pi)], w=[(L, "sz", slot, cb - 8)])
            yield

    def back(ti):
        slot = ti % 2
        rows = slice(ti * 128, (ti + 1) * 128)
        xt = C.xt[slot]
        u_, sz_ = u[slot], sz[slot]
        P.add("dve", lambda e: e.tensor_reduce(out=st[:, 4:5], in_=st[:, 0:4], axis=AX.X, op=ALU.add),
              r=[(L, "ssv", i) for i in range(4)], w=[L + "st4"])
        P.add("act", lambda e: e.activation(out=st[:, 5:6], in_=st[:, 4:5], func=AF.Sqrt, scale=1.0 / DI, bias=C.epsb[:, 0:1]),
              r=[L + "st4", "epsb"], w=[L + "st5"])
        P.add("dve", lambda e: e.reciprocal(out=st[:, 6:7], in_=st[:, 5:6]), r=[L + "st5"], w=[L + "st6"])
        for b in range(4):
            P.add("dve", lambda e, b=b: e.tensor_scalar(out=vs[:, b * 512:(b + 1) * 512], in0=v[:, b * 512:(b + 1) * 512],
                                                         scalar1=st[:, 6:7], scalar2=None, op0=ALU.mult),
                  r=[(L, "v", b), L + "st6"], w=[(L, "vs", b)])
        yield
        for b in range(4):
            pi = next_psA()
            pst = C.psA[pi]
            for gg in range(2):
                g = 2 * b + gg
                P.add("pe", lambda e, g=g, gg=gg, pst=pst: e.matmul(
                    pst[:, gg * 256:(gg + 1) * 256], lhsT=wsT[:, g, :], rhs=vs[:, g * 256:(g + 1) * 256],
                    start=True, stop=True),
                    r=[L + "wsT", (L, "vs", b)], w=[("psA", pi)])
            tt = t1[b % 2]
            P.add("dve", lambda e, pst=pst, tt=tt, b=b: e.tensor_tensor(out=tt[:], in0=pst[:], in1=vgtab[:, b * 512:(b + 1) * 512],
                                                                        op=ALU.mult),
                  r=[("psA", pi), L + "vgtab"], w=[(L, "t1", b % 2)])
            for gg in range(2):
                g = 2 * b + gg
                P.add("dve", lambda e, tt=tt, g=g, gg=gg: e.scalar_tensor_tensor(
                    out=tt[:, gg * 256:(gg + 1) * 256], in0=tt[:, gg * 256:(gg + 1) * 256], scalar=bs[:, g:g + 1],
                    in1=u_[:, g * 256:(g + 1) * 256], op0=ALU.add, op1=ALU.mult),
                    r=[(L, "t1", b % 2), L + "bs", (L, "u", slot, b)], w=[(L, "t1", b % 2)])
            P.add("pool", lambda e, tt=tt, b=b: e.tensor_tensor(out=yb[:, b * 512:(b + 1) * 512], in0=tt[:],
                                                                in1=sz_[:, b * 512:(b + 1) * 512], op=ALU.mult),
                  r=[(L, "t1", b % 2), (L, "sz", slot, b)], w=[(L, "yb", b // 2)])
            yield
        for kc in range(16):
            P.add("pe", lambda e, kc=kc: e.transpose(out=C.psYT[:, kc * 128:(kc + 1) * 128], in_=yb[:, kc * 128:(kc + 1) * 128],
                                                      identity=C.ident[:]),
                  r=[(L, "yb", kc // 8), "ident"], w=["psYT"])
        P.add("act", lambda e: e.copy(out=yT[:], in_=C.psYT[:]), r=["psYT"], w=[L + "yT"])
        yield
        for nb in range(2):
            for kc in range(16):
                P.add("pe", lambda e, kc=kc, nb=nb: e.matmul(
                    C.psO[:, nb * 512:(nb + 1) * 512], lhsT=yT[:, kc * 128:(kc + 1) * 128],
                    rhs=Wout[:, kc, nb * 512:(nb + 1) * 512], start=(kc == 0), stop=(kc == 15)),
                    r=[L + "yT", Wout_bufs[kc // 4]], w=[("psO", nb)])
            yield
        P.add("dve", lambda e: e.tensor_tensor(out=xn[:], in0=C.psO[:], in1=xt[:], op=ALU.add),
              r=[("psO", 0), ("psO", 1), ("xt", slot)], w=[L + "xn"])
        P.add("sp", lambda e, rows=rows: e.dma_start(out=x_dst[rows, :], in_=xn[:]),
              r=[L + "xn"], w=[("xdram", li, ti)], dma=("xst", li, ti % 2))
        yield

    interleave([front(0)])
    for ti in range(NT):
        gens = [back(ti)]
        if ti + 1 < NT:
            gens.append(front(ti + 1))
        interleave(gens)


def layer_B(nc, P, C, es, x_src, x_dst, li, NT, W):
    L = "B%d" % li
    w_in = W["b_w_in"][0]

    def sb(name, shape, dt):
        return es.enter_context(nc.sbuf_tensor("sb_" + L + name, list(shape), dt))

    Wqk = sb("Wqk", [128, 8, 1024], BF16)
    Wvg = sb("Wvg", [128, 8, 4096], BF16)
    Waf = sb("Waf", [128, 8, 16], BF16)
    Wab = sb("Wab", [128, 8, 16], BF16)
    Wout = Wvg[:, 0:4, :].rearrange("p a (b f) -> p (a b) f", b=4)
    wg = [sb("wg%d" % d, [32, 512], BF16) for d in range(2)]
    aT = [sb("aT%d" % d, [32, 128], BF16) for d in range(2)]
    cm = sb("cm", [128, 6, 128], F32)
    gmask = sb("gmask", [128, 2, 512], BF16)
    ogtab = sb("ogtab", [128, 512], F32)
    qT2 = [sb("qT%d" % i, [128, 512], F32) for i in range(2)]
    kT2 = [sb("kT%d" % i, [128, 512], F32) for i in range(2)]
    ex = sb("ex", [128, 512], F32)
    sp2 = [[sb("sp%d_%d" % (i, d), [128, 512], F32) for d in range(2)] for i in range(2)]
    E = [sb("E%d" % i, [128, 512], F32) for i in range(2)]
    qq = [sb("qq%d" % d, [128, 512], BF16) for d in range(2)]
    kk = [sb("kk%d" % d, [128, 512], BF16) for d in range(2)]
    qi = [sb("qi%d" % d, [128, 512], BF16) for d in range(2)]
    ki = [sb("ki%d" % d, [128, 512], BF16) for d in range(2)]
    kiT = sb("kiT", [128, 1024], BF16)
    scT = [sb("scT%d" % d, [128, 512], BF16) for d in range(2)]
    vb2 = [sb("vb%d" % i, [128, DI], BF16) for i in range(2)]
    sg2 = [sb("sg%d" % i, [128, DI], BF16) for i in range(2)]
    opart = sb("opart", [128, DI], F32)
    dec = sb("dec", [128, 8], F32)
    St = sb("St", [128, DI], F32)
    Stb = sb("Stb", [128, DI], BF16)
    yb = sb("yb", [128, DI], BF16)
    yT = sb("yT", [128, DI], BF16)
    xn = sb("xn", [128, D], F32)
    st = sb("st", [128, 16], F32)
    qi2 = sb("qi2", [128, 512], BF16)
    kiT2 = sb("kiT2", [128, 512], BF16)
    dec2 = sb("dec2", [128, 4], F32)

    st_o = C.dram(L + "st_o", [NT * 128, DI], F32)
    st_v = C.dram(L + "st_v", [NT * 128, DI], BF16)
    st_sg = C.dram(L + "st_sg", [NT * 128, DI], BF16)
    st_qi = C.dram(L + "st_qi", [NT * 128, 512], BF16)
    st_ki = C.dram(L + "st_ki", [NT * 128, 512], BF16)
    st_df = C.dram(L + "st_df", [NT * 128, 4], F32)

    def next_psA():
        i = C.pa[0] % 3
        C.pa[0] += 1
        return i

    wv = w_in.rearrange("(kc p) f -> p kc f", p=128)
    for kc in range(8):
        P.add("pool", lambda e, kc=kc: e.dma_start(out=Wqk[:, kc, :], in_=wv[:, kc, 0:1024]), w=[(L, "Wqk", kc)], dma=(L, "W", 0))
        P.add("pool", lambda e, kc=kc: e.dma_start(out=Wvg[:, kc, :], in_=wv[:, kc, 1024:5120]), w=[(L, "Wvg", kc)], dma=(L, "W", 1))
        P.add("pool", lambda e, kc=kc: e.dma_start(out=Waf[:, kc, :], in_=wv[:, kc, 5120:5136]), w=[(L, "Waf")], dma=(L, "W", 2))
        P.add("pool", lambda e, kc=kc: e.dma_start(out=Wab[:, kc, :], in_=wv[:, kc, 5136:5152]), w=[(L, "Wab")], dma=(L, "W", 3))
    for d in range(2):
        P.add("pool", lambda e, d=d: e.dma_start(out=wg[d][0:16, :], in_=W["b_w_gate"][0, d]), w=[(L, "wg", d)], dma=(L, "W", 2))
        P.add("pool", lambda e, d=d: e.dma_start(out=wg[d][16:17, :], in_=W["b_gate_bias"][0, d:d + 1, :]), w=[(L, "wgb", d)], dma=(L, "W", 3))
        P.add("pool", lambda e, d=d: e.memset(aT[d][:], 1.0), w=[(L, "aT", d)])
    P.add("sp", lambda e: e.dma_start(out=cm[:], in_=C.cm_in.rearrange("m a b -> a m b")), w=[L + "cm"], dma=L + "cm")
    P.add("sp", lambda e: e.dma_start(out=gmask[:], in_=C.gmask_in.rearrange("m a b -> a m b")), w=[L + "gmask"], dma=L + "gmask")
    P.add("sp", lambda e: e.dma_start(out=C.gtab[:], in_=W["norm_g"][li:li + 1, :].partition_broadcast(128)), w=["gtab"], dma="gtab")
    P.add("sp", lambda e: e.dma_start(out=ogtab[:], in_=W["b_o_g"][0:1, :].partition_broadcast(128)), w=[L + "ogtab"], dma=L + "ogtab")
    P.add("dve", lambda e: e.memset(St[:], 0.0), w=[L + "St"])
    P.add("pool", lambda e: e.memset(Stb[:], 0.0), w=[(L, "Stb", h) for h in range(4)])
    Wout_bufs = [(L, "Wout", kc) for kc in range(0, 16, 4)]

    def front1(ti, par):
        slot = par
        rows = slice(ti * 128, (ti + 1) * 128)
        emit_norm_T(P, C, x_src[rows, :], L, C.gtab, slot)
        hT = C.hT[slot]
        hTb = ("hT", slot)
        qT, kT, sp, vb, sg = qT2[par], kT2[par], sp2[par], vb2[par], sg2[par]
        yield
        for qk in range(2):
            pi = next_psA()
            pst = C.psA[pi]
            for h in range(4):
                blk = qk * 4 + h
                for kc in range(8):
                    P.add("pe", lambda e, kc=kc, blk=blk, h=h, pst=pst: e.matmul(
                        pst[:, h * 128:(h + 1) * 128], lhsT=Wqk[:, kc, blk * 128:(blk + 1) * 128], rhs=hT[:, kc * 128:(kc + 1) * 128],
                        start=(kc == 0), stop=(kc == 7)), r=[hTb, (L, "Wqk", kc)], w=[("psA", pi)])
            if qk == 0:
                P.add("act", lambda e, pst=pst: e.activation(out=qT[:], in_=pst[:], func=AF.Copy, scale=128.0 ** -0.5),
                      r=[("psA", pi)], w=[(L, "qT", par)])
            else:
                P.add("act", lambda e, pst=pst: e.copy(out=kT[:], in_=pst[:]), r=[("psA", pi)], w=[(L, "kT", par)])
            yield
        for d, Wa in enumerate((Waf, Wab)):
            pi = next_psA()
            pst = C.psA[pi]
            for kc in range(8):
                P.add("pe", lambda e, kc=kc, Wa=Wa, pst=pst: e.matmul(
                    pst[0:16, 0:128], lhsT=Wa[:, kc, :], rhs=hT[:, kc * 128:(kc + 1) * 128], start=(kc == 0), stop=(kc == 7)),
                    r=[hTb, (L, "Waf"), (L, "Wab")], w=[("psA", pi)])
            P.add("dve", lambda e, d=d, pst=pst: e.tensor_copy(out=aT[d][0:16, :], in_=pst[0:16, 0:128]),
                  r=[("psA", pi)], w=[(L, "aT", d)])
        for d in range(2):
            pi = next_psA()
            pst = C.psA[pi]
            P.add("pe", lambda e, d=d, pst=pst: e.matmul(pst[:], lhsT=aT[d][0:17, :], rhs=wg[d][0:17, :], start=True, stop=True),
                  r=[(L, "aT", d), (L, "wg", d), (L, "wgb", d)], w=[("psA", pi)])
            P.add("act", lambda e, pst=pst: e.activation(out=ex[:], in_=pst[:], func=AF.Exp, scale=-1.0),
                  r=[("psA", pi)], w=[L + "ex"])
            P.add("act", lambda e, d=d: e.activation(out=sp[d][:], in_=ex[:], func=AF.Ln, bias=C.oneb[:, 0:1]),
                  r=[L + "ex", "oneb"], w=[(L, "sp", par, d)])
            yield
        for cb in range(8):
            pi = next_psA()
            pst = C.psA[pi]
            for kc in range(8):
                P.add("pe", lambda e, kc=kc, cb=cb, pst=pst: e.matmul(
                    pst[:], lhsT=hT[:, kc * 128:(kc + 1) * 128], rhs=Wvg[:, kc, cb * 512:(cb + 1) * 512],
                    start=(kc == 0), stop=(kc == 7)), r=[hTb, (L, "Wvg", kc)], w=[("psA", pi)])
            c0 = (cb % 4) * 512
            if cb < 4:
                P.add("dve", lambda e, pst=pst, c0=c0: e.tensor_copy(out=vb[:, c0:c0 + 512], in_=pst[:]),
                      r=[("psA", pi)], w=[(L, "vb", par, cb)])
            else:
                P.add("act", lambda e, pst=pst, c0=c0: e.activation(out=sg[:, c0:c0 + 512], in_=pst[:], func=AF.Silu),
                      r=[("psA", pi)], w=[(L, "sg", par, cb - 4)])
            yield

    def back1(ti, par):
        rows = slice(ti * 128, (ti + 1) * 128)
        qT, kT, sp, vb, sg = qT2[par], kT2[par], sp2[par], vb2[par], sg2[par]
        for d in range(2):
            for m in range(3):
                pi = next_psA()
                pst = C.psA[pi]
                for h in range(4):
                    P.add("pe", lambda e, d=d, m=m, h=h, pst=pst: e.matmul(
                        pst[:, h * 128:(h + 1) * 128], lhsT=sp[d][:, h * 128:(h + 1) * 128], rhs=cm[:, d * 3 + m, :],
                        start=True, stop=True), r=[(L, "sp", par, d), L + "cm"], w=[("psA", pi)])
                if m == 0:
                    P.add("act", lambda e, pst=pst: e.activation(out=E[0][:], in_=pst[:], func=AF.Exp), r=[("psA", pi)], w=[(L, "E", 0)])
                    P.add("dve", lambda e, d=d: e.tensor_tensor(out=qq[d][:], in0=qT[:], in1=E[0][:], op=ALU.mult),
                          r=[(L, "qT", par), (L, "E", 0)], w=[(L, "qq", d)])
                    P.add("act", lambda e, pst=pst: e.activation(out=E[1][:], in_=pst[:], func=AF.Exp, scale=-1.0),
                          r=[("psA", pi)], w=[(L, "E", 1)])
                    P.add("dve", lambda e, d=d: e.tensor_tensor(out=kk[d][:], in0=kT[:], in1=E[1][:], op=ALU.mult),
                          r=[(L, "kT", par), (L, "E", 1)], w=[(L, "kk", d)])
                elif m == 1:
                    P.add("act", lambda e, pst=pst: e.activation(out=E[0][:], in_=pst[:], func=AF.Exp), r=[("psA", pi)], w=[(L, "E", 0)])
                    P.add("dve", lambda e, d=d: e.tensor_tensor(out=qi[d][:], in0=qT[:], in1=E[0][:], op=ALU.mult),
                          r=[(L, "qT", par), (L, "E", 0)], w=[(L, "qi", d)])
                    col = 127 if d == 0 else 0
                    P.add("dve", lambda e, d=d, col=col: e.tensor_copy(
                        out=dec[:, d * 4:(d + 1) * 4], in_=E[0][:].rearrange("p (h t) -> p h t", h=4)[:, :, col]),
                        r=[(L, "E", 0)], w=[(L, "dec", d)])
                else:
                    P.add("act", lambda e, pst=pst: e.activation(out=E[1][:], in_=pst[:], func=AF.Exp), r=[("psA", pi)], w=[(L, "E", 1)])
                    P.add("dve", lambda e, d=d: e.tensor_tensor(out=ki[d][:], in0=kT[:], in1=E[1][:], op=ALU.mult),
                          r=[(L, "kT", par), (L, "E", 1)], w=[(L, "ki", d)])
                yield
        for d in range(2):
            for h in range(4):
                blk = d * 4 + h
                P.add("pe", lambda e, d=d, h=h, blk=blk: e.transpose(out=C.psT[:, blk * 128:(blk + 1) * 128],
                                                                      in_=ki[d][:, h * 128:(h + 1) * 128], identity=C.ident[:]),
                      r=[(L, "ki", d), "ident"], w=["psT"])
        P.add("act", lambda e: e.copy(out=kiT[:], in_=C.psT[:]), r=["psT"], w=[L + "kiT"])
        yield
        for d in range(2):
            pi = next_psA()
            pst = C.psA[pi]
            for h in range(4):
                P.add("pe", lambda e, d=d, h=h, pst=pst: e.matmul(
                    pst[:, h * 128:(h + 1) * 128], lhsT=kk[d][:, h * 128:(h + 1) * 128], rhs=qq[d][:, h * 128:(h + 1) * 128],
                    start=True, stop=True), r=[(L, "kk", d), (L, "qq", d)], w=[("psA", pi)])
            P.add("dve", lambda e, d=d, pst=pst: e.tensor_tensor(out=scT[d][:], in0=pst[:], in1=gmask[:, d, :], op=ALU.mult),
                  r=[("psA", pi), L + "gmask"], w=[(L, "scT", d)])
            yield
        for h in range(4):
            pi = next_psA()
            pst = C.psA[pi]
            hs = slice(h * 128, (h + 1) * 128)
            vs_ = slice(h * 512, (h + 1) * 512)
            P.add("pe", lambda e, pst=pst, hs=hs, vs_=vs_: e.matmul(pst[:], lhsT=scT[0][:, hs], rhs=vb[:, vs_], start=True, stop=False),
                  r=[(L, "scT", 0), (L, "vb", par, h)], w=[("psA", pi)])
            P.add("pe", lambda e, pst=pst, hs=hs, vs_=vs_: e.matmul(pst[:], lhsT=scT[1][:, hs], rhs=vb[:, vs_], start=False, stop=False),
                  r=[(L, "scT", 1), (L, "vb", par, h)], w=[("psA", pi)])
            P.add("pe", lambda e, pst=pst, hs=hs, vs_=vs_: e.matmul(pst[:], lhsT=qi[1][:, hs], rhs=Stb[:, vs_], start=False, stop=True),
                  r=[(L, "qi", 1), (L, "Stb", h)], w=[("psA", pi)])
            P.add("act", lambda e, pst=pst, vs_=vs_: e.copy(out=opart[:, vs_], in_=pst[:]), r=[("psA", pi)], w=[(L, "opart", h)])
            yield
        for h in range(4):
            pi = next_psA()
            pst = C.psA[pi]
            hs2 = slice((4 + h) * 128, (5 + h) * 128)
            vs_ = slice(h * 512, (h + 1) * 512)
            P.add("pe", lambda e, pst=pst, hs2=hs2, vs_=vs_: e.matmul(pst[:], lhsT=kiT[:, hs2], rhs=vb[:, vs_], start=True, stop=True),
                  r=[L + "kiT", (L, "vb", par, h)], w=[("psA", pi)])
            P.add("dve", lambda e, pst=pst, vs_=vs_, h=h: e.scalar_tensor_tensor(
                out=St[:, vs_], in0=St[:, vs_], scalar=dec[:, 4 + h:5 + h], in1=pst[:], op0=ALU.mult, op1=ALU.add),
                r=[("psA", pi), (L, "dec", 1), L + "St"], w=[L + "St"])
            P.add("act", lambda e, vs_=vs_: e.copy(out=Stb[:, vs_], in_=St[:, vs_]), r=[L + "St"], w=[(L, "Stb", h)])
            yield
        k2 = ti % 2
        P.add("sp", lambda e: e.dma_start(out=st_o[rows, :], in_=opart[:]), r=[(L, "opart", h) for h in range(4)],
              w=[(L, "d_o", ti)], dma=(L, "s0", k2))
        P.add("sp", lambda e: e.dma_start(out=st_v[rows, :], in_=vb[:]), r=[(L, "vb", par, h) for h in range(4)],
              w=[(L, "d_v", ti)], dma=(L, "s1", k2))
        P.add("sp", lambda e: e.dma_start(out=st_sg[rows, :], in_=sg[:]), r=[(L, "sg", par, h) for h in range(4)],
              w=[(L, "d_sg", ti)], dma=(L, "s2", k2))
        P.add("sp", lambda e: e.dma_start(out=st_qi[rows, :], in_=qi[0][:]), r=[(L, "qi", 0)], w=[(L, "d_qi", ti)], dma=(L, "s3", k2))
        P.add("sp", lambda e: e.dma_start(out=st_ki[rows, :], in_=kiT[:, 0:512]), r=[L + "kiT"], w=[(L, "d_ki", ti)], dma=(L, "s4", k2))
        P.add("sp", lambda e: e.dma_start(out=st_df[rows, :], in_=dec[:, 0:4]), r=[(L, "dec", 0)], w=[(L, "d_df", ti)], dma=(L, "s5", k2))

        yield

    order = list(reversed(range(NT)))
    interleave([front1(order[0], 0)])
    for n, ti in enumerate(order):
        gens = [back1(ti, n % 2)]
        if n + 1 < NT:
            gens.append(front1(order[n + 1], (n + 1) % 2))
        interleave(gens)

    P.add("dve", lambda e: e.memset(St[:], 0.0), r=[L + "St"], w=[L + "St"])
    P.add("pool", lambda e: e.memset(Stb[:], 0.0), w=[(L, "Stb", h) for h in range(4)])
    wo = W["b_w_out"][0].rearrange("(kc p) f -> p kc f", p=128)
    for kc in range(0, 16, 4):
        P.add("pool", lambda e, kc=kc: e.dma_start(out=Wout[:, kc:kc + 4, :], in_=wo[:, kc:kc + 4, :]),
              w=[(L, "Wout", kc), (L, "Wvg", kc // 4)], dma=(L, "W", 0))

    yb2 = [yb, sb("yb2", [128, DI], BF16)]
    qi22 = [qi2, sb("qi2b", [128, 512], BF16)]
    kiT22 = [kiT2, sb("kiT2b", [128, 512], BF16)]
    dec22 = [dec2, sb("dec2b", [128, 4], F32)]
    st22 = [st, sb("stb", [128, 16], F32)]

    def front2(ti):
        slot = ti % 2
        rows = slice(ti * 128, (ti + 1) * 128)
        xt = C.xt[slot]
        vb, sg, yb_, qi2_, kiT2_, dec2_, st_ = vb2[slot], sg2[slot], yb2[slot], qi22[slot], kiT22[slot], dec22[slot], st22[slot]
        P.add("sp", lambda e: e.dma_start(out=xt[:], in_=x_src[rows, :]), w=[("xt", slot)], dma=("xt", slot))
        P.add("sp", lambda e: e.dma_start(out=opart[:], in_=st_o[rows, :]), r=[(L, "d_o", ti)], w=[(L, "opart", h) for h in range(4)], dma=(L, "l0"))
        P.add("sp", lambda e: e.dma_start(out=vb[:], in_=st_v[rows, :]), r=[(L, "d_v", ti)], w=[(L, "vb", slot, h) for h in range(4)], dma=(L, "l1", slot))
        P.add("sp", lambda e: e.dma_start(out=sg[:], in_=st_sg[rows, :]), r=[(L, "d_sg", ti)], w=[(L, "sg", slot, h) for h in range(4)], dma=(L, "l2", slot))
        P.add("sp", lambda e: e.dma_start(out=qi2_[:], in_=st_qi[rows, :]), r=[(L, "d_qi", ti)], w=[(L, "qi2", slot)], dma=(L, "l3", slot))
        P.add("sp", lambda e: e.dma_start(out=kiT2_[:], in_=st_ki[rows, :]), r=[(L, "d_ki", ti)], w=[(L, "kiT2", slot)], dma=(L, "l4", slot))
        P.add("sp", lambda e: e.dma_start(out=dec2_[:], in_=st_df[rows, :]), r=[(L, "d_df", ti)], w=[(L, "dec2", slot)], dma=(L, "l5", slot))
        yield
        for h in range(4):
            pi = next_psA()
            pst = C.psA[pi]
            hs = slice(h * 128, (h + 1) * 128)
            vs_ = slice(h * 512, (h + 1) * 512)
            P.add("pe", lambda e, pst=pst, hs=hs, vs_=vs_: e.matmul(pst[:], lhsT=qi2_[:, hs], rhs=Stb[:, vs_], start=True, stop=True),
                  r=[(L, "qi2", slot), (L, "Stb", h)], w=[("psA", pi)])
            P.add("dve", lambda e, pst=pst, vs_=vs_: e.tensor_tensor(out=opart[:, vs_], in0=pst[:], in1=opart[:, vs_], op=ALU.add),
                  r=[("psA", pi), (L, "opart", h)], w=[(L, "opart", h)])
            P.add("act", lambda e, vs_=vs_, h=h: e.activation(out=C.junk[:, 0:512], in_=opart[:, vs_], func=AF.Square,
                                                               accum_out=st_[:, h:h + 1]),
                  r=[(L, "opart", h)], w=["junk", (L, "sso", slot, h)])
            yield
        for h in range(4):
            pi = next_psA()
            pst = C.psA[pi]
            hs = slice(h * 128, (h + 1) * 128)
            vs_ = slice(h * 512, (h + 1) * 512)
            P.add("pe", lambda e, pst=pst, hs=hs, vs_=vs_: e.matmul(pst[:], lhsT=kiT2_[:, hs], rhs=vb[:, vs_], start=True, stop=True),
                  r=[(L, "kiT2", slot), (L, "vb", slot, h)], w=[("psA", pi)])
            P.add("dve", lambda e, pst=pst, vs_=vs_, h=h: e.scalar_tensor_tensor(
                out=St[:, vs_], in0=St[:, vs_], scalar=dec2_[:, h:h + 1], in1=pst[:], op0=ALU.mult, op1=ALU.add),
                r=[("psA", pi), (L, "dec2", slot), L + "St"], w=[L + "St"])
            P.add("act", lambda e, vs_=vs_: e.copy(out=Stb[:, vs_], in_=St[:, vs_]), r=[L + "St"], w=[(L, "Stb", h)])
            yield
        P.add("act", lambda e: e.activation(out=st_[:, 4:8], in_=st_[:, 0:4], func=AF.Sqrt, scale=1.0 / 512, bias=C.epsb[:, 0:1]),
              r=[(L, "sso", slot, h) for h in range(4)] + ["epsb"], w=[(L, "st4", slot)])
        P.add("dve", lambda e: e.reciprocal(out=st_[:, 8:12], in_=st_[:, 4:8]), r=[(L, "st4", slot)], w=[(L, "st8", slot)])
        for h in range(4):
            vs_ = slice(h * 512, (h + 1) * 512)
            P.add("dve", lambda e, vs_=vs_, h=h: e.scalar_tensor_tensor(
                out=opart[:, vs_], in0=opart[:, vs_], scalar=st_[:, 8 + h:9 + h], in1=ogtab[:], op0=ALU.mult, op1=ALU.mult),
                r=[(L, "opart", h), (L, "st8", slot), L + "ogtab"], w=[(L, "opart", h)])
            P.add("pool", lambda e, vs_=vs_: e.tensor_tensor(out=yb_[:, vs_], in0=opart[:, vs_], in1=sg[:, vs_], op=ALU.mult),
                  r=[(L, "opart", h), (L, "sg", slot, h)], w=[(L, "yb", slot, h)])
            yield

    def back2(ti):
        slot = ti % 2
        rows = slice(ti * 128, (ti + 1) * 128)
        yield from emit_out_proj(P, C, L, li, ti, yb2[slot], [(L, "yb", slot, h) for h in range(4)], yT, Wout, Wout_bufs, C.xt[slot],
                                 ("xt", slot), xn, x_dst, rows, gen=True)

    interleave([front2(0)])
    for ti in range(NT):
        gens = [back2(ti)]
        if ti + 1 < NT:
            gens.append(front2(ti + 1))
        interleave(gens)


BAND = 132.0


def layer_C(nc, P, C, es0, x_src, x_dst, li, NT, W):
    import contextlib
    import math
    L = "C%d" % li
    S = NT * 128
    lambda_init = 0.8 - 0.6 * math.exp(-0.3 * li)
    w_in = W["c_w_in"][0]
    wv = w_in.rearrange("(kc p) f -> p kc f", p=128)

    st_qT = C.dram(L + "st_qT", [16, 128, S], BF16)
    st_kT = C.dram(L + "st_kT", [16, 128, S], BF16)
    st_v = C.dram(L + "st_v", [S, DI], BF16)
    st_sz = C.dram(L + "st_sz", [S, DI], BF16)
    st_o = C.dram(L + "st_o", [S, DI], F32)

    def next_psA():
        i = C.pa[0] % 3
        C.pa[0] += 1
        return i

    P.add("sp", lambda e: e.dma_start(out=C.gtab[:], in_=W["norm_g"][li:li + 1, :].partition_broadcast(128)), w=["gtab"], dma="gtab")

    with contextlib.ExitStack() as es:
        def sb(name, shape, dt):
            return es.enter_context(nc.sbuf_tensor("sb_" + L + "a" + name, list(shape), dt))
        Wqk = sb("Wqk", [128, 8, 4096], BF16)
        gt = [sb("gt%d" % i, [128, 128], F32) for i in range(2)]
        qf2 = [sb("qf%d" % i, [128, DI], F32) for i in range(2)]
        sq = sb("sq", [128, 512], F32)
        ssq2 = [sb("ssq%d" % i, [128, 48], F32) for i in range(2)]
        qn2 = [sb("qn%d" % i, [128, DI], BF16) for i in range(2)]
        qTt2 = [sb("qTt%d" % i, [128, DI], BF16) for i in range(2)]
        for kc in range(8):
            P.add("pool", lambda e, kc=kc: e.dma_start(out=Wqk[:, kc, :], in_=wv[:, kc, 0:4096]), w=[(L, "Wqk", kc)], dma=(L, "W", kc % 2))
        P.add("sp", lambda e: e.dma_start(out=gt[0][:], in_=W["c_q_g"][0:1, :].partition_broadcast(128)), w=[(L, "gt", 0)], dma=L + "gt0")
        P.add("sp", lambda e: e.dma_start(out=gt[1][:], in_=W["c_k_g"][0:1, :].partition_broadcast(128)), w=[(L, "gt", 1)], dma=L + "gt1")

        def f1a(ti, qk, par):
            slot = ti % 2
            rows = slice(ti * 128, (ti + 1) * 128)
            if qk == 0:
                emit_norm_T(P, C, x_src[rows, :], L, C.gtab, slot)
                yield
            hT = C.hT[slot]
            qf, ssq = qf2[par], ssq2[par]
            for cb in range(4):
                pi = next_psA()
                pst = C.psA[pi]
                col = qk * 2048 + cb * 512
                for kc in range(8):
                    P.add("pe", lambda e, kc=kc, col=col, pst=pst: e.matmul(
                        pst[:], lhsT=hT[:, kc * 128:(kc + 1) * 128], rhs=Wqk[:, kc, col:col + 512],
                        start=(kc == 0), stop=(kc == 7)), r=[("hT", slot), (L, "Wqk", kc)], w=[("psA", pi)])
                P.add("act", lambda e, pst=pst, cb=cb: e.copy(out=qf[:, cb * 512:(cb + 1) * 512], in_=pst[:]),
                      r=[("psA", pi)], w=[(L, "qf", par, cb)])
                P.add("pool", lambda e, cb=cb: e.tensor_tensor(out=sq[:], in0=qf[:, cb * 512:(cb + 1) * 512],
                                                               in1=qf[:, cb * 512:(cb + 1) * 512], op=ALU.mult),
                      r=[(L, "qf", par, cb)], w=[L + "sq"])
                P.add("dve", lambda e, cb=cb: e.tensor_reduce(out=ssq[:, cb * 4:(cb + 1) * 4],
                                                               in_=sq[:].rearrange("p (g d) -> p g d", g=4), axis=AX.X, op=ALU.add),
                      r=[L + "sq"], w=[(L, "ssq", par, cb)])
                yield

        def b1a(ti, qk, par):
            qf, ssq, qn, qTt = qf2[par], ssq2[par], qn2[par], qTt2[par]
            P.add("act", lambda e: e.activation(out=ssq[:, 16:32], in_=ssq[:, 0:16], func=AF.Sqrt, scale=1.0 / 128, bias=C.epsb[:, 0:1]),
                  r=[(L, "ssq", par, cb) for cb in range(4)] + ["epsb"], w=[(L, "ssq16", par)])
            P.add("dve", lambda e: e.reciprocal(out=ssq[:, 32:48], in_=ssq[:, 16:32]), r=[(L, "ssq16", par)], w=[(L, "ssq32", par)])
            if qk == 0:
                P.add("dve", lambda e: e.tensor_scalar(out=ssq[:, 32:48], in0=ssq[:, 32:48], scalar1=128.0 ** -0.5, scalar2=None, op0=ALU.mult),
                      r=[(L, "ssq32", par)], w=[(L, "ssq32", par)])
            yield
            for g in range(16):
                P.add("dve", lambda e, g=g: e.scalar_tensor_tensor(
                    out=qn[:, g * 128:(g + 1) * 128], in0=qf[:, g * 128:(g + 1) * 128], scalar=ssq[:, 32 + g:33 + g],
                    in1=gt[qk][:], op0=ALU.mult, op1=ALU.mult),
                    r=[(L, "qf", par, g // 4), (L, "ssq32", par), (L, "gt", qk)], w=[(L, "qn", par, g // 4)])
                if g % 4 == 3:
                    yield
            for half in range(2):
                for g8 in range(8):
                    g = half * 8 + g8
                    P.add("pe", lambda e, g=g, g8=g8: e.transpose(out=C.psT[:, g8 * 128:(g8 + 1) * 128], in_=qn[:, g * 128:(g + 1) * 128],
                                                                  identity=C.ident[:]),
                          r=[(L, "qn", par, g // 4), "ident"], w=["psT"])
                P.add("act", lambda e, half=half: e.copy(out=qTt[:, half * 1024:(half + 1) * 1024], in_=C.psT[:]),
                      r=["psT"], w=[(L, "qTt", par, half)])
                yield
            dst = st_qT if qk == 0 else st_kT
            P.add("sp", lambda e, dst=dst: e.dma_start(out=dst.rearrange("g d s -> d g s")[:, :, ti * 128:(ti + 1) * 128],
                                                       in_=qTt[:].rearrange("p (g t) -> p g t", g=16)),
                  r=[(L, "qTt", par, 0), (L, "qTt", par, 1)], w=[(L, "d_qk", qk, ti)], dma=(L, "sq", qk))
            yield

        jobs = [(ti, qk) for ti in range(NT) for qk in range(2)]
        interleave([f1a(jobs[0][0], jobs[0][1], 0)])
        for n, (ti, qk) in enumerate(jobs):
            gens = [b1a(ti, qk, n % 2)]
            if n + 1 < len(jobs):
                gens.append(f1a(jobs[n + 1][0], jobs[n + 1][1], (n + 1) % 2))
            interleave(gens)
    P.barrier()

    with contextlib.ExitStack() as es:
        def sb(name, shape, dt):
            return es.enter_context(nc.sbuf_tensor("sb_" + L + "b" + name, list(shape), dt))
        Wvz = sb("Wvz", [128, 8, 4096], BF16)
        vbb = [sb("vb%d" % i, [128, DI], BF16) for i in range(2)]
        szbb = [sb("szb%d" % i, [128, DI], BF16) for i in range(2)]
        for kc in range(8):
            P.add("pool", lambda e, kc=kc: e.dma_start(out=Wvz[:, kc, :], in_=wv[:, kc, 4096:8192]), w=[(L, "Wvz", kc)], dma=(L, "W", kc % 2))

        def p1b(ti):
            slot = ti % 2
            vb, szb = vbb[slot], szbb[slot]
            rows = slice(ti * 128, (ti + 1) * 128)
            emit_norm_T(P, C, x_src[rows, :], L, C.gtab, slot)
            hT = C.hT[slot]
            for cb in range(8):
                pi = next_psA()
                pst = C.psA[pi]
                for kc in range(8):
                    P.add("pe", lambda e, kc=kc, cb=cb, pst=pst: e.matmul(
                        pst[:], lhsT=hT[:, kc * 128:(kc + 1) * 128], rhs=Wvz[:, kc, cb * 512:(cb + 1) * 512],
                        start=(kc == 0), stop=(kc == 7)), r=[("hT", slot), (L, "Wvz", kc)], w=[("psA", pi)])
                c0 = (cb % 4) * 512
                if cb < 4:
                    P.add("dve", lambda e, pst=pst, c0=c0: e.tensor_copy(out=vb[:, c0:c0 + 512], in_=pst[:]),
                          r=[("psA", pi)], w=[(L, "vb", slot, cb)])
                else:
                    P.add("act", lambda e, pst=pst, c0=c0: e.activation(out=szb[:, c0:c0 + 512], in_=pst[:], func=AF.Silu),
                          r=[("psA", pi)], w=[(L, "szb", slot, cb - 4)])
            P.add("sp", lambda e: e.dma_start(out=st_v[rows, :], in_=vb[:]), r=[(L, "vb", slot, i) for i in range(4)], w=[(L, "d_v", ti)], dma=(L, "sv", ti % 2))
            P.add("sp", lambda e: e.dma_start(out=st_sz[rows, :], in_=szb[:]), r=[(L, "szb", slot, i) for i in range(4)], w=[(L, "d_sz", ti)], dma=(L, "ssz", ti % 2))
        for ti in range(NT):
            p1b(ti)
    P.barrier()

    with contextlib.ExitStack() as es:
        def sb(name, shape, dt):
            return es.enter_context(nc.sbuf_tensor("sb_" + L + "c" + name, list(shape), dt))
        kS = sb("kS", [128, 2, S], BF16)
        vS = sb("vS", [128, NT, 257], BF16)
        tab = sb("tab", [128, 4, 512], F32)
        qS = [sb("qS%d" % i, [128, 2, 256], BF16) for i in range(2)]
        tmp = [sb("tmp%d" % i, [128, 512], F32) for i in range(4)]
        pT = [sb("pT%d" % i, [128, 512], BF16) for i in range(4)]
        sbank = [C.psA[0][:, :], C.psA[1][:, :], C.psO[:, 0:512], C.psO[:, 512:1024]]
        sbankb = [("psA", 0), ("psA", 1), ("psO", 0), ("psO", 1)]
        lam = sb("lam", [128, 4, 128], F32)
        lw = sb("lw", [128, 2, 128], F32)
        lv = sb("lv", [128, 8], F32)
        rr = sb("rr", [128, 8], F32)
        ot = [sb("ot%d" % i, [128, 256], F32) for i in range(2)]
        oo = [sb("oo%d" % i, [128, 256], F32) for i in range(2)]
        acc = [C.psA[2][:, 0:257], C.bankT[:, 0:257], C.bankY[:, 0:257], C.bankY[:, 512:769]]
        accb = [("psA", 2), "psT", ("psYT", 0), ("psYT", 1)]

        P.add("sp", lambda e: e.dma_start(out=lam[:].rearrange("p a b -> p (a b)"),
                                          in_=W["c_lam"][0:1].rearrange("o a b -> o (a b)").partition_broadcast(128)),
              w=[L + "lam"], dma=L + "lam")
        P.add("dve", lambda e: e.tensor_tensor(out=lw[:, 0, :], in0=lam[:, 0, :], in1=lam[:, 1, :], op=ALU.mult), r=[L + "lam"], w=[L + "lw0"])
        P.add("dve", lambda e: e.tensor_tensor(out=lw[:, 1, :], in0=lam[:, 2, :], in1=lam[:, 3, :], op=ALU.mult), r=[L + "lam"], w=[L + "lw1"])
        P.add("dve", lambda e: e.tensor_reduce(out=lv[:, 0:2], in_=lw[:], axis=AX.X, op=ALU.add), r=[L + "lw0", L + "lw1"], w=[L + "lv0"])
        P.add("act", lambda e: e.activation(out=lv[:, 2:4], in_=lv[:, 0:2], func=AF.Exp), r=[L + "lv0"], w=[L + "lv2"])
        P.add("dve", lambda e: e.tensor_tensor(out=lv[:, 4:5], in0=lv[:, 2:3], in1=lv[:, 3:4], op=ALU.subtract), r=[L + "lv2"], w=[L + "lv4"])
        P.add("dve", lambda e: e.tensor_scalar(out=lv[:, 5:6], in0=lv[:, 4:5], scalar1=-1.0, scalar2=-lambda_init, op0=ALU.mult, op1=ALU.add),
              r=[L + "lv4"], w=[L + "neglam"])
        P.add("pool", lambda e: e.memset(vS[:], 1.0), w=[L + "vS"])

        QT = 256
        NQ = S // QT
        cnt = [0]
        for h in range(8):
            slope = 2.0 ** (-(h + 1))
            dmax = BAND / slope
            for m in range(2):
                P.add("sp", lambda e, h=h, m=m: e.dma_start(out=kS[:, m, :], in_=st_kT[2 * h + m]), w=[(L, "kS", m)], dma=(L, "kS", m))
            NPART = max(4, NT // 8)
            for part in range(NPART):
                n0 = part * NT // NPART
                n1 = (part + 1) * NT // NPART
                if n1 > n0:
                    P.add("pool", lambda e, h=h, n0=n0, n1=n1: e.dma_start(
                        out=vS[:, n0:n1, 0:256], in_=st_v[n0 * 128:n1 * 128, h * 256:(h + 1) * 256].rearrange("(n p) c -> p n c", p=128)),
                        r=[], w=[L + "vS"], dma=(L, "vS", part % 4))
            P.add("sp", lambda e, h=h: e.dma_start(out=tab[:], in_=C.alibi_in[h].rearrange("a p j -> p a j")), w=[L + "tab"], dma=L + "tab")

            units = []
            for qi_ in range(NQ):
                q0 = qi_ * QT
                kbs = []
                for kb in range(NT):
                    k0 = kb * 128
                    if k0 >= q0 + QT:
                        dist = k0 - (q0 + QT - 1)
                    elif k0 + 127 < q0:
                        dist = q0 - (k0 + 127)
                    else:
                        dist = 0
                    if dist <= dmax:
                        kbs.append(kb)
                for ik, kb in enumerate(kbs):
                    units.append((qi_, kb, ik, len(kbs)))

            def front(un, h=h, slope=slope):
                qi_, kb, ik, nk = un
                q0 = qi_ * QT
                qslot = qi_ % 2
                qs_ = qS[qslot]
                if ik == 0:
                    P.add("sp", lambda e: e.dma_start(out=qs_[:], in_=st_qT[2 * h:2 * h + 2, :, q0:q0 + QT].rearrange("m d s -> d m s")),
                          w=[(L, "qS", qslot)], dma=(L, "qS", qslot))
                k0 = kb * 128
                delta = q0 - k0
                if delta >= 128:
                    tsel, cc = 0, -slope * delta
                elif delta <= -256:
                    tsel, cc = 1, slope * delta
                elif delta == 0:
                    tsel, cc = 2, 0.0
                else:
                    assert delta == -128
                    tsel, cc = 3, 0.0
                u = cnt[0] % 4
                cnt[0] += 1
                pst = sbank[u]
                for m in range(2):
                    P.add("pe", lambda e, m=m: e.matmul(
                        pst[:, m * 256:(m + 1) * 256], lhsT=kS[:, m, k0:k0 + 128], rhs=qs_[:, m, :], start=True, stop=True),
                        r=[(L, "kS", m), (L, "qS", qslot)], w=[sbankb[u]])
                tm = tmp[u]
                P.add("dve", lambda e: e.scalar_tensor_tensor(
                    out=tm[:], in0=pst, scalar=float(cc), in1=tab[:, tsel, :], op0=ALU.add, op1=ALU.add),
                    r=[sbankb[u], L + "tab"], w=[(L, "tmp", u)])
                pt = pT[u]
                P.add("act", lambda e: e.activation(out=pt[:], in_=tm[:], func=AF.Exp),
                      r=[(L, "tmp", u)], w=[(L, "pT", u)])
                return u

            def back(un, u, h=h):
                qi_, kb, ik, nk = un
                q0 = qi_ * QT
                pt = pT[u]
                for m in range(2):
                    for qh in range(2):
                        a = m * 2 + qh
                        P.add("pe", lambda e, a=a, m=m, qh=qh: e.matmul(
                            acc[a], lhsT=pt[:, m * 256 + qh * 128:m * 256 + (qh + 1) * 128], rhs=vS[:, kb, :],
                            start=(ik == 0), stop=(ik == nk - 1)),
                            r=[(L, "pT", u), L + "vS"], w=[accb[a]])
                if ik != nk - 1:
                    return
                for a in range(4):
                    P.add("dve", lambda e, a=a: e.reciprocal(out=rr[:, a:a + 1], in_=acc[a][:, 256:257]), r=[accb[a]], w=[(L, "rr", a)])
                for qh in range(2):
                    a0, a1 = qh, 2 + qh
                    P.add("dve", lambda e, a1=a1: e.tensor_tensor(out=rr[:, 4 + a1:5 + a1], in0=rr[:, a1:a1 + 1], in1=lv[:, 5:6], op=ALU.mult),
                          r=[(L, "rr", a1), L + "neglam"], w=[(L, "rl", a1)])
                    P.add("act", lambda e, a0=a0, qh=qh: e.activation(out=ot[qh][:], in_=acc[a0][:, 0:256], func=AF.Copy, scale=rr[:, a0:a0 + 1]),
                          r=[accb[a0], (L, "rr", a0)], w=[(L, "ot", qh)])
                    o_ = oo[qh]
                    P.add("dve", lambda e, a1=a1, o_=o_, qh=qh: e.scalar_tensor_tensor(
                        out=o_[:], in0=acc[a1][:, 0:256], scalar=rr[:, 4 + a1:5 + a1], in1=ot[qh][:], op0=ALU.mult, op1=ALU.add),
                        r=[accb[a1], (L, "rl", a1), (L, "ot", qh)], w=[(L, "oo", qh)])
                    r0 = q0 + qh * 128
                    P.add("sp", lambda e, o_=o_, r0=r0: e.dma_start(out=st_o[r0:r0 + 128, h * 256:(h + 1) * 256], in_=o_[:]),
                          r=[(L, "oo", qh)], w=[(L, "d_o", h, qi_, qh)], dma=(L, "so", qh))

            LAG = 3
            ubuf = {}
            for idx in range(len(units) + LAG):
                if idx < len(units):
                    ubuf[idx] = front(units[idx])
                if idx - LAG >= 0:
                    back(units[idx - LAG], ubuf.pop(idx - LAG))
    P.barrier()

    with contextlib.ExitStack() as es:
        def sb(name, shape, dt):
            return es.enter_context(nc.sbuf_tensor("sb_" + L + "d" + name, list(shape), dt))
        Wout = sb("Wout", [128, 16, D], BF16)
        ogtab = sb("ogtab", [128, 256], F32)
        of2 = [sb("of%d" % i, [128, DI], F32) for i in range(2)]
        sq3 = sb("sq", [128, DI], F32)
        szb32 = [sb("szb%d" % i, [128, DI], BF16) for i in range(2)]
        st2 = [sb("st%d" % i, [128, 32], F32) for i in range(2)]
        yb2 = [sb("yb%d" % i, [128, DI], BF16) for i in range(2)]
        yT = sb("yT", [128, DI], BF16)
        xn = sb("xn", [128, D], F32)
        wo = W["c_w_out"][0].rearrange("(kc p) f -> p kc f", p=128)
        for kc in range(0, 16, 4):
            P.add("pool", lambda e, kc=kc: e.dma_start(out=Wout[:, kc:kc + 4, :], in_=wo[:, kc:kc + 4, :]), w=[(L, "Wout", kc)], dma=(L, "W", 0))
        Wout_bufs = [(L, "Wout", kc) for kc in range(0, 16, 4)]
        P.add("sp", lambda e: e.dma_start(out=ogtab[:], in_=W["c_o_g"][0:1, :].partition_broadcast(128)), w=[L + "ogtab"], dma=L + "ogtab")

        def f3(ti):
            slot = ti % 2
            rows = slice(ti * 128, (ti + 1) * 128)
            xt = C.xt[slot]
            of, szb3, st, yb = of2[slot], szb32[slot], st2[slot], yb2[slot]
            P.add("sp", lambda e: e.dma_start(out=xt[:], in_=x_src[rows, :]), w=[("xt", slot)], dma=("xt", slot))
            P.add("sp", lambda e: e.dma_start(out=of[:], in_=st_o[rows, :]), w=[(L, "of", slot)], dma=(L, "lof", slot))
            P.add("sp", lambda e: e.dma_start(out=szb3[:], in_=st_sz[rows, :]), w=[(L, "szb3", slot)], dma=(L, "lsz", slot))
            yield
            P.add("pool", lambda e: e.tensor_tensor(out=sq3[:], in0=of[:], in1=of[:], op=ALU.mult), r=[(L, "of", slot)], w=[L + "sq3"])
            P.add("dve", lambda e: e.tensor_reduce(out=st[:, 0:8], in_=sq3[:].rearrange("p (g d) -> p g d", g=8), axis=AX.X, op=ALU.add),
                  r=[L + "sq3"], w=[(L, "st0", slot)])
            P.add("act", lambda e: e.activation(out=st[:, 8:16], in_=st[:, 0:8], func=AF.Sqrt, scale=1.0 / 256, bias=C.epsb[:, 0:1]),
                  r=[(L, "st0", slot), "epsb"], w=[(L, "st8", slot)])
            P.add("dve", lambda e: e.reciprocal(out=st[:, 16:24], in_=st[:, 8:16]), r=[(L, "st8", slot)], w=[(L, "st16", slot)])
            P.add("dve", lambda e: e.tensor_scalar(out=st[:, 16:24], in0=st[:, 16:24], scalar1=1.0 - lambda_init, scalar2=None, op0=ALU.mult),
                  r=[(L, "st16", slot)], w=[(L, "st16", slot)])
            yield
            for g in range(8):
                P.add("dve", lambda e, g=g: e.scalar_tensor_tensor(
                    out=of[:, g * 256:(g + 1) * 256], in0=of[:, g * 256:(g + 1) * 256], scalar=st[:, 16 + g:17 + g], in1=ogtab[:],
                    op0=ALU.mult, op1=ALU.mult), r=[(L, "of", slot), (L, "st16", slot), L + "ogtab", L + "sq3"], w=[(L, "of", slot)])
                if g % 4 == 3:
                    yield
            P.add("pool", lambda e: e.tensor_tensor(out=yb[:], in0=of[:], in1=szb3[:], op=ALU.mult),
                  r=[(L, "of", slot), (L, "szb3", slot)], w=[(L, "yb", slot)])
            yield

        def b3(ti):
            slot = ti % 2
            rows = slice(ti * 128, (ti + 1) * 128)
            yield from emit_out_proj(P, C, L, li, ti, yb2[slot], [(L, "yb", slot)] * 4, yT, Wout, Wout_bufs, C.xt[slot], ("xt", slot),
                                     xn, x_dst, rows, gen=True)

        interleave([f3(0)])
        for ti in range(NT):
            gens = [b3(ti)]
            if ti + 1 < NT:
                gens.append(f3(ti + 1))
            interleave(gens)


def emit_out_proj(P, C, L, li, ti, yb, yb_bufs, yT, Wout, Wout_bufs, xt, xtb, xn, x_dst, rows, gen=False):
    g = _emit_out_proj(P, C, L, li, ti, yb, yb_bufs, yT, Wout, Wout_bufs, xt, xtb, xn, x_dst, rows)
    if gen:
        return g
    for _ in g:
        pass


def _emit_out_proj(P, C, L, li, ti, yb, yb_bufs, yT, Wout, Wout_bufs, xt, xtb, xn, x_dst, rows):
    for kc in range(16):
        P.add("pe", lambda e, kc=kc: e.transpose(out=C.psYT[:, kc * 128:(kc + 1) * 128], in_=yb[:, kc * 128:(kc + 1) * 128],
                                                  identity=C.ident[:]),
              r=[yb_bufs[kc // 4], "ident"], w=["psYT"])
    P.add("act", lambda e: e.copy(out=yT[:], in_=C.psYT[:]), r=["psYT"], w=[L + "yT"])
    yield
    for nb in range(2):
        for kc in range(16):
            P.add("pe", lambda e, kc=kc, nb=nb: e.matmul(
                C.psO[:, nb * 512:(nb + 1) * 512], lhsT=yT[:, kc * 128:(kc + 1) * 128],
                rhs=Wout[:, kc, nb * 512:(nb + 1) * 512], start=(kc == 0), stop=(kc == 15)),
                r=[L + "yT", Wout_bufs[kc // 4]], w=[("psO", nb)])
        yield
    P.add("dve", lambda e: e.tensor_tensor(out=xn[:], in0=C.psO[:], in1=xt[:], op=ALU.add),
          r=[("psO", 0), ("psO", 1), xtb], w=[L + "xn"])
    P.add("sp", lambda e: e.dma_start(out=x_dst[rows, :], in_=xn[:]),
          r=[L + "xn"], w=[("xdram", li, ti)], dma=("xst", li, ti % 2))
    yield


def host_consts():
    ident = np.eye(128, dtype=np.float32).astype(ml_dtypes.bfloat16)
    tp = np.arange(128)[:, None]
    t = np.arange(128)[None, :]
    f = lambda m: m.astype(np.float32)
    cm = np.stack([
        f(tp <= t) - f(tp <= 64), f(tp <= t), f(tp > t),
        f(tp >= t) - f(tp >= 63), f(tp >= t), f(tp < t),
    ]).astype(np.float32) * (-1.0 / 16.0)
    sidx = np.arange(128)[:, None]
    tidx = np.arange(128)[None, :]
    mf = (tidx >= sidx).astype(np.float32)
    mb = (tidx < sidx).astype(np.float32)
    gmask = np.stack([np.tile(mf, (1, 4)), np.tile(mb, (1, 4))]).astype(ml_dtypes.bfloat16)
    p = np.arange(128, dtype=np.float64)[:, None]
    j = np.tile(np.arange(256, dtype=np.float64), 2)[None, :]
    alibi = np.zeros((8, 4, 128, 512), np.float32)
    for h in range(8):
        slope = 2.0 ** (-(h + 1))
        alibi[h, 0] = -slope * (j - p)
        alibi[h, 1] = -slope * (p - j)
        alibi[h, 2] = -slope * np.abs(j - p)
        alibi[h, 3] = -slope * np.abs(j - p - 128)
    return {"ident": ident, "cm": cm, "gmask": gmask, "alibi": alibi}


def run_model(inputs, S, depth):
    nc = build_program(S, depth)
    xall = np.concatenate([np.asarray(inputs["x_prompt"], np.float32), np.asarray(inputs["x_sample"], np.float32)], axis=0)
    consts = host_consts()
    in_maps = []
    for c in range(NCORES):
        m = {"x": np.ascontiguousarray(xall[c]) if c < 3 else np.zeros_like(xall[0])}
        m.update(consts)
        for k in WSHAPES:
            m[k] = np.ascontiguousarray(np.asarray(inputs[k], np.float32))
        in_maps.append(m)
    res = run_bass_kernel_spmd(nc, in_maps, core_ids=list(range(NCORES)))
    yall = np.stack([res.results[c]["y"] for c in range(3)], axis=0)
    return np.ascontiguousarray(yall[0:2]), np.ascontiguousarray(yall[2:3])


def kernel(**inputs):
    return run_model(inputs, 16384, 4)
```

```python
import os
import numpy as np
import ml_dtypes
import concourse.bass as bass
import concourse.mybir as mybir
from concourse.bass_utils import run_bass_kernel_spmd

F32 = mybir.dt.float32
BF16 = mybir.dt.bfloat16
AF = mybir.ActivationFunctionType
ALU = mybir.AluOpType
AX = mybir.AxisListType

NCORES = 8
D = 1024
DI = 2048
EPS = 1e-6


class _Op:
    __slots__ = ("eng", "fn", "deps", "semkey", "val", "isdma")


def _is_psum(b):
    n = b[0] if isinstance(b, tuple) else b
    return isinstance(n, str) and n.startswith("ps")


class Prog:
    ENGS = ("pe", "act", "dve", "pool", "sp")

    def __init__(self):
        self.ops = {e: [] for e in self.ENGS}
        self.ncomp = {e: 0 for e in self.ENGS}
        self.bufs = {}
        self.dma_last = {}
        self.dma_cnt = {}
        self.all_dma_keys = []
        self.bar = {e: [] for e in self.ENGS}
        self.eng_epochs = set()
        self.EPOCH = int(os.environ.get("EPOCH", "32000"))

    def barrier(self):
        deps = [ops[-1] for e, ops in self.ops.items() if ops]
        for e in self.ENGS:
            cl = [o for o in reversed(self.ops[e]) if not o.isdma]
            if cl:
                deps.append(cl[0])
        deps += list(self.dma_last.values())
        for e in self.ENGS:
            self.bar[e] = list(deps)

    def add(self, eng, fn, r=(), w=(), dma=None):
        op = _Op()
        op.eng = eng
        op.fn = fn
        op.deps = []
        if self.bar[eng]:
            op.deps.extend(self.bar[eng])
            self.bar[eng] = []
        op.isdma = dma is not None
        for b in r:
            st = self.bufs.get(b)
            if st is not None and st[0] is not None:
                op.deps.append(st[0])
            if st is not None and _is_psum(b):
                for o in st[1]:
                    if o.eng != eng:
                        op.deps.append(o)
        for b in w:
            st = self.bufs.get(b)
            if st is not None:
                if st[0] is not None:
                    op.deps.append(st[0])
                op.deps.extend(st[1])
        for b in r:
            st = self.bufs.setdefault(b, [None, []])
            st[1].append(op)
        for b in w:
            self.bufs[b] = [op, []]
        if dma is not None:
            if isinstance(dma, str):
                dma = ("misc", len(dma) % 2)
            prev = self.dma_last.get(dma)
            if prev is not None:
                op.deps.append(prev)
            else:
                self.all_dma_keys.append(dma)
            self.dma_last[dma] = op
            self.dma_cnt[dma] = self.dma_cnt.get(dma, 0) + 16
            op.semkey = ("dma", dma)
            op.val = self.dma_cnt[dma]
        else:
            n = self.ncomp[eng]
            self.ncomp[eng] += 1
            op.semkey = ("eng", eng, n // self.EPOCH)
            op.val = n % self.EPOCH + 1
            self.eng_epochs.add(op.semkey)
        self.ops[eng].append(op)
        return op

    def emit(self, nc, tail_wait_keys=()):
        import contextlib
        with contextlib.ExitStack() as es:
            sems = {}
            for k in sorted(self.eng_epochs):
                sems[k] = es.enter_context(nc.semaphore("s_%s_%d" % (k[1], k[2])))
            for i, k in enumerate(self.all_dma_keys):
                sems[("dma", k)] = es.enter_context(nc.semaphore("d%d" % i))
            block = es.enter_context(nc.Block())

            def run(engname, eng):
                waited = {}
                for op in self.ops[engname]:
                    for d in op.deps:
                        if d.eng == "pe" and engname == "pe" and not d.isdma and not op.isdma:
                            continue
                        if waited.get(d.semkey, 0) >= d.val:
                            continue
                        eng.wait_ge(sems[d.semkey], d.val)
                        waited[d.semkey] = d.val
                    ins = op.fn(eng)
                    ins.then_inc(sems[op.semkey], 16 if op.isdma else 1)
                for k in self.all_dma_keys:
                    last = self.dma_last[k]
                    if last.eng == engname and waited.get(last.semkey, 0) < last.val:
                        eng.wait_ge(sems[last.semkey], last.val)

            @block.tensor
            def _(e):
                run("pe", e)

            @block.scalar
            def _(e):
                run("act", e)

            @block.vector
            def _(e):
                run("dve", e)

            @block.gpsimd
            def _(e):
                run("pool", e)

            @block.sync
            def _(e):
                run("sp", e)


def bcast_rows(ap_row, nparts):
    return ap_row.partition_broadcast(nparts)


class Ctx:
    pass


def emit_norm_T(P, C, x_src_ap, tag, gtab, slot):
    xt = C.xt[slot]
    xb = C.xb
    hT = C.hT[slot]
    P.add("sp", lambda e: e.dma_start(out=xt[:], in_=x_src_ap), w=[("xt", slot)], dma=("xt", slot))
    P.add("act", lambda e: e.activation(out=C.junk[:, 0:D], in_=xt[:], func=AF.Square, accum_out=C.ss[:, 0:1]),
          r=[("xt", slot)], w=["junk", "ss"])
    P.add("act", lambda e: e.activation(out=C.ss[:, 1:2], in_=C.ss[:, 0:1], func=AF.Sqrt, scale=1.0 / D, bias=C.epsb[:, 0:1]),
          r=["ss", "epsb"], w=["ss1"])
    P.add("dve", lambda e: e.reciprocal(out=C.ss[:, 2:3], in_=C.ss[:, 1:2]), r=["ss1"], w=["ss2"])
    P.add("dve", lambda e: e.scalar_tensor_tensor(out=xb[:], in0=xt[:], scalar=C.ss[:, 2:3], in1=gtab[:],
                                                  op0=ALU.mult, op1=ALU.mult),
          r=[("xt", slot), "ss2", "gtab"], w=["xb"])
    for kc in range(8):
        P.add("pe", lambda e, kc=kc: e.transpose(out=C.psT[:, kc * 128:(kc + 1) * 128], in_=xb[:, kc * 128:(kc + 1) * 128],
                                                  identity=C.ident[:]),
              r=["xb", "ident"], w=["psT"])
    P.add("act", lambda e: e.copy(out=hT[:], in_=C.psT[:]), r=["psT"], w=[("hT", slot)])


def build_program(SL, depth):
    NT = SL // 128
    nc = bass.Bass("TRN2", target_bir_lowering=False)
    P = Prog()
    C = Ctx()
    C.NT = NT

    def din(name, shape, dt=F32):
        return nc.dram_tensor(name, list(shape), dt, kind="ExternalInput").ap()

    x_in = din("x", [SL, D])
    W = {}
    for k, shp in WSHAPES.items():
        W[k] = din(k, shp)
    ident_in = din("ident", [128, 128], BF16)
    C.cm_in = din("cm", [6, 128, 128])
    C.gmask_in = din("gmask", [2, 128, 512], BF16)
    C.alibi_in = din("alibi", [8, 4, 128, 512])
    y_out = nc.dram_tensor("y", [SL, D], F32, kind="ExternalOutput").ap()
    xs = [nc.dram_tensor("xs%d" % i, [SL, D], F32, kind="Internal").ap() for i in range(2)]
    C.dram = lambda name, shape, dt: nc.dram_tensor(name, list(shape), dt, kind="Internal").ap()

    import contextlib
    with contextlib.ExitStack() as es:
        def sb(name, shape, dt):
            return es.enter_context(nc.sbuf_tensor("sb_" + name, list(shape), dt))

        def ps(name, shape, dt):
            return es.enter_context(nc.psum_tensor("ps_" + name, list(shape), dt))

        C.ident = sb("ident", [128, 128], BF16)
        C.xt = [sb("xt%d" % i, [128, D], F32) for i in range(2)]
        C.xb = sb("xb", [128, D], BF16)
        C.hT = [sb("hT%d" % i, [128, D], BF16) for i in range(2)]
        C.junk = sb("junk", [128, D], BF16)
        C.ss = sb("ss", [128, 16], F32)
        C.epsb = sb("epsb", [128, 1], F32)
        C.oneb = sb("oneb", [128, 1], F32)
        C.gtab = sb("gtab", [128, D], F32)
        C.psA = [ps("psA%d" % i, [128, 512], F32) for i in range(3)]
        C.bankT = ps("bankT", [128, 512], F32)
        C.psT = C.bankT.bitcast(BF16)
        C.bankY = ps("bankY", [128, 1024], F32)
        C.psYT = C.bankY.bitcast(BF16)
        C.psO = ps("psO", [128, D], F32)
        C.nc = nc
        C.pa = [0]

        P.add("sp", lambda e: e.dma_start(out=C.ident[:], in_=ident_in), w=["ident"], dma="ident")
        P.add("pool", lambda e: e.memset(C.epsb[:], EPS), w=["epsb"])
        P.add("pool", lambda e: e.memset(C.oneb[:], 1.0), w=["oneb"])

        layer_kinds = [0, 1, 2, 0][:depth]
        cur = x_in
        for li, kind in enumerate(layer_kinds):
            dst = y_out if li == depth - 1 else xs[li % 2]
            with contextlib.ExitStack() as les:
                P.barrier()
                if kind == 0:
                    layer_A(nc, P, C, les, cur, dst, li, li // 3, NT, W["norm_g"], W["a_w_in"], W["a_v_g"], W["a_w_s"],
                            W["a_b_s"], W["a_w_out"])
                elif kind == 1:
                    layer_B(nc, P, C, les, cur, dst, li, NT, W)
                else:
                    layer_C(nc, P, C, les, cur, dst, li, NT, W)
            cur = dst
        P.emit(nc)
    return nc


WSHAPES = {
    "norm_g": [4, D], "a_w_in": [2, D, 3 * DI], "a_v_g": [2, DI], "a_w_s": [2, 8, 128, 128], "a_b_s": [2, 8, 128],
    "a_w_out": [2, DI, D], "b_w_in": [1, D, 5152], "b_w_gate": [1, 2, 16, 512], "b_gate_bias": [1, 2, 512],
    "b_o_g": [1, 512], "b_w_out": [1, DI, D], "c_w_in": [1, D, 4 * DI], "c_q_g": [1, 128], "c_k_g": [1, 128],
    "c_lam": [1, 4, 128], "c_o_g": [1, 256], "c_w_out": [1, DI, D],
}


def interleave(gens):
    gens = list(gens)
    while gens:
        nxt = []
        for g in gens:
            try:
                next(g)
                nxt.append(g)
            except StopIteration:
                pass
        gens = nxt


def layer_A(nc, P, C, es, x_src, x_dst, li, j, NT, norm_g, a_w_in, a_v_g, a_w_s, a_b_s, a_w_out):
    L = "A%d" % li

    def sb(name, shape, dt):
        return es.enter_context(nc.sbuf_tensor("sb_" + L + name, list(shape), dt))

    Win = sb("Win", [128, 8, 3 * DI], BF16)
    Wout = sb("Wout", [128, 16, D], BF16)
    t1 = [sb("t1%d" % i, [128, 512], F32) for i in range(2)]
    yb = sb("yb", [128, DI], BF16)
    wsq = sb("wsq", [128, 8, 128], F32)
    wsqb = yb[:, 0:1024].rearrange("p (g q) -> p g q", g=8)
    wsT = sb("wsT", [128, 8, 128], BF16)
    bs = sb("bs", [128, 8], F32)
    vgtab = sb("vgtab", [128, DI], F32)
    u = [sb("u%d" % i, [128, DI], BF16) for i in range(2)]
    sz = [sb("sz%d" % i, [128, DI], BF16) for i in range(2)]
    v = sb("v", [128, DI], F32)
    vs = sb("vs", [128, DI], BF16)
    yT = sb("yT", [128, DI], BF16)
    xn = sb("xn", [128, D], F32)
    st = sb("st", [128, 16], F32)

    wv = a_w_in[j].rearrange("(kc p) f -> p kc f", p=128)
    for kc in range(8):
        P.add("pool", lambda e, kc=kc: e.dma_start(out=Win[:, kc, :], in_=wv[:, kc, :]),
              w=[(L, "Win", kc)], dma=(L, "Win", kc % 2))
    wo = a_w_out[j].rearrange("(kc p) f -> p kc f", p=128)
    for kc in range(0, 16, 4):
        P.add("pool", lambda e, kc=kc: e.dma_start(out=Wout[:, kc:kc + 4, :], in_=wo[:, kc:kc + 4, :]),
              w=[(L, "Wout", kc)], dma=(L, "Wout", (kc // 4) % 2))
    P.add("sp", lambda e: e.dma_start(out=wsq[:], in_=a_w_s[j].rearrange("g q p -> q g p")), w=[L + "wsq"], dma=L + "wsq")
    P.add("sp", lambda e: e.dma_start(out=bs[:], in_=a_b_s[j].rearrange("g q -> q g"), allow_slow_non_contiguous=True),
          w=[L + "bs"], dma=L + "bs")
    P.add("sp", lambda e: e.dma_start(out=C.gtab[:], in_=norm_g[li:li + 1, :].partition_broadcast(128)),
          w=["gtab"], dma="gtab")
    P.add("sp", lambda e: e.dma_start(out=vgtab[:], in_=a_v_g[j:j + 1, :].partition_broadcast(128)),
          w=[L + "vgtab"], dma=L + "vgtab")
    P.add("dve", lambda e: e.tensor_copy(out=wsqb, in_=wsq[:]), r=[L + "wsq"], w=[(L, "yb", 0), (L, "yb", 1)])
    for g in range(8):
        P.add("pe", lambda e, g=g: e.transpose(out=C.psT[:, g * 128:(g + 1) * 128], in_=wsqb[:, g, :], identity=C.ident[:]),
              r=[(L, "yb", 0), (L, "yb", 1), "ident"], w=["psT"])
    P.add("act", lambda e: e.copy(out=wsT[:].rearrange("p g q -> p (g q)"), in_=C.psT[:]), r=["psT"], w=[L + "wsT"])

    Win_bufs = [(L, "Win", kc) for kc in range(8)]
    Wout_bufs = [(L, "Wout", kc) for kc in range(0, 16, 4)]

    def next_psA():
        i = C.pa[0] % 3
        C.pa[0] += 1
        return i

    def front(ti):
        slot = ti % 2
        rows = slice(ti * 128, (ti + 1) * 128)
        emit_norm_T(P, C, x_src[rows, :], L, C.gtab, slot)
        hT = C.hT[slot]
        u_, sz_ = u[slot], sz[slot]
        yield
        for cb in range(12):
            pi = next_psA()
            pst = C.psA[pi]
            for kc in range(8):
                P.add("pe", lambda e, kc=kc, cb=cb, pst=pst: e.matmul(
                    pst[:], lhsT=hT[:, kc * 128:(kc + 1) * 128], rhs=Win[:, kc, cb * 512:(cb + 1) * 512],
                    start=(kc == 0), stop=(kc == 7)),
                    r=[("hT", slot), Win_bufs[kc]], w=[("psA", pi)])
            c0 = (cb % 4) * 512
            if cb < 4:
                P.add("act", lambda e, pst=pst, c0=c0: e.copy(out=u_[:, c0:c0 + 512], in_=pst[:]),
                      r=[("psA", pi)], w=[(L, "u", slot, cb)])
            elif cb < 8:
                P.add("dve", lambda e, pst=pst, c0=c0: e.tensor_copy(out=v[:, c0:c0 + 512], in_=pst[:]),
                      r=[("psA", pi)], w=[(L, "v", cb - 4)])
                P.add("act", lambda e, pst=pst, cb=cb: e.activation(out=C.junk[:, 0:512], in_=pst[:], func=AF.Square,
                                                                      accum_out=st[:, cb - 4:cb - 3]),
                      r=[("psA", pi)], w=["junk", (L, "ssv", cb - 4)])
            else:
                P.add("act", lambda e, pst=pst, c0=c0: e.activation(out=sz_[:, c0:c0 + 512], in_=pst[:], func=AF.Silu),
                      r=[("psA", pi)], w=[(L, "sz", slot, cb - 8)])
            yield

    def back(ti):
        slot = ti % 2
        rows = slice(ti * 128, (ti + 1) * 128)
        xt = C.xt[slot]
        u_, sz_ = u[slot], sz[slot]
        P.add("dve", lambda e: e.tensor_reduce(out=st[:, 4:5], in_=st[:, 0:4], axis=AX.X, op=ALU.add),
              r=[(L, "ssv", i) for i in range(4)], w=[L + "st4"])
        P.add("act", lambda e: e.activation(out=st[:, 5:6], in_=st[:, 4:5], func=AF.Sqrt, scale=1.0 / DI, bias=C.epsb[:, 0:1]),
              r=[L + "st4", "epsb"], w=[L + "st5"])
        P.add("dve", lambda e: e.reciprocal(out=st[:, 6:7], in_=st[:, 5:6]), r=[L + "st5"], w=[L + "st6"])
        for b in range(4):
            P.add("dve", lambda e, b=b: e.tensor_scalar(out=vs[:, b * 512:(b + 1) * 512], in0=v[:, b * 512:(b + 1) * 512],
                                                         scalar1=st[:, 6:7], scalar2=None, op0=ALU.mult),
                  r=[(L, "v", b), L + "st6"], w=[(L, "vs", b)])
        yield
        for b in range(4):
            pi = next_psA()
            pst = C.psA[pi]
            for gg in range(2):
                g = 2 * b + gg
                P.add("pe", lambda e, g=g, gg=gg, pst=pst: e.matmul(
                    pst[:, gg * 256:(gg + 1) * 256], lhsT=wsT[:, g, :], rhs=vs[:, g * 256:(g + 1) * 256],
                    start=True, stop=True),
                    r=[L + "wsT", (L, "vs", b)], w=[("psA", pi)])
            tt = t1[b % 2]
            P.add("dve", lambda e, pst=pst, tt=tt, b=b: e.tensor_tensor(out=tt[:], in0=pst[:], in1=vgtab[:, b * 512:(b + 1) * 512],
                                                                        op=ALU.mult),
                  r=[("psA", pi), L + "vgtab"], w=[(L, "t1", b % 2)])
            for gg in range(2):
                g = 2 * b + gg
                P.add("dve", lambda e, tt=tt, g=g, gg=gg: e.scalar_tensor_tensor(
                    out=tt[:, gg * 256:(gg + 1) * 256], in0=tt[:, gg * 256:(gg + 1) * 256], scalar=bs[:, g:g + 1],
                    in1=u_[:, g * 256:(g + 1) * 256], op0=ALU.add, op1=ALU.mult),
                    r=[(L, "t1", b % 2), L + "bs", (L, "u", slot, b)], w=[(L, "t1", b % 2)])
            P.add("pool", lambda e, tt=tt, b=b: e.tensor_tensor(out=yb[:, b * 512:(b + 1) * 512], in0=tt[:],
                                                                in1=sz_[:, b * 512:(b + 1) * 512], op=ALU.mult),
                  r=[(L, "t1", b % 2), (L, "sz", slot, b)], w=[(L, "yb", b // 2)])
            yield
        for kc in range(16):
            P.add("pe", lambda e, kc=kc: e.transpose(out=C.psYT[:, kc * 128:(kc + 1) * 128], in_=yb[:, kc * 128:(kc + 1) * 128],
                                                      identity=C.ident[:]),
                  r=[(L, "yb", kc // 8), "ident"], w=["psYT"])
        P.add("act", lambda e: e.copy(out=yT[:], in_=C.psYT[:]), r=["psYT"], w=[L + "yT"])
        yield
        for nb in range(2):
            for kc in range(16):
                P.add("pe", lambda e, kc=kc, nb=nb: e.matmul(
                    C.psO[:, nb * 512:(nb + 1) * 512], lhsT=yT[:, kc * 128:(kc + 1) * 128],
                    rhs=Wout[:, kc, nb * 512:(nb + 1) * 512], start=(kc == 0), stop=(kc == 15)),
                    r=[L + "yT", Wout_bufs[kc // 4]], w=[("psO", nb)])
            yield
        P.add("dve", lambda e: e.tensor_tensor(out=xn[:], in0=C.psO[:], in1=xt[:], op=ALU.add),
              r=[("psO", 0), ("psO", 1), ("xt", slot)], w=[L + "xn"])
        P.add("sp", lambda e, rows=rows: e.dma_start(out=x_dst[rows, :], in_=xn[:]),
              r=[L + "xn"], w=[("xdram", li, ti)], dma=("xst", li, ti % 2))
        yield

    interleave([front(0)])
    for ti in range(NT):
        gens = [back(ti)]
        if ti + 1 < NT:
            gens.append(front(ti + 1))
        interleave(gens)


def layer_B(nc, P, C, es, x_src, x_dst, li, NT, W):
    L = "B%d" % li
    w_in = W["b_w_in"][0]

    def sb(name, shape, dt):
        return es.enter_context(nc.sbuf_tensor("sb_" + L + name, list(shape), dt))

    Wqk = sb("Wqk", [128, 8, 1024], BF16)
    Wvg = sb("Wvg", [128, 8, 4096], BF16)
    Waf = sb("Waf", [128, 8, 16], BF16)
    Wab = sb("Wab", [128, 8, 16], BF16)
    Wout = Wvg[:, 0:4, :].rearrange("p a (b f) -> p (a b) f", b=4)
    wg = [sb("wg%d" % d, [32, 512], BF16) for d in range(2)]
    aT = [sb("aT%d" % d, [32, 128], BF16) for d in range(2)]
    cm = sb("cm", [128, 6, 128], F32)
    gmask = sb("gmask", [128, 2, 512], BF16)
    ogtab = sb("ogtab", [128, 512], F32)
    qT2 = [sb("qT%d" % i, [128, 512], F32) for i in range(2)]
    kT2 = [sb("kT%d" % i, [128, 512], F32) for i in range(2)]
    ex = sb("ex", [128, 512], F32)
    sp2 = [[sb("sp%d_%d" % (i, d), [128, 512], F32) for d in range(2)] for i in range(2)]
    E = [sb("E%d" % i, [128, 512], F32) for i in range(2)]
    qq = [sb("qq%d" % d, [128, 512], BF16) for d in range(2)]
    kk = [sb("kk%d" % d, [128, 512], BF16) for d in range(2)]
    qi = [sb("qi%d" % d, [128, 512], BF16) for d in range(2)]
    ki = [sb("ki%d" % d, [128, 512], BF16) for d in range(2)]
    kiT = sb("kiT", [128, 1024], BF16)
    scT = [sb("scT%d" % d, [128, 512], BF16) for d in range(2)]
    vb2 = [sb("vb%d" % i, [128, DI], BF16) for i in range(2)]
    sg2 = [sb("sg%d" % i, [128, DI], BF16) for i in range(2)]
    opart = sb("opart", [128, DI], F32)
    dec = sb("dec", [128, 8], F32)
    St = sb("St", [128, DI], F32)
    Stb = sb("Stb", [128, DI], BF16)
    yb = sb("yb", [128, DI], BF16)
    yT = sb("yT", [128, DI], BF16)
    xn = sb("xn", [128, D], F32)
    st = sb("st", [128, 16], F32)
    qi2 = sb("qi2", [128, 512], BF16)
    kiT2 = sb("kiT2", [128, 512], BF16)
    dec2 = sb("dec2", [128, 4], F32)

    st_o = C.dram(L + "st_o", [NT * 128, DI], F32)
    st_v = C.dram(L + "st_v", [NT * 128, DI], BF16)
    st_sg = C.dram(L + "st_sg", [NT * 128, DI], BF16)
    st_qi = C.dram(L + "st_qi", [NT * 128, 512], BF16)
    st_ki = C.dram(L + "st_ki", [NT * 128, 512], BF16)
    st_df = C.dram(L + "st_df", [NT * 128, 4], F32)

    def next_psA():
        i = C.pa[0] % 3
        C.pa[0] += 1
        return i

    wv = w_in.rearrange("(kc p) f -> p kc f", p=128)
    for kc in range(8):
        P.add("pool", lambda e, kc=kc: e.dma_start(out=Wqk[:, kc, :], in_=wv[:, kc, 0:1024]), w=[(L, "Wqk", kc)], dma=(L, "W", 0))
        P.add("pool", lambda e, kc=kc: e.dma_start(out=Wvg[:, kc, :], in_=wv[:, kc, 1024:5120]), w=[(L, "Wvg", kc)], dma=(L, "W", 1))
        P.add("pool", lambda e, kc=kc: e.dma_start(out=Waf[:, kc, :], in_=wv[:, kc, 5120:5136]), w=[(L, "Waf")], dma=(L, "W", 2))
        P.add("pool", lambda e, kc=kc: e.dma_start(out=Wab[:, kc, :], in_=wv[:, kc, 5136:5152]), w=[(L, "Wab")], dma=(L, "W", 3))
    for d in range(2):
        P.add("pool", lambda e, d=d: e.dma_start(out=wg[d][0:16, :], in_=W["b_w_gate"][0, d]), w=[(L, "wg", d)], dma=(L, "W", 2))
        P.add("pool", lambda e, d=d: e.dma_start(out=wg[d][16:17, :], in_=W["b_gate_bias"][0, d:d + 1, :]), w=[(L, "wgb", d)], dma=(L, "W", 3))
        P.add("pool", lambda e, d=d: e.memset(aT[d][:], 1.0), w=[(L, "aT", d)])
    P.add("sp", lambda e: e.dma_start(out=cm[:], in_=C.cm_in.rearrange("m a b -> a m b")), w=[L + "cm"], dma=L + "cm")
    P.add("sp", lambda e: e.dma_start(out=gmask[:], in_=C.gmask_in.rearrange("m a b -> a m b")), w=[L + "gmask"], dma=L + "gmask")
    P.add("sp", lambda e: e.dma_start(out=C.gtab[:], in_=W["norm_g"][li:li + 1, :].partition_broadcast(128)), w=["gtab"], dma="gtab")
    P.add("sp", lambda e: e.dma_start(out=ogtab[:], in_=W["b_o_g"][0:1, :].partition_broadcast(128)), w=[L + "ogtab"], dma=L + "ogtab")
    P.add("dve", lambda e: e.memset(St[:], 0.0), w=[L + "St"])
    P.add("pool", lambda e: e.memset(Stb[:], 0.0), w=[(L, "Stb", h) for h in range(4)])
    Wout_bufs = [(L, "Wout", kc) for kc in range(0, 16, 4)]

    def front1(ti, par):
        slot = par
        rows = slice(ti * 128, (ti + 1) * 128)
        emit_norm_T(P, C, x_src[rows, :], L, C.gtab, slot)
        hT = C.hT[slot]
        hTb = ("hT", slot)
        qT, kT, sp, vb, sg = qT2[par], kT2[par], sp2[par], vb2[par], sg2[par]
        yield
        for qk in range(2):
            pi = next_psA()
            pst = C.psA[pi]
            for h in range(4):
                blk = qk * 4 + h
                for kc in range(8):
                    P.add("pe", lambda e, kc=kc, blk=blk, h=h, pst=pst: e.matmul(
                        pst[:, h * 128:(h + 1) * 128], lhsT=Wqk[:, kc, blk * 128:(blk + 1) * 128], rhs=hT[:, kc * 128:(kc + 1) * 128],
                        start=(kc == 0), stop=(kc == 7)), r=[hTb, (L, "Wqk", kc)], w=[("psA", pi)])
            if qk == 0:
                P.add("act", lambda e, pst=pst: e.activation(out=qT[:], in_=pst[:], func=AF.Copy, scale=128.0 ** -0.5),
                      r=[("psA", pi)], w=[(L, "qT", par)])
            else:
                P.add("act", lambda e, pst=pst: e.copy(out=kT[:], in_=pst[:]), r=[("psA", pi)], w=[(L, "kT", par)])
            yield
        for d, Wa in enumerate((Waf, Wab)):
            pi = next_psA()
            pst = C.psA[pi]
            for kc in range(8):
                P.add("pe", lambda e, kc=kc, Wa=Wa, pst=pst: e.matmul(
                    pst[0:16, 0:128], lhsT=Wa[:, kc, :], rhs=hT[:, kc * 128:(kc + 1) * 128], start=(kc == 0), stop=(kc == 7)),
                    r=[hTb, (L, "Waf"), (L, "Wab")], w=[("psA", pi)])
            P.add("dve", lambda e, d=d, pst=pst: e.tensor_copy(out=aT[d][0:16, :], in_=pst[0:16, 0:128]),
                  r=[("psA", pi)], w=[(L, "aT", d)])
        for d in range(2):
            pi = next_psA()
            pst = C.psA[pi]
            P.add("pe", lambda e, d=d, pst=pst: e.matmul(pst[:], lhsT=aT[d][0:17, :], rhs=wg[d][0:17, :], start=True, stop=True),
                  r=[(L, "aT", d), (L, "wg", d), (L, "wgb", d)], w=[("psA", pi)])
            P.add("act", lambda e, pst=pst: e.activation(out=ex[:], in_=pst[:], func=AF.Exp, scale=-1.0),
                  r=[("psA", pi)], w=[L + "ex"])
            P.add("act", lambda e, d=d: e.activation(out=sp[d][:], in_=ex[:], func=AF.Ln, bias=C.oneb[:, 0:1]),
                  r=[L + "ex", "oneb"], w=[(L, "sp", par, d)])
            yield
        for cb in range(8):
            pi = next_psA()
            pst = C.psA[pi]
            for kc in range(8):
                P.add("pe", lambda e, kc=kc, cb=cb, pst=pst: e.matmul(
                    pst[:], lhsT=hT[:, kc * 128:(kc + 1) * 128], rhs=Wvg[:, kc, cb * 512:(cb + 1) * 512],
                    start=(kc == 0), stop=(kc == 7)), r=[hTb, (L, "Wvg", kc)], w=[("psA", pi)])
            c0 = (cb % 4) * 512
            if cb < 4:
                P.add("dve", lambda e, pst=pst, c0=c0: e.tensor_copy(out=vb[:, c0:c0 + 512], in_=pst[:]),
                      r=[("psA", pi)], w=[(L, "vb", par, cb)])
            else:
                P.add("act", lambda e, pst=pst, c0=c0: e.activation(out=sg[:, c0:c0 + 512], in_=pst[:], func=AF.Silu),
                      r=[("psA", pi)], w=[(L, "sg", par, cb - 4)])
            yield

    def back1(ti, par):
        rows = slice(ti * 128, (ti + 1) * 128)
        qT, kT, sp, vb, sg = qT2[par], kT2[par], sp2[par], vb2[par], sg2[par]
        for d in range(2):
            for m in range(3):
                pi = next_psA()
                pst = C.psA[pi]
                for h in range(4):
                    P.add("pe", lambda e, d=d, m=m, h=h, pst=pst: e.matmul(
                        pst[:, h * 128:(h + 1) * 128], lhsT=sp[d][:, h * 128:(h + 1) * 128], rhs=cm[:, d * 3 + m, :],
                        start=True, stop=True), r=[(L, "sp", par, d), L + "cm"], w=[("psA", pi)])
                if m == 0:
                    P.add("act", lambda e, pst=pst: e.activation(out=E[0][:], in_=pst[:], func=AF.Exp), r=[("psA", pi)], w=[(L, "E", 0)])
                    P.add("dve", lambda e, d=d: e.tensor_tensor(out=qq[d][:], in0=qT[:], in1=E[0][:], op=ALU.mult),
                          r=[(L, "qT", par), (L, "E", 0)], w=[(L, "qq", d)])
                    P.add("act", lambda e, pst=pst: e.activation(out=E[1][:], in_=pst[:], func=AF.Exp, scale=-1.0),
                          r=[("psA", pi)], w=[(L, "E", 1)])
                    P.add("dve", lambda e, d=d: e.tensor_tensor(out=kk[d][:], in0=kT[:], in1=E[1][:], op=ALU.mult),
                          r=[(L, "kT", par), (L, "E", 1)], w=[(L, "kk", d)])
                elif m == 1:
                    P.add("act", lambda e, pst=pst: e.activation(out=E[0][:], in_=pst[:], func=AF.Exp), r=[("psA", pi)], w=[(L, "E", 0)])
                    P.add("dve", lambda e, d=d: e.tensor_tensor(out=qi[d][:], in0=qT[:], in1=E[0][:], op=ALU.mult),
                          r=[(L, "qT", par), (L, "E", 0)], w=[(L, "qi", d)])
                    col = 127 if d == 0 else 0
                    P.add("dve", lambda e, d=d, col=col: e.tensor_copy(
                        out=dec[:, d * 4:(d + 1) * 4], in_=E[0][:].rearrange("p (h t) -> p h t", h=4)[:, :, col]),
                        r=[(L, "E", 0)], w=[(L, "dec", d)])
                else:
                    P.add("act", lambda e, pst=pst: e.activation(out=E[1][:], in_=pst[:], func=AF.Exp), r=[("psA", pi)], w=[(L, "E", 1)])
                    P.add("dve", lambda e, d=d: e.tensor_tensor(out=ki[d][:], in0=kT[:], in1=E[1][:], op=ALU.mult),
                          r=[(L, "kT", par), (L, "E", 1)], w=[(L, "ki", d)])
                yield
        for d in range(2):
            for h in range(4):
                blk = d * 4 + h
                P.add("pe", lambda e, d=d, h=h, blk=blk: e.transpose(out=C.psT[:, blk * 128:(blk + 1) * 128],
                                                                      in_=ki[d][:, h * 128:(h + 1) * 128], identity=C.ident[:]),
                      r=[(L, "ki", d), "ident"], w=["psT"])
        P.add("act", lambda e: e.copy(out=kiT[:], in_=C.psT[:]), r=["psT"], w=[L + "kiT"])
        yield
        for d in range(2):
            pi = next_psA()
            pst = C.psA[pi]
            for h in range(4):
                P.add("pe", lambda e, d=d, h=h, pst=pst: e.matmul(
                    pst[:, h * 128:(h + 1) * 128], lhsT=kk[d][:, h * 128:(h + 1) * 128], rhs=qq[d][:, h * 128:(h + 1) * 128],
                    start=True, stop=True), r=[(L, "kk", d), (L, "qq", d)], w=[("psA", pi)])
            P.add("dve", lambda e, d=d, pst=pst: e.tensor_tensor(out=scT[d][:], in0=pst[:], in1=gmask[:, d, :], op=ALU.mult),
                  r=[("psA", pi), L + "gmask"], w=[(L, "scT", d)])
            yield
        for h in range(4):
            pi = next_psA()
            pst = C.psA[pi]
            hs = slice(h * 128, (h + 1) * 128)
            vs_ = slice(h * 512, (h + 1) * 512)
            P.add("pe", lambda e, pst=pst, hs=hs, vs_=vs_: e.matmul(pst[:], lhsT=scT[0][:, hs], rhs=vb[:, vs_], start=True, stop=False),
                  r=[(L, "scT", 0), (L, "vb", par, h)], w=[("psA", pi)])
            P.add("pe", lambda e, pst=pst, hs=hs, vs_=vs_: e.matmul(pst[:], lhsT=scT[1][:, hs], rhs=vb[:, vs_], start=False, stop=False),
                  r=[(L, "scT", 1), (L, "vb", par, h)], w=[("psA", pi)])
            P.add("pe", lambda e, pst=pst, hs=hs, vs_=vs_: e.matmul(pst[:], lhsT=qi[1][:, hs], rhs=Stb[:, vs_], start=False, stop=True),
                  r=[(L, "qi", 1), (L, "Stb", h)], w=[("psA", pi)])
            P.add("act", lambda e, pst=pst, vs_=vs_: e.copy(out=opart[:, vs_], in_=pst[:]), r=[("psA", pi)], w=[(L, "opart", h)])
            yield
        for h in range(4):
            pi = next_psA()
            pst = C.psA[pi]
            hs2 = slice((4 + h) * 128, (5 + h) * 128)
            vs_ = slice(h * 512, (h + 1) * 512)
            P.add("pe", lambda e, pst=pst, hs2=hs2, vs_=vs_: e.matmul(pst[:], lhsT=kiT[:, hs2], rhs=vb[:, vs_], start=True, stop=True),
                  r=[L + "kiT", (L, "vb", par, h)], w=[("psA", pi)])
            P.add("dve", lambda e, pst=pst, vs_=vs_, h=h: e.scalar_tensor_tensor(
                out=St[:, vs_], in0=St[:, vs_], scalar=dec[:, 4 + h:5 + h], in1=pst[:], op0=ALU.mult, op1=ALU.add),
                r=[("psA", pi), (L, "dec", 1), L + "St"], w=[L + "St"])
            P.add("act", lambda e, vs_=vs_: e.copy(out=Stb[:, vs_], in_=St[:, vs_]), r=[L + "St"], w=[(L, "Stb", h)])
            yield
        k2 = ti % 2
        P.add("sp", lambda e: e.dma_start(out=st_o[rows, :], in_=opart[:]), r=[(L, "opart", h) for h in range(4)],
              w=[(L, "d_o", ti)], dma=(L, "s0", k2))
        P.add("sp", lambda e: e.dma_start(out=st_v[rows, :], in_=vb[:]), r=[(L, "vb", par, h) for h in range(4)],
              w=[(L, "d_v", ti)], dma=(L, "s1", k2))
        P.add("sp", lambda e: e.dma_start(out=st_sg[rows, :], in_=sg[:]), r=[(L, "sg", par, h) for h in range(4)],
              w=[(L, "d_sg", ti)], dma=(L, "s2", k2))
        P.add("sp", lambda e: e.dma_start(out=st_qi[rows, :], in_=qi[0][:]), r=[(L, "qi", 0)], w=[(L, "d_qi", ti)], dma=(L, "s3", k2))
        P.add("sp", lambda e: e.dma_start(out=st_ki[rows, :], in_=kiT[:, 0:512]), r=[L + "kiT"], w=[(L, "d_ki", ti)], dma=(L, "s4", k2))
        P.add("sp", lambda e: e.dma_start(out=st_df[rows, :], in_=dec[:, 0:4]), r=[(L, "dec", 0)], w=[(L, "d_df", ti)], dma=(L, "s5", k2))

        yield

    order = list(reversed(range(NT)))
    interleave([front1(order[0], 0)])
    for n, ti in enumerate(order):
        gens = [back1(ti, n % 2)]
        if n + 1 < NT:
            gens.append(front1(order[n + 1], (n + 1) % 2))
        interleave(gens)

    P.add("dve", lambda e: e.memset(St[:], 0.0), r=[L + "St"], w=[L + "St"])
    P.add("pool", lambda e: e.memset(Stb[:], 0.0), w=[(L, "Stb", h) for h in range(4)])
    wo = W["b_w_out"][0].rearrange("(kc p) f -> p kc f", p=128)
    for kc in range(0, 16, 4):
        P.add("pool", lambda e, kc=kc: e.dma_start(out=Wout[:, kc:kc + 4, :], in_=wo[:, kc:kc + 4, :]),
              w=[(L, "Wout", kc), (L, "Wvg", kc // 4)], dma=(L, "W", 0))

    yb2 = [yb, sb("yb2", [128, DI], BF16)]
    qi22 = [qi2, sb("qi2b", [128, 512], BF16)]
    kiT22 = [kiT2, sb("kiT2b", [128, 512], BF16)]
    dec22 = [dec2, sb("dec2b", [128, 4], F32)]
    st22 = [st, sb("stb", [128, 16], F32)]

    def front2(ti):
        slot = ti % 2
        rows = slice(ti * 128, (ti + 1) * 128)
        xt = C.xt[slot]
        vb, sg, yb_, qi2_, kiT2_, dec2_, st_ = vb2[slot], sg2[slot], yb2[slot], qi22[slot], kiT22[slot], dec22[slot], st22[slot]
        P.add("sp", lambda e: e.dma_start(out=xt[:], in_=x_src[rows, :]), w=[("xt", slot)], dma=("xt", slot))
        P.add("sp", lambda e: e.dma_start(out=opart[:], in_=st_o[rows, :]), r=[(L, "d_o", ti)], w=[(L, "opart", h) for h in range(4)], dma=(L, "l0"))
        P.add("sp", lambda e: e.dma_start(out=vb[:], in_=st_v[rows, :]), r=[(L, "d_v", ti)], w=[(L, "vb", slot, h) for h in range(4)], dma=(L, "l1", slot))
        P.add("sp", lambda e: e.dma_start(out=sg[:], in_=st_sg[rows, :]), r=[(L, "d_sg", ti)], w=[(L, "sg", slot, h) for h in range(4)], dma=(L, "l2", slot))
        P.add("sp", lambda e: e.dma_start(out=qi2_[:], in_=st_qi[rows, :]), r=[(L, "d_qi", ti)], w=[(L, "qi2", slot)], dma=(L, "l3", slot))
        P.add("sp", lambda e: e.dma_start(out=kiT2_[:], in_=st_ki[rows, :]), r=[(L, "d_ki", ti)], w=[(L, "kiT2", slot)], dma=(L, "l4", slot))
        P.add("sp", lambda e: e.dma_start(out=dec2_[:], in_=st_df[rows, :]), r=[(L, "d_df", ti)], w=[(L, "dec2", slot)], dma=(L, "l5", slot))
        yield
        for h in range(4):
            pi = next_psA()
            pst = C.psA[pi]
            hs = slice(h * 128, (h + 1) * 128)
            vs_ = slice(h * 512, (h + 1) * 512)
            P.add("pe", lambda e, pst=pst, hs=hs, vs_=vs_: e.matmul(pst[:], lhsT=qi2_[:, hs], rhs=Stb[:, vs_], start=True, stop=True),
                  r=[(L, "qi2", slot), (L, "Stb", h)], w=[("psA", pi)])
            P.add("dve", lambda e, pst=pst, vs_=vs_: e.tensor_tensor(out=opart[:, vs_], in0=pst[:], in1=opart[:, vs_], op=ALU.add),
                  r=[("psA", pi), (L, "opart", h)], w=[(L, "opart", h)])
            P.add("act", lambda e, vs_=vs_, h=h: e.activation(out=C.junk[:, 0:512], in_=opart[:, vs_], func=AF.Square,
                                                               accum_out=st_[:, h:h + 1]),
                  r=[(L, "opart", h)], w=["junk", (L, "sso", slot, h)])
            yield
        for h in range(4):
            pi = next_psA()
            pst = C.psA[pi]
            hs = slice(h * 128, (h + 1) * 128)
            vs_ = slice(h * 512, (h + 1) * 512)
            P.add("pe", lambda e, pst=pst, hs=hs, vs_=vs_: e.matmul(pst[:], lhsT=kiT2_[:, hs], rhs=vb[:, vs_], start=True, stop=True),
                  r=[(L, "kiT2", slot), (L, "vb", slot, h)], w=[("psA", pi)])
            P.add("dve", lambda e, pst=pst, vs_=vs_, h=h: e.scalar_tensor_tensor(
                out=St[:, vs_], in0=St[:, vs_], scalar=dec2_[:, h:h + 1], in1=pst[:], op0=ALU.mult, op1=ALU.add),
                r=[("psA", pi), (L, "dec2", slot), L + "St"], w=[L + "St"])
            P.add("act", lambda e, vs_=vs_: e.copy(out=Stb[:, vs_], in_=St[:, vs_]), r=[L + "St"], w=[(L, "Stb", h)])
            yield
        P.add("act", lambda e: e.activation(out=st_[:, 4:8], in_=st_[:, 0:4], func=AF.Sqrt, scale=1.0 / 512, bias=C.epsb[:, 0:1]),
              r=[(L, "sso", slot, h) for h in range(4)] + ["epsb"], w=[(L, "st4", slot)])
        P.add("dve", lambda e: e.reciprocal(out=st_[:, 8:12], in_=st_[:, 4:8]), r=[(L, "st4", slot)], w=[(L, "st8", slot)])
        for h in range(4):
            vs_ = slice(h * 512, (h + 1) * 512)
            P.add("dve", lambda e, vs_=vs_, h=h: e.scalar_tensor_tensor(
                out=opart[:, vs_], in0=opart[:, vs_], scalar=st_[:, 8 + h:9 + h], in1=ogtab[:], op0=ALU.mult, op1=ALU.mult),
                r=[(L, "opart", h), (L, "st8", slot), L + "ogtab"], w=[(L, "opart", h)])
            P.add("pool", lambda e, vs_=vs_: e.tensor_tensor(out=yb_[:, vs_], in0=opart[:, vs_], in1=sg[:, vs_], op=ALU.mult),
                  r=[(L, "opart", h), (L, "sg", slot, h)], w=[(L, "yb", slot, h)])
            yield

    def back2(ti):
        slot = ti % 2
        rows = slice(ti * 128, (ti + 1) * 128)
        yield from emit_out_proj(P, C, L, li, ti, yb2[slot], [(L, "yb", slot, h) for h in range(4)], yT, Wout, Wout_bufs, C.xt[slot],
                                 ("xt", slot), xn, x_dst, rows, gen=True)

    interleave([front2(0)])
    for ti in range(NT):
        gens = [back2(ti)]
        if ti + 1 < NT:
            gens.append(front2(ti + 1))
        interleave(gens)


BAND = 132.0


def layer_C(nc, P, C, es0, x_src, x_dst, li, NT, W):
    import contextlib
    import math
    L = "C%d" % li
    S = NT * 128
    lambda_init = 0.8 - 0.6 * math.exp(-0.3 * li)
    w_in = W["c_w_in"][0]
    wv = w_in.rearrange("(kc p) f -> p kc f", p=128)

    st_qT = C.dram(L + "st_qT", [16, 128, S], BF16)
    st_kT = C.dram(L + "st_kT", [16, 128, S], BF16)
    st_v = C.dram(L + "st_v", [S, DI], BF16)
    st_sz = C.dram(L + "st_sz", [S, DI], BF16)
    st_o = C.dram(L + "st_o", [S, DI], F32)

    def next_psA():
        i = C.pa[0] % 3
        C.pa[0] += 1
        return i

    P.add("sp", lambda e: e.dma_start(out=C.gtab[:], in_=W["norm_g"][li:li + 1, :].partition_broadcast(128)), w=["gtab"], dma="gtab")

    with contextlib.ExitStack() as es:
        def sb(name, shape, dt):
            return es.enter_context(nc.sbuf_tensor("sb_" + L + "a" + name, list(shape), dt))
        Wqk = sb("Wqk", [128, 8, 4096], BF16)
        gt = [sb("gt%d" % i, [128, 128], F32) for i in range(2)]
        qf2 = [sb("qf%d" % i, [128, DI], F32) for i in range(2)]
        sq = sb("sq", [128, 512], F32)
        ssq2 = [sb("ssq%d" % i, [128, 48], F32) for i in range(2)]
        qn2 = [sb("qn%d" % i, [128, DI], BF16) for i in range(2)]
        qTt2 = [sb("qTt%d" % i, [128, DI], BF16) for i in range(2)]
        for kc in range(8):
            P.add("pool", lambda e, kc=kc: e.dma_start(out=Wqk[:, kc, :], in_=wv[:, kc, 0:4096]), w=[(L, "Wqk", kc)], dma=(L, "W", kc % 2))
        P.add("sp", lambda e: e.dma_start(out=gt[0][:], in_=W["c_q_g"][0:1, :].partition_broadcast(128)), w=[(L, "gt", 0)], dma=L + "gt0")
        P.add("sp", lambda e: e.dma_start(out=gt[1][:], in_=W["c_k_g"][0:1, :].partition_broadcast(128)), w=[(L, "gt", 1)], dma=L + "gt1")

        def f1a(ti, qk, par):
            slot = ti % 2
            rows = slice(ti * 128, (ti + 1) * 128)
            if qk == 0:
                emit_norm_T(P, C, x_src[rows, :], L, C.gtab, slot)
                yield
            hT = C.hT[slot]
            qf, ssq = qf2[par], ssq2[par]
            for cb in range(4):
                pi = next_psA()
                pst = C.psA[pi]
                col = qk * 2048 + cb * 512
                for kc in range(8):
                    P.add("pe", lambda e, kc=kc, col=col, pst=pst: e.matmul(
                        pst[:], lhsT=hT[:, kc * 128:(kc + 1) * 128], rhs=Wqk[:, kc, col:col + 512],
                        start=(kc == 0), stop=(kc == 7)), r=[("hT", slot), (L, "Wqk", kc)], w=[("psA", pi)])
                P.add("act", lambda e, pst=pst, cb=cb: e.copy(out=qf[:, cb * 512:(cb + 1) * 512], in_=pst[:]),
                      r=[("psA", pi)], w=[(L, "qf", par, cb)])
                P.add("pool", lambda e, cb=cb: e.tensor_tensor(out=sq[:], in0=qf[:, cb * 512:(cb + 1) * 512],
                                                               in1=qf[:, cb * 512:(cb + 1) * 512], op=ALU.mult),
                      r=[(L, "qf", par, cb)], w=[L + "sq"])
                P.add("dve", lambda e, cb=cb: e.tensor_reduce(out=ssq[:, cb * 4:(cb + 1) * 4],
                                                               in_=sq[:].rearrange("p (g d) -> p g d", g=4), axis=AX.X, op=ALU.add),
                      r=[L + "sq"], w=[(L, "ssq", par, cb)])
                yield

        def b1a(ti, qk, par):
            qf, ssq, qn, qTt = qf2[par], ssq2[par], qn2[par], qTt2[par]
            P.add("act", lambda e: e.activation(out=ssq[:, 16:32], in_=ssq[:, 0:16], func=AF.Sqrt, scale=1.0 / 128, bias=C.epsb[:, 0:1]),
                  r=[(L, "ssq", par, cb) for cb in range(4)] + ["epsb"], w=[(L, "ssq16", par)])
            P.add("dve", lambda e: e.reciprocal(out=ssq[:, 32:48], in_=ssq[:, 16:32]), r=[(L, "ssq16", par)], w=[(L, "ssq32", par)])
            if qk == 0:
                P.add("dve", lambda e: e.tensor_scalar(out=ssq[:, 32:48], in0=ssq[:, 32:48], scalar1=128.0 ** -0.5, scalar2=None, op0=ALU.mult),
                      r=[(L, "ssq32", par)], w=[(L, "ssq32", par)])
            yield
            for g in range(16):
                P.add("dve", lambda e, g=g: e.scalar_tensor_tensor(
                    out=qn[:, g * 128:(g + 1) * 128], in0=qf[:, g * 128:(g + 1) * 128], scalar=ssq[:, 32 + g:33 + g],
                    in1=gt[qk][:], op0=ALU.mult, op1=ALU.mult),
                    r=[(L, "qf", par, g // 4), (L, "ssq32", par), (L, "gt", qk)], w=[(L, "qn", par, g // 4)])
                if g % 4 == 3:
                    yield
            for half in range(2):
                for g8 in range(8):
                    g = half * 8 + g8
                    P.add("pe", lambda e, g=g, g8=g8: e.transpose(out=C.psT[:, g8 * 128:(g8 + 1) * 128], in_=qn[:, g * 128:(g + 1) * 128],
                                                                  identity=C.ident[:]),
                          r=[(L, "qn", par, g // 4), "ident"], w=["psT"])
                P.add("act", lambda e, half=half: e.copy(out=qTt[:, half * 1024:(half + 1) * 1024], in_=C.psT[:]),
                      r=["psT"], w=[(L, "qTt", par, half)])
                yield
            dst = st_qT if qk == 0 else st_kT
            P.add("sp", lambda e, dst=dst: e.dma_start(out=dst.rearrange("g d s -> d g s")[:, :, ti * 128:(ti + 1) * 128],
                                                       in_=qTt[:].rearrange("p (g t) -> p g t", g=16)),
                  r=[(L, "qTt", par, 0), (L, "qTt", par, 1)], w=[(L, "d_qk", qk, ti)], dma=(L, "sq", qk))
            yield

        jobs = [(ti, qk) for ti in range(NT) for qk in range(2)]
        interleave([f1a(jobs[0][0], jobs[0][1], 0)])
        for n, (ti, qk) in enumerate(jobs):
            gens = [b1a(ti, qk, n % 2)]
            if n + 1 < len(jobs):
                gens.append(f1a(jobs[n + 1][0], jobs[n + 1][1], (n + 1) % 2))
            interleave(gens)
    P.barrier()

    with contextlib.ExitStack() as es:
        def sb(name, shape, dt):
            return es.enter_context(nc.sbuf_tensor("sb_" + L + "b" + name, list(shape), dt))
        Wvz = sb("Wvz", [128, 8, 4096], BF16)
        vbb = [sb("vb%d" % i, [128, DI], BF16) for i in range(2)]
        szbb = [sb("szb%d" % i, [128, DI], BF16) for i in range(2)]
        for kc in range(8):
            P.add("pool", lambda e, kc=kc: e.dma_start(out=Wvz[:, kc, :], in_=wv[:, kc, 4096:8192]), w=[(L, "Wvz", kc)], dma=(L, "W", kc % 2))

        def p1b(ti):
            slot = ti % 2
            vb, szb = vbb[slot], szbb[slot]
            rows = slice(ti * 128, (ti + 1) * 128)
            emit_norm_T(P, C, x_src[rows, :], L, C.gtab, slot)
            hT = C.hT[slot]
            for cb in range(8):
                pi = next_psA()
                pst = C.psA[pi]
                for kc in range(8):
                    P.add("pe", lambda e, kc=kc, cb=cb, pst=pst: e.matmul(
                        pst[:], lhsT=hT[:, kc * 128:(kc + 1) * 128], rhs=Wvz[:, kc, cb * 512:(cb + 1) * 512],
                        start=(kc == 0), stop=(kc == 7)), r=[("hT", slot), (L, "Wvz", kc)], w=[("psA", pi)])
                c0 = (cb % 4) * 512
                if cb < 4:
                    P.add("dve", lambda e, pst=pst, c0=c0: e.tensor_copy(out=vb[:, c0:c0 + 512], in_=pst[:]),
                          r=[("psA", pi)], w=[(L, "vb", slot, cb)])
                else:
                    P.add("act", lambda e, pst=pst, c0=c0: e.activation(out=szb[:, c0:c0 + 512], in_=pst[:], func=AF.Silu),
                          r=[("psA", pi)], w=[(L, "szb", slot, cb - 4)])
            P.add("sp", lambda e: e.dma_start(out=st_v[rows, :], in_=vb[:]), r=[(L, "vb", slot, i) for i in range(4)], w=[(L, "d_v", ti)], dma=(L, "sv", ti % 2))
            P.add("sp", lambda e: e.dma_start(out=st_sz[rows, :], in_=szb[:]), r=[(L, "szb", slot, i) for i in range(4)], w=[(L, "d_sz", ti)], dma=(L, "ssz", ti % 2))
        for ti in range(NT):
            p1b(ti)
    P.barrier()

    with contextlib.ExitStack() as es:
        def sb(name, shape, dt):
            return es.enter_context(nc.sbuf_tensor("sb_" + L + "c" + name, list(shape), dt))
        kS = sb("kS", [128, 2, S], BF16)
        vS = sb("vS", [128, NT, 257], BF16)
        tab = sb("tab", [128, 4, 512], F32)
        qS = [sb("qS%d" % i, [128, 2, 256], BF16) for i in range(2)]
        tmp = [sb("tmp%d" % i, [128, 512], F32) for i in range(4)]
        pT = [sb("pT%d" % i, [128, 512], BF16) for i in range(4)]
        sbank = [C.psA[0][:, :], C.psA[1][:, :], C.psO[:, 0:512], C.psO[:, 512:1024]]
        sbankb = [("psA", 0), ("psA", 1), ("psO", 0), ("psO", 1)]
        lam = sb("lam", [128, 4, 128], F32)
        lw = sb("lw", [128, 2, 128], F32)
        lv = sb("lv", [128, 8], F32)
        rr = sb("rr", [128, 8], F32)
        ot = [sb("ot%d" % i, [128, 256], F32) for i in range(2)]
        oo = [sb("oo%d" % i, [128, 256], F32) for i in range(2)]
        acc = [C.psA[2][:, 0:257], C.bankT[:, 0:257], C.bankY[:, 0:257], C.bankY[:, 512:769]]
        accb = [("psA", 2), "psT", ("psYT", 0), ("psYT", 1)]

        P.add("sp", lambda e: e.dma_start(out=lam[:].rearrange("p a b -> p (a b)"),
                                          in_=W["c_lam"][0:1].rearrange("o a b -> o (a b)").partition_broadcast(128)),
              w=[L + "lam"], dma=L + "lam")
        P.add("dve", lambda e: e.tensor_tensor(out=lw[:, 0, :], in0=lam[:, 0, :], in1=lam[:, 1, :], op=ALU.mult), r=[L + "lam"], w=[L + "lw0"])
        P.add("dve", lambda e: e.tensor_tensor(out=lw[:, 1, :], in0=lam[:, 2, :], in1=lam[:, 3, :], op=ALU.mult), r=[L + "lam"], w=[L + "lw1"])
        P.add("dve", lambda e: e.tensor_reduce(out=lv[:, 0:2], in_=lw[:], axis=AX.X, op=ALU.add), r=[L + "lw0", L + "lw1"], w=[L + "lv0"])
        P.add("act", lambda e: e.activation(out=lv[:, 2:4], in_=lv[:, 0:2], func=AF.Exp), r=[L + "lv0"], w=[L + "lv2"])
        P.add("dve", lambda e: e.tensor_tensor(out=lv[:, 4:5], in0=lv[:, 2:3], in1=lv[:, 3:4], op=ALU.subtract), r=[L + "lv2"], w=[L + "lv4"])
        P.add("dve", lambda e: e.tensor_scalar(out=lv[:, 5:6], in0=lv[:, 4:5], scalar1=-1.0, scalar2=-lambda_init, op0=ALU.mult, op1=ALU.add),
              r=[L + "lv4"], w=[L + "neglam"])
        P.add("pool", lambda e: e.memset(vS[:], 1.0), w=[(L, "vS", part) for part in range(max(4, NT // 8))])

        QT = 256
        NQ = S // QT
        cnt = [0]
        for h in range(8):
            slope = 2.0 ** (-(h + 1))
            dmax = BAND / slope
            KCH = 4 if NT >= 4 else 1
            CW = S // KCH
            for m in range(2):
                for ch in range(KCH):
                    P.add("sp", lambda e, h=h, m=m, ch=ch: e.dma_start(out=kS[:, m, ch * CW:(ch + 1) * CW],
                                                                     in_=st_kT[2 * h + m][:, ch * CW:(ch + 1) * CW]),
                          w=[(L, "kS", m, ch)], dma=(L, "kS", m))
            NPART = max(4, NT // 8)
            for part in range(NPART):
                n0 = part * NT // NPART
                n1 = (part + 1) * NT // NPART
                if n1 > n0:
                    P.add("pool", lambda e, h=h, n0=n0, n1=n1: e.dma_start(
                        out=vS[:, n0:n1, 0:256], in_=st_v[n0 * 128:n1 * 128, h * 256:(h + 1) * 256].rearrange("(n p) c -> p n c", p=128)),
                        r=[], w=[(L, "vS", part)], dma=(L, "vS", part % 4))
            P.add("sp", lambda e, h=h: e.dma_start(out=tab[:], in_=C.alibi_in[h].rearrange("a p j -> p a j")), w=[L + "tab"], dma=L + "tab")

            units = []
            for qi_ in range(NQ):
                q0 = qi_ * QT
                kbs = []
                for kb in range(NT):
                    k0 = kb * 128
                    if k0 >= q0 + QT:
                        dist = k0 - (q0 + QT - 1)
                    elif k0 + 127 < q0:
                        dist = q0 - (k0 + 127)
                    else:
                        dist = 0
                    if dist <= dmax:
                        kbs.append(kb)
                for ik, kb in enumerate(kbs):
                    units.append((qi_, kb, ik, len(kbs)))

            def front(un, h=h, slope=slope):
                qi_, kb, ik, nk = un
                q0 = qi_ * QT
                qslot = qi_ % 2
                qs_ = qS[qslot]
                if ik == 0:
                    P.add("sp", lambda e: e.dma_start(out=qs_[:], in_=st_qT[2 * h:2 * h + 2, :, q0:q0 + QT].rearrange("m d s -> d m s")),
                          w=[(L, "qS", qslot)], dma=(L, "qS", qslot))
                k0 = kb * 128
                delta = q0 - k0
                if delta >= 128:
                    tsel, cc = 0, -slope * delta
                elif delta <= -256:
                    tsel, cc = 1, slope * delta
                elif delta == 0:
                    tsel, cc = 2, 0.0
                else:
                    assert delta == -128
                    tsel, cc = 3, 0.0
                u = cnt[0] % 4
                cnt[0] += 1
                pst = sbank[u]
                for m in range(2):
                    P.add("pe", lambda e, m=m: e.matmul(
                        pst[:, m * 256:(m + 1) * 256], lhsT=kS[:, m, k0:k0 + 128], rhs=qs_[:, m, :], start=True, stop=True),
                        r=[(L, "kS", m, k0 // CW), (L, "qS", qslot)], w=[sbankb[u]])
                tm = tmp[u]
                P.add("dve", lambda e: e.scalar_tensor_tensor(
                    out=tm[:], in0=pst, scalar=float(cc), in1=tab[:, tsel, :], op0=ALU.add, op1=ALU.add),
                    r=[sbankb[u], L + "tab"], w=[(L, "tmp", u)])
                pt = pT[u]
                P.add("act", lambda e: e.activation(out=pt[:], in_=tm[:], func=AF.Exp),
                      r=[(L, "tmp", u)], w=[(L, "pT", u)])
                return u

            def back(un, u, h=h):
                qi_, kb, ik, nk = un
                q0 = qi_ * QT
                pt = pT[u]
                for m in range(2):
                    for qh in range(2):
                        a = m * 2 + qh
                        P.add("pe", lambda e, a=a, m=m, qh=qh: e.matmul(
                            acc[a], lhsT=pt[:, m * 256 + qh * 128:m * 256 + (qh + 1) * 128], rhs=vS[:, kb, :],
                            start=(ik == 0), stop=(ik == nk - 1)),
                            r=[(L, "pT", u), (L, "vS", kb * NPART // NT)], w=[accb[a]])
                if ik != nk - 1:
                    return
                for a in range(4):
                    P.add("dve", lambda e, a=a: e.reciprocal(out=rr[:, a:a + 1], in_=acc[a][:, 256:257]), r=[accb[a]], w=[(L, "rr", a)])
                for qh in range(2):
                    a0, a1 = qh, 2 + qh
                    P.add("dve", lambda e, a1=a1: e.tensor_tensor(out=rr[:, 4 + a1:5 + a1], in0=rr[:, a1:a1 + 1], in1=lv[:, 5:6], op=ALU.mult),
                          r=[(L, "rr", a1), L + "neglam"], w=[(L, "rl", a1)])
                    P.add("act", lambda e, a0=a0, qh=qh: e.activation(out=ot[qh][:], in_=acc[a0][:, 0:256], func=AF.Copy, scale=rr[:, a0:a0 + 1]),
                          r=[accb[a0], (L, "rr", a0)], w=[(L, "ot", qh)])
                    o_ = oo[qh]
                    P.add("dve", lambda e, a1=a1, o_=o_, qh=qh: e.scalar_tensor_tensor(
                        out=o_[:], in0=acc[a1][:, 0:256], scalar=rr[:, 4 + a1:5 + a1], in1=ot[qh][:], op0=ALU.mult, op1=ALU.add),
                        r=[accb[a1], (L, "rl", a1), (L, "ot", qh)], w=[(L, "oo", qh)])
                    r0 = q0 + qh * 128
                    P.add("sp", lambda e, o_=o_, r0=r0: e.dma_start(out=st_o[r0:r0 + 128, h * 256:(h + 1) * 256], in_=o_[:]),
                          r=[(L, "oo", qh)], w=[(L, "d_o", h, qi_, qh)], dma=(L, "so", qh))

            LAG = 3
            ubuf = {}
            for idx in range(len(units) + LAG):
                if idx < len(units):
                    ubuf[idx] = front(units[idx])
                if idx - LAG >= 0:
                    back(units[idx - LAG], ubuf.pop(idx - LAG))
    P.barrier()

    with contextlib.ExitStack() as es:
        def sb(name, shape, dt):
            return es.enter_context(nc.sbuf_tensor("sb_" + L + "d" + name, list(shape), dt))
        Wout = sb("Wout", [128, 16, D], BF16)
        ogtab = sb("ogtab", [128, 256], F32)
        of2 = [sb("of%d" % i, [128, DI], F32) for i in range(2)]
        sq3 = sb("sq", [128, DI], F32)
        szb32 = [sb("szb%d" % i, [128, DI], BF16) for i in range(2)]
        st2 = [sb("st%d" % i, [128, 32], F32) for i in range(2)]
        yb2 = [sb("yb%d" % i, [128, DI], BF16) for i in range(2)]
        yT = sb("yT", [128, DI], BF16)
        xn = sb("xn", [128, D], F32)
        wo = W["c_w_out"][0].rearrange("(kc p) f -> p kc f", p=128)
        for kc in range(0, 16, 4):
            P.add("pool", lambda e, kc=kc: e.dma_start(out=Wout[:, kc:kc + 4, :], in_=wo[:, kc:kc + 4, :]), w=[(L, "Wout", kc)], dma=(L, "W", 0))
        Wout_bufs = [(L, "Wout", kc) for kc in range(0, 16, 4)]
        P.add("sp", lambda e: e.dma_start(out=ogtab[:], in_=W["c_o_g"][0:1, :].partition_broadcast(128)), w=[L + "ogtab"], dma=L + "ogtab")

        def f3(ti):
            slot = ti % 2
            rows = slice(ti * 128, (ti + 1) * 128)
            xt = C.xt[slot]
            of, szb3, st, yb = of2[slot], szb32[slot], st2[slot], yb2[slot]
            P.add("sp", lambda e: e.dma_start(out=xt[:], in_=x_src[rows, :]), w=[("xt", slot)], dma=("xt", slot))
            P.add("sp", lambda e: e.dma_start(out=of[:], in_=st_o[rows, :]), w=[(L, "of", slot)], dma=(L, "lof", slot))
            P.add("sp", lambda e: e.dma_start(out=szb3[:], in_=st_sz[rows, :]), w=[(L, "szb3", slot)], dma=(L, "lsz", slot))
            yield
            P.add("pool", lambda e: e.tensor_tensor(out=sq3[:], in0=of[:], in1=of[:], op=ALU.mult), r=[(L, "of", slot)], w=[L + "sq3"])
            P.add("dve", lambda e: e.tensor_reduce(out=st[:, 0:8], in_=sq3[:].rearrange("p (g d) -> p g d", g=8), axis=AX.X, op=ALU.add),
                  r=[L + "sq3"], w=[(L, "st0", slot)])
            P.add("act", lambda e: e.activation(out=st[:, 8:16], in_=st[:, 0:8], func=AF.Sqrt, scale=1.0 / 256, bias=C.epsb[:, 0:1]),
                  r=[(L, "st0", slot), "epsb"], w=[(L, "st8", slot)])
            P.add("dve", lambda e: e.reciprocal(out=st[:, 16:24], in_=st[:, 8:16]), r=[(L, "st8", slot)], w=[(L, "st16", slot)])
            P.add("dve", lambda e: e.tensor_scalar(out=st[:, 16:24], in0=st[:, 16:24], scalar1=1.0 - lambda_init, scalar2=None, op0=ALU.mult),
                  r=[(L, "st16", slot)], w=[(L, "st16", slot)])
            yield
            for g in range(8):
                P.add("dve", lambda e, g=g: e.scalar_tensor_tensor(
                    out=of[:, g * 256:(g + 1) * 256], in0=of[:, g * 256:(g + 1) * 256], scalar=st[:, 16 + g:17 + g], in1=ogtab[:],
                    op0=ALU.mult, op1=ALU.mult), r=[(L, "of", slot), (L, "st16", slot), L + "ogtab", L + "sq3"], w=[(L, "of", slot)])
                if g % 4 == 3:
                    yield
            P.add("pool", lambda e: e.tensor_tensor(out=yb[:], in0=of[:], in1=szb3[:], op=ALU.mult),
                  r=[(L, "of", slot), (L, "szb3", slot)], w=[(L, "yb", slot)])
            yield

        def b3(ti):
            slot = ti % 2
            rows = slice(ti * 128, (ti + 1) * 128)
            yield from emit_out_proj(P, C, L, li, ti, yb2[slot], [(L, "yb", slot)] * 4, yT, Wout, Wout_bufs, C.xt[slot], ("xt", slot),
                                     xn, x_dst, rows, gen=True)

        interleave([f3(0)])
        for ti in range(NT):
            gens = [b3(ti)]
            if ti + 1 < NT:
                gens.append(f3(ti + 1))
            interleave(gens)


def emit_out_proj(P, C, L, li, ti, yb, yb_bufs, yT, Wout, Wout_bufs, xt, xtb, xn, x_dst, rows, gen=False):
    g = _emit_out_proj(P, C, L, li, ti, yb, yb_bufs, yT, Wout, Wout_bufs, xt, xtb, xn, x_dst, rows)
    if gen:
        return g
    for _ in g:
        pass


def _emit_out_proj(P, C, L, li, ti, yb, yb_bufs, yT, Wout, Wout_bufs, xt, xtb, xn, x_dst, rows):
    for kc in range(16):
        P.add("pe", lambda e, kc=kc: e.transpose(out=C.psYT[:, kc * 128:(kc + 1) * 128], in_=yb[:, kc * 128:(kc + 1) * 128],
                                                  identity=C.ident[:]),
              r=[yb_bufs[kc // 4], "ident"], w=["psYT"])
    P.add("act", lambda e: e.copy(out=yT[:], in_=C.psYT[:]), r=["psYT"], w=[L + "yT"])
    yield
    for nb in range(2):
        for kc in range(16):
            P.add("pe", lambda e, kc=kc, nb=nb: e.matmul(
                C.psO[:, nb * 512:(nb + 1) * 512], lhsT=yT[:, kc * 128:(kc + 1) * 128],
                rhs=Wout[:, kc, nb * 512:(nb + 1) * 512], start=(kc == 0), stop=(kc == 15)),
                r=[L + "yT", Wout_bufs[kc // 4]], w=[("psO", nb)])
        yield
    P.add("dve", lambda e: e.tensor_tensor(out=xn[:], in0=C.psO[:], in1=xt[:], op=ALU.add),
          r=[("psO", 0), ("psO", 1), xtb], w=[L + "xn"])
    P.add("sp", lambda e: e.dma_start(out=x_dst[rows, :], in_=xn[:]),
          r=[L + "xn"], w=[("xdram", li, ti)], dma=("xst", li, ti % 2))
    yield


def host_consts():
    ident = np.eye(128, dtype=np.float32).astype(ml_dtypes.bfloat16)
    tp = np.arange(128)[:, None]
    t = np.arange(128)[None, :]
    f = lambda m: m.astype(np.float32)
    cm = np.stack([
        f(tp <= t) - f(tp <= 64), f(tp <= t), f(tp > t),
        f(tp >= t) - f(tp >= 63), f(tp >= t), f(tp < t),
    ]).astype(np.float32) * (-1.0 / 16.0)
    sidx = np.arange(128)[:, None]
    tidx = np.arange(128)[None, :]
    mf = (tidx >= sidx).astype(np.float32)
    mb = (tidx < sidx).astype(np.float32)
    gmask = np.stack([np.tile(mf, (1, 4)), np.tile(mb, (1, 4))]).astype(ml_dtypes.bfloat16)
    p = np.arange(128, dtype=np.float64)[:, None]
    j = np.tile(np.arange(256, dtype=np.float64), 2)[None, :]
    alibi = np.zeros((8, 4, 128, 512), np.float32)
    for h in range(8):
        slope = 2.0 ** (-(h + 1))
        alibi[h, 0] = -slope * (j - p)
        alibi[h, 1] = -slope * (p - j)
        alibi[h, 2] = -slope * np.abs(j - p)
        alibi[h, 3] = -slope * np.abs(j - p - 128)
    return {"ident": ident, "cm": cm, "gmask": gmask, "alibi": alibi}


def run_model(inputs, S, depth):
    nc = build_program(S, depth)
    xall = np.concatenate([np.asarray(inputs["x_prompt"], np.float32), np.asarray(inputs["x_sample"], np.float32)], axis=0)
    consts = host_consts()
    in_maps = []
    for c in range(NCORES):
        m = {"x": np.ascontiguousarray(xall[c]) if c < 3 else np.zeros_like(xall[0])}
        m.update(consts)
        for k in WSHAPES:
            m[k] = np.ascontiguousarray(np.asarray(inputs[k], np.float32))
        in_maps.append(m)
    res = run_bass_kernel_spmd(nc, in_maps, core_ids=list(range(NCORES)))
    yall = np.stack([res.results[c]["y"] for c in range(3)], axis=0)
    return np.ascontiguousarray(yall[0:2]), np.ascontiguousarray(yall[2:3])


def kernel(**inputs):
    return run_model(inputs, 16384, 4)
```
